# Optimizing a Trainium2 kernel written in Bass

```python
import jax, jax.numpy as jnp
from jax import lax
import numpy as np

D_MODEL = 1024
BATCH = 8
SEQ = 4096
DEPTH = 4

CHUNK = 64
N_BRANCH = 3
A_WIDTH = 512
A_GROUPS = 4
A_BLOCK = 128
B_WIDTH = 512
B_CONV = 3
C_WIDTH = 512
C_CONV = 31
D_FF = 2816
N_ADA = 9
EPS = 1e-6
IN_COLS = 2 * A_WIDTH + 3 * B_WIDTH + 2 * C_WIDTH + N_BRANCH * D_MODEL

kernel_name = "hybrid_gmlp_shortconv_conformer_macaron_block"


def rmsnorm(x, g):
    xf = x.astype(jnp.float32)
    y = xf * lax.rsqrt(jnp.mean(xf * xf, axis=-1, keepdims=True) + EPS)
    return (y * g.astype(jnp.float32)).astype(x.dtype)


def layernorm(x, g, b):
    xf = x.astype(jnp.float32)
    mu = jnp.mean(xf, axis=-1, keepdims=True)
    var = jnp.mean(jnp.square(xf - mu), axis=-1, keepdims=True)
    y = (xf - mu) * lax.rsqrt(var + EPS)
    return (y * g.astype(jnp.float32) + b.astype(jnp.float32)).astype(x.dtype)


def ada_norm(x, g, shift, scale):
    return rmsnorm(x, g) * (1.0 + scale[:, None, :]) + shift[:, None, :]


def swiglu(h, w13, w2):
    a, b = jnp.split(h @ w13, 2, axis=-1)
    return (jax.nn.silu(a) * b) @ w2


def causal_depthwise_conv(z, w):
    k_width, ch = w.shape
    return lax.conv_general_dilated(
        z, w[:, None, :].astype(z.dtype), window_strides=(1,),
        padding=[(k_width - 1, 0)], dimension_numbers=("NWC", "WIO", "NWC"),
        feature_group_count=ch)


def spatial_gating(u, v, ln_g, ln_b, ws, bs):
    bsz, seq, _ = u.shape
    v = layernorm(v, ln_g, ln_b)
    vb = v.reshape(bsz, seq // A_BLOCK, A_BLOCK, A_GROUPS, A_WIDTH // A_GROUPS)
    pos_chunk = jnp.arange(A_BLOCK) // CHUNK
    mask = pos_chunk[:, None] >= pos_chunk[None, :]
    w = jnp.where(mask[None], ws, jnp.zeros((), ws.dtype))
    sv = jnp.einsum("gpq,bnqgc->bnpgc", w, vb) + bs.T[None, None, :, :, None]
    return u * sv.reshape(bsz, seq, A_WIDTH)


def setup_inputs(seed: int = 0) -> dict:
    key = jax.random.key(seed)
    ks = jax.random.split(key, 32)

    def nrm(k, shape, s):
        return jax.random.normal(k, shape, jnp.float32) * s

    def gain(k, shape):
        return 1.0 + 0.01 * jax.random.normal(k, shape, jnp.float32)

    d, f, L = D_MODEL, D_FF, DEPTH
    return {
        "x": nrm(ks[0], (BATCH, SEQ, d), 1.0),
        "c": nrm(ks[1], (BATCH, d), 1.0),
        "w_ada": nrm(ks[2], (L, d, N_ADA * d), d ** -0.5),
        "b_ada": nrm(ks[3], (L, N_ADA * d), 0.01),
        "g_ffn1": gain(ks[4], (L, d)),
        "ffn1_w13": nrm(ks[5], (L, d, 2 * f), d ** -0.5),
        "ffn1_w2": nrm(ks[6], (L, f, d), f ** -0.5),
        "g_mix": gain(ks[7], (L, d)),
        "w_in": nrm(ks[8], (L, d, IN_COLS), d ** -0.5),
        "a_ln_g": gain(ks[9], (L, A_WIDTH)),
        "a_ln_b": nrm(ks[10], (L, A_WIDTH), 0.01),
        "a_ws": nrm(ks[11], (L, A_GROUPS, A_BLOCK, A_BLOCK), A_BLOCK ** -0.5),
        "a_bs": gain(ks[12], (L, A_GROUPS, A_BLOCK)),
        "b_conv": nrm(ks[13], (L, B_CONV, B_WIDTH), B_CONV ** -0.5),
        "c_conv": nrm(ks[14], (L, C_CONV, C_WIDTH), C_CONV ** -0.5),
        "c_conv_b": nrm(ks[15], (L, C_WIDTH), 0.01),
        "c_ln_g": gain(ks[16], (L, C_WIDTH)),
        "c_ln_b": nrm(ks[17], (L, C_WIDTH), 0.01),
        "w_branch_a": nrm(ks[18], (L, A_WIDTH, d), A_WIDTH ** -0.5),
        "w_branch_b": nrm(ks[19], (L, B_WIDTH, d), B_WIDTH ** -0.5),
        "w_branch_c": nrm(ks[20], (L, C_WIDTH, d), C_WIDTH ** -0.5),
        "w_o": nrm(ks[21], (L, d, d), d ** -0.5),
        "g_ffn2": gain(ks[22], (L, d)),
        "ffn2_w13": nrm(ks[23], (L, d, 2 * f), d ** -0.5),
        "ffn2_w2": nrm(ks[24], (L, f, d), f ** -0.5),
        "g_final": gain(ks[25], (d,)),
    }


def reference(x, c, w_ada, b_ada, g_ffn1, ffn1_w13, ffn1_w2, g_mix, w_in,
              a_ln_g, a_ln_b, a_ws, a_bs, b_conv, c_conv, c_conv_b, c_ln_g, c_ln_b,
              w_branch_a, w_branch_b, w_branch_c, w_o, g_ffn2, ffn2_w13, ffn2_w2,
              g_final):
    bsz, seq, d = x.shape
    splits = np.cumsum([A_WIDTH, A_WIDTH, B_WIDTH, B_WIDTH, B_WIDTH, C_WIDTH, C_WIDTH]).tolist()
    c_act = jax.nn.silu(c)
    for l in range(DEPTH):
        ada = (c_act @ w_ada[l] + b_ada[l]).reshape(bsz, N_ADA, d)

        h = ada_norm(x, g_ffn1[l], ada[:, 0], ada[:, 1])
        x = x + 0.5 * ada[:, 2][:, None, :] * swiglu(h, ffn1_w13[l], ffn1_w2[l])

        h = ada_norm(x, g_mix[l], ada[:, 3], ada[:, 4])
        proj = h @ w_in[l]
        a_u, a_v, b_b, b_c, b_h, c_a, c_g, gates = jnp.split(proj, splits, axis=-1)

        y_a = spatial_gating(jax.nn.gelu(a_u, approximate=False), jax.nn.gelu(a_v, approximate=False),
                             a_ln_g[l], a_ln_b[l], a_ws[l], a_bs[l])
        y_b = b_b * causal_depthwise_conv(b_c * b_h, b_conv[l])
        z = c_a * jax.nn.sigmoid(c_g)
        z = causal_depthwise_conv(z, c_conv[l]) + c_conv_b[l]
        y_c = jax.nn.silu(layernorm(z, c_ln_g[l], c_ln_b[l]))

        gates = jax.nn.sigmoid(gates).reshape(bsz, seq, N_BRANCH, d)
        merged = (gates[:, :, 0] * (y_a @ w_branch_a[l])
                  + gates[:, :, 1] * (y_b @ w_branch_b[l])
                  + gates[:, :, 2] * (y_c @ w_branch_c[l]))
        x = x + ada[:, 5][:, None, :] * (merged @ w_o[l])

        h = ada_norm(x, g_ffn2[l], ada[:, 6], ada[:, 7])
        x = x + 0.5 * ada[:, 8][:, None, :] * swiglu(h, ffn2_w13[l], ffn2_w2[l])

    return rmsnorm(x, g_final)
```

```python
import contextlib
import numpy as np
import concourse.bass as bass
import concourse.mybir as mybir
from concourse.bass_utils import run_bass_kernel_spmd

F32 = mybir.dt.float32
BF16 = mybir.dt.bfloat16
AF = mybir.ActivationFunctionType
ALU = mybir.AluOpType

D = 1024
NCH = 8
T = 512
DFF = 2816
NHC = 22
IN_COLS = 6656
N_ADA = 9
EPS = 1e-6
SLOTW = 4096
RING = 6
HALO = 32
LSP = 252
SEQ = 4096
DEPTH = 4
NCORES = 8


def _kc_tile(w, cols):
    K = w.shape[0]
    sub = w[:, cols]
    sub = sub.reshape(K // 128, 128, len(cols))
    return np.ascontiguousarray(sub.transpose(1, 0, 2)).reshape(128, -1)


def _layer_slots(l, ins):
    slots = []

    def ffn(w13, w2):
        for j2 in range(NHC // 2):
            cols = np.concatenate([
                np.arange((2 * j2) * 128, (2 * j2 + 2) * 128),
                DFF + np.arange((2 * j2) * 128, (2 * j2 + 2) * 128)])
            slots.append(_kc_tile(w13, cols))
        blk = w2.reshape(NHC, 128, NCH, 128).transpose(1, 2, 0, 3)
        blk = np.ascontiguousarray(blk).reshape(128, NCH * NHC * 128)
        for s in range(0, NCH * NHC * 128, SLOTW):
            slots.append(np.ascontiguousarray(blk[:, s:s + SLOTW]))

    ffn(ins["ffn1_w13"][l], ins["ffn1_w2"][l])
    w_in = ins["w_in"][l]
    for base in (0, 512, 1536, 2048, 1024, 3072, 2560):
        slots.append(_kc_tile(w_in, np.arange(base, base + 512)))
    wb = [ins["w_branch_a"][l], ins["w_branch_b"][l], ins["w_branch_c"][l]]
    for dc in range(NCH):
        g = [_kc_tile(w_in, 3584 + i * 1024 + dc * 128 + np.arange(128)) for i in range(3)]
        slots.append(np.concatenate(g, axis=1))
        b = [_kc_tile(wb[i], dc * 128 + np.arange(128)) for i in range(3)]
        slots.append(np.concatenate(b, axis=1))
    w_o = ins["w_o"][l]
    for s in range(2):
        slots.append(_kc_tile(w_o, np.arange(s * 512, (s + 1) * 512)))
    ffn(ins["ffn2_w13"][l], ins["ffn2_w2"][l])
    return slots


def _slot_widths():
    ffn = [SLOTW] * 11 + [SLOTW] * 5 + [NCH * NHC * 128 - 5 * SLOTW]
    mix = [SLOTW] * 7 + [3072, 1536] * NCH + [SLOTW] * 2
    return ffn + mix + ffn


def _fm(v, n):
    return np.ascontiguousarray(v.reshape(n, 128).T)


def prep_inputs(ins, n_layers, n_tiles, batch_ids):
    widths = _slot_widths()
    wst = np.empty((n_layers, 128 * sum(widths)), np.float32)
    for l in range(n_layers):
        off = 0
        for s, w in zip(_layer_slots(l, ins), widths):
            assert s.shape == (128, w), (s.shape, w)
            wst[l, off:off + 128 * w] = s.reshape(-1)
            off += 128 * w
    wada = np.empty((n_layers, 18, 128, SLOTW), np.float32)
    for l in range(n_layers):
        for s in range(18):
            wada[l, s] = _kc_tile(ins["w_ada"][l], np.arange(s * 512, (s + 1) * 512))
    nsp = n_layers * LSP + 16
    spw = np.empty((n_layers, 128, 1024), np.float32)
    sp_common = np.zeros((128, nsp), np.float32)
    for l in range(n_layers):
        b = l * LSP
        sp_common[:, b + 0:b + 8] = _fm(ins["g_ffn1"][l], 8)
        sp_common[:, b + 8:b + 16] = _fm(ins["g_mix"][l], 8)
        sp_common[:, b + 16:b + 24] = _fm(ins["g_ffn2"][l], 8)
        sp_common[:, b + 24:b + 96] = _fm(ins["b_ada"][l], 72)
        sp_common[:, b + 96:b + 100] = _fm(ins["a_ln_g"][l], 4)
        sp_common[:, b + 100:b + 104] = _fm(ins["a_ln_b"][l], 4)
        bc = ins["b_conv"][l].reshape(3, 4, 128).transpose(2, 1, 0).reshape(128, 12)
        cc = ins["c_conv"][l].reshape(31, 4, 128).transpose(2, 1, 0).reshape(128, 124)
        sp_common[:, b + 104:b + 116] = bc
        sp_common[:, b + 116:b + 240] = cc
        sp_common[:, b + 240:b + 244] = _fm(ins["c_conv_b"][l], 4)
        sp_common[:, b + 244:b + 248] = _fm(ins["c_ln_g"][l], 4)
        sp_common[:, b + 248:b + 252] = _fm(ins["c_ln_b"][l], 4)
        spw[l, :, 0:512] = ins["a_ws"][l].transpose(2, 0, 1).reshape(128, 512)
        spw[l, :, 512:1024] = np.broadcast_to(ins["a_bs"][l].reshape(1, 512), (128, 512))
    gb = n_layers * LSP
    sp_common[:, gb:gb + 8] = _fm(ins["g_final"], 8)
    maps = []
    for bi in batch_ids:
        sp = sp_common.copy()
        sp[:, gb + 8:gb + 16] = _fm(ins["c"][bi], 8)
        xT = np.ascontiguousarray(ins["x"][bi, :n_tiles * T, :].T)
        maps.append({"xT": xT, "wst": wst, "wada": wada, "sp": sp, "spw": spw})
    return maps


class Res:
    __slots__ = ("w", "r", "name")

    def __init__(self, name=""):
        self.w = None
        self.r = {}
        self.name = name


class Sched:
    ENGS = ("pe", "act", "dve", "pool", "sp")

    def __init__(self):
        self.ops = {e: [] for e in self.ENGS}
        self.cnt = {e: 0 for e in self.ENGS}
        self.seen = {e: {} for e in self.ENGS}
        self.clock = {e: [None] for e in self.ENGS}
        self.dma_cnt = {}

    def _deps(self, e, reads, writes):
        raw, other = {}, {}

        def add(dst, kv):
            k, v = kv
            if dst.get(k, 0) < v:
                dst[k] = v
        for r in reads:
            if r.w is not None:
                add(raw, r.w)
        for r in writes:
            if r.w is not None:
                add(other, r.w)
            for kv in r.r.items():
                add(other, kv)
        waits = []
        seen = self.seen[e]
        changed = False
        for src, is_raw in ((raw, True), (other, False)):
            for k, v in src.items():
                if k == e:
                    if e in ("pe", "pool", "sp") or not is_raw:
                        continue
                if seen.get(k, 0) >= v:
                    continue
                if not changed:
                    seen = dict(seen)
                    changed = True
                seen[k] = v
                waits.append((k, v))
                if k in self.clock:
                    snap = self.clock[k][v]
                    for k2, v2 in snap.items():
                        if seen.get(k2, 0) < v2:
                            seen[k2] = v2
        if changed:
            self.seen[e] = seen
        return waits

    def op(self, e, fn, reads=(), writes=()):
        waits = self._deps(e, reads, writes)
        self.cnt[e] += 1
        n = self.cnt[e]
        self.clock[e].append(self.seen[e])
        self.ops[e].append((waits, fn, (e, 1)))
        for r in writes:
            r.w = (e, n)
            r.r = {}
        for r in reads:
            if r.r.get(e, 0) < n:
                r.r[e] = n

    def dma(self, q, fn, semname, reads=(), writes=()):
        waits = self._deps(q, reads, writes)
        prev = self.dma_cnt.get(semname, 0)
        if prev and self.seen[q].get(semname, 0) < prev:
            s = dict(self.seen[q])
            s[semname] = prev
            self.seen[q] = s
            waits.append((semname, prev))
        val = prev + 16
        self.dma_cnt[semname] = val
        self.ops[q].append((waits, fn, (semname, 16)))
        for r in writes:
            r.w = (semname, val)
            r.r = {}
        for r in reads:
            r.r[semname] = val

    def final_wait(self, e, semname):
        v = self.dma_cnt.get(semname, 0)
        if v and self.seen[e].get(semname, 0) < v:
            self.ops[e].append(([(semname, v)], None, None))


def build_program(n_layers, n_tiles, final_norm=True):
    nc = bass.Bass("TRN2", target_bir_lowering=False)
    S = n_tiles * T
    widths = _slot_widths()
    offs = np.concatenate([[0], np.cumsum([128 * w for w in widths])]).astype(np.int64)
    nsp = n_layers * LSP + 16
    xT = nc.dram_tensor("xT", [D, S], F32, kind="ExternalInput").ap()
    wst = nc.dram_tensor("wst", [n_layers, int(offs[-1])], F32, kind="ExternalInput").ap()
    wada = nc.dram_tensor("wada", [n_layers, 18, 128, SLOTW], F32, kind="ExternalInput").ap()
    spd = nc.dram_tensor("sp", [128, nsp], F32, kind="ExternalInput").ap()
    spwd = nc.dram_tensor("spw", [n_layers, 128, 1024], F32, kind="ExternalInput").ap()
    outT = nc.dram_tensor("outT", [D, S], F32, kind="ExternalOutput").ap()

    sch = Sched()
    es = contextlib.ExitStack()

    def sb(name, shape, dt):
        return es.enter_context(nc.sbuf_tensor(name, shape, dt))

    X = sb("X", [128, NCH, T], F32)
    OST = sb("OST", [128, 2, T], F32)
    SQ = sb("SQ", [128, NCH, T], BF16)
    RSTD = sb("RSTD", [128, T], F32)
    H = sb("H", [128, NCH, T], BF16)
    G = sb("G", [128, NHC, T], BF16)
    NTMP = 6
    TMP = sb("TMP", [128, NTMP, T], F32)
    SA = sb("SA", [128, 4, T], F32)
    SB_ = sb("SBb", [128, 4, T], F32)
    SC = sb("SC", [128, 4, HALO + T], F32)
    SD = sb("SD", [128, 4, T], F32)
    VN = sb("VN", [128, 2, 512], BF16)
    ST6 = sb("ST6", [128, 2, 8], F32)
    MV = sb("MV", [128, 2, 2], F32)
    STAT = sb("STAT", [128, 2, T], F32)
    GT = sb("GT", [128, 3, T], F32)
    RINGB = sb("RINGB", [128, RING, SLOTW], BF16)
    SP = sb("SP", [128, nsp], F32)
    SCR = sb("SCR", [128, 1024], F32)
    DER = sb("DER", [128, n_layers, 72], F32)
    GF = sb("GF", [128, 8], F32)
    WMT = sb("WMT", [128, n_layers, 4, 128], BF16)
    BM = sb("BM", [128, n_layers, 4, 128], F32)
    ONES = sb("ONES", [128, 128], BF16)
    CACT = sb("CACT", [128, 8], BF16)
    CST = sb("CST", [128, 2], F32)
    TAILB = sb("TAILB", [128, n_layers, 4, 2], F32)
    TAILC = sb("TAILC", [128, n_layers, 4, 30], F32)
    PS = [es.enter_context(nc.psum_tensor(f"ps{i}", [128, 512], F32)) for i in range(8)]

    rX = [Res(f"X{c}") for c in range(NCH)]
    rOST = [Res() for _ in range(2)]
    rSQ = [Res() for _ in range(NCH)]
    rRSTD = Res()
    rH = [Res() for _ in range(NCH)]
    rG = [Res() for _ in range(NHC)]
    rTMP = [Res() for _ in range(NTMP)]
    rSA = [Res() for _ in range(4)]
    rSB = [Res() for _ in range(4)]
    rSC = [Res() for _ in range(4)]
    rSD = [Res() for _ in range(4)]
    rVN = [Res() for _ in range(2)]
    rST = [Res() for _ in range(2)]
    rSTAT = [Res() for _ in range(2)]
    rGT = [Res() for _ in range(3)]
    rRING = [Res() for _ in range(RING)]
    rSP = Res()
    rSCR = Res()
    rDER = [Res() for _ in range(n_layers)]
    rGF = Res()
    rWMT = [Res() for _ in range(n_layers)]
    rBM = [Res() for _ in range(n_layers)]
    rONES = Res()
    rCACT = Res()
    rCST = Res()
    rTB = [Res() for _ in range(n_layers)]
    rTC = [Res() for _ in range(n_layers)]
    rPS = [Res() for _ in range(8)]

    state = {"bank": 0, "tmp": 0, "slot": 0}

    def bank():
        b = state["bank"]
        state["bank"] = (b + 1) % 8
        return b

    def tmp():
        t = state["tmp"]
        state["tmp"] = (t + 1) % NTMP
        return t

    def load_slot(src_ap, width):
        n = state["slot"]
        state["slot"] = n + 1
        i = n % RING
        dst = RINGB[:, i, 0:width]
        sch.dma("pool", lambda e, dst=dst, src=src_ap: e.dma_start(out=dst, in_=src), f"ring{i}",
                writes=[rRING[i]])
        return i

    def wslot(l, k):
        w = widths[k]
        src = wst[l, int(offs[k]):int(offs[k + 1])].rearrange("(p w) -> p w", p=128)
        return load_slot(src, w)

    sch.dma("sp", lambda e: e.dma_start(out=SP[:], in_=spd), "spld", writes=[rSP])
    sch.op("dve", lambda e: e.memset(ONES[:], 1.0), writes=[rONES])
    sch.op("dve", lambda e: e.memset(CST[:, 0:1], float(EPS * D)), writes=[rCST])
    sch.op("dve", lambda e: e.memset(CST[:, 1:2], float(EPS)), writes=[rCST])
    sch.op("dve", lambda e: e.memset(TAILB[:], 0.0), writes=rTB)
    sch.op("dve", lambda e: e.memset(TAILC[:], 0.0), writes=rTC)
    gb = n_layers * LSP
    sch.op("act", lambda e: e.activation(out=CACT[:], in_=SP[:, gb + 8:gb + 16], func=AF.Silu),
           reads=[rSP], writes=[rCACT])
    sch.op("dve", lambda e: e.tensor_scalar(out=GF[:], in0=SP[:, gb:gb + 8], scalar1=float(np.sqrt(D)),
                                            scalar2=None, op0=ALU.mult), reads=[rSP], writes=[rGF])
    for l in range(n_layers):
        b0 = l * LSP
        sch.dma("sp", lambda e, l=l: e.dma_start(out=SCR[:], in_=spwd[l]), "scrld", writes=[rSCR])
        sch.op("dve", lambda e, l=l: e.tensor_copy(out=WMT[:, l].rearrange("p g q -> p (g q)"), in_=SCR[:, 0:512]),
               reads=[rSCR], writes=[rWMT[l]])
        sch.op("dve", lambda e, l=l: e.memset(WMT[64:128, l, :, 0:64], 0.0), writes=[rWMT[l]])
        bk = bank()
        sch.op("pe", lambda e, l=l, bk=bk: e.matmul(PS[bk][:], ONES[:], WMT[:, l].rearrange("p g q -> p (g q)"),
                                                   start=True, stop=True),
               reads=[rONES, rWMT[l]], writes=[rPS[bk]])
        for g in range(4):
            sch.op("dve", lambda e, l=l, g=g, bk=bk, b0=b0: e.scalar_tensor_tensor(
                out=BM[:, l, g, :], in0=PS[bk][:, g * 128:(g + 1) * 128], scalar=SP[:, b0 + 100 + g:b0 + 101 + g],
                in1=SCR[:, 512 + g * 128:512 + (g + 1) * 128], op0=ALU.mult, op1=ALU.add),
                reads=[rPS[bk], rSP, rSCR], writes=[rBM[l]])

    def ada_stage(l):
        b0 = l * LSP
        bk = bank()
        for s in range(18):
            i = load_slot(wada[l, s], SLOTW)

            def f(e, i=i, s=s, bk=bk):
                ins = None
                for cc in range(4):
                    col = s * 4 + cc
                    for kc in range(NCH):
                        ins = e.matmul(PS[bk][:, col:col + 1],
                                       RINGB[:, i, kc * 512 + cc * 128:kc * 512 + (cc + 1) * 128],
                                       CACT[:, kc:kc + 1], start=(kc == 0), stop=(kc == NCH - 1))
                return ins
            sch.op("pe", f, reads=[rRING[i], rCACT], writes=[rPS[bk]])
        dl = DER[:, l, :]
        sch.op("dve", lambda e: e.tensor_tensor(out=dl, in0=PS[bk][:, 0:72], in1=SP[:, b0 + 24:b0 + 96], op=ALU.add),
               reads=[rPS[bk], rSP], writes=[rDER[l]])
        for j, gcol in ((1, 0), (4, 8), (7, 16)):
            sch.op("dve", lambda e, j=j, gcol=gcol: e.scalar_tensor_tensor(
                out=DER[:, l, j * 8:j * 8 + 8], in0=DER[:, l, j * 8:j * 8 + 8], scalar=1.0,
                in1=SP[:, b0 + gcol:b0 + gcol + 8], op0=ALU.add, op1=ALU.mult),
                reads=[rDER[l], rSP], writes=[rDER[l]])
            sch.op("dve", lambda e, j=j: e.tensor_scalar(
                out=DER[:, l, j * 8:j * 8 + 8], in0=DER[:, l, j * 8:j * 8 + 8], scalar1=float(np.sqrt(D)),
                scalar2=None, op0=ALU.mult), reads=[rDER[l]], writes=[rDER[l]])
        for j in (2, 8):
            sch.op("dve", lambda e, j=j: e.tensor_scalar(
                out=DER[:, l, j * 8:j * 8 + 8], in0=DER[:, l, j * 8:j * 8 + 8], scalar1=0.5,
                scalar2=None, op0=ALU.mult), reads=[rDER[l]], writes=[rDER[l]])

    def rms_stats():
        for c in range(NCH):
            sch.op("act", lambda e, c=c: e.activation(out=SQ[:, c, :], in_=X[:, c, :], func=AF.Square),
                   reads=[rX[c]], writes=[rSQ[c]])
        bk = bank()

        def f(e):
            ins = None
            for c in range(NCH):
                ins = e.matmul(PS[bk][:], ONES[:], SQ[:, c, :], start=(c == 0), stop=(c == NCH - 1))
            return ins
        sch.op("pe", f, reads=rSQ + [rONES], writes=[rPS[bk]])
        sch.op("act", lambda e: e.activation(out=RSTD[:], in_=PS[bk][:], func=AF.Sqrt, bias=CST[:, 0:1], scale=1.0),
               reads=[rPS[bk], rCST], writes=[rRSTD])
        sch.op("dve", lambda e: e.reciprocal(out=RSTD[:], in_=RSTD[:]), reads=[rRSTD], writes=[rRSTD])

    def norm(l, jshift, jscale):
        rms_stats()
        for c in range(NCH):
            t = tmp()
            sch.op("dve", lambda e, c=c, t=t: e.tensor_tensor(out=TMP[:, t, :], in0=X[:, c, :], in1=RSTD[:], op=ALU.mult),
                   reads=[rX[c], rRSTD], writes=[rTMP[t]])
            sch.op("act", lambda e, c=c, t=t: e.activation(
                out=H[:, c, :], in_=TMP[:, t, :], func=AF.Identity,
                bias=DER[:, l, jshift * 8 + c:jshift * 8 + c + 1], scale=DER[:, l, jscale * 8 + c:jscale * 8 + c + 1]),
                reads=[rTMP[t], rDER[l]], writes=[rH[c]])

    def mm_group(out_ap, pairs, reads, bk):
        def f(e):
            ins = None
            n = len(pairs)
            for k, (lhsT, rhs) in enumerate(pairs):
                ins = e.matmul(out_ap, lhsT, rhs, start=(k == 0), stop=(k == n - 1))
            return ins
        sch.op("pe", f, reads=reads, writes=[rPS[bk]])

    def ffn_stage(l, k0, jshift, jscale, jgate):
        norm(l, jshift, jscale)
        for j in range(NHC):
            if j % 2 == 0:
                i = wslot(l, k0 + j // 2)
            jj = j % 2
            ba, bb = bank(), bank()
            for q, bk in ((jj, ba), (2 + jj, bb)):
                mm_group(PS[bk][:], [(RINGB[:, i, kc * 512 + q * 128:kc * 512 + (q + 1) * 128], H[:, kc, :])
                                      for kc in range(NCH)], [rRING[i]] + rH, bk)
            t = tmp()
            sch.op("act", lambda e, t=t, ba=ba: e.activation(out=TMP[:, t, :], in_=PS[ba][:], func=AF.Silu),
                   reads=[rPS[ba]], writes=[rTMP[t]])
            sch.op("dve", lambda e, t=t, bb=bb, j=j: e.tensor_tensor(out=G[:, j, :], in0=TMP[:, t, :], in1=PS[bb][:], op=ALU.mult),
                   reads=[rTMP[t], rPS[bb]], writes=[rG[j]])
        cur = {}
        for dc in range(NCH):
            pairs, reads = [], list(rG)
            for hc in range(NHC):
                bi = dc * NHC + hc
                s = bi // 32
                if s not in cur:
                    cur.clear()
                    cur[s] = wslot(l, k0 + 11 + s)
                i = cur[s]
                if rRING[i] not in reads:
                    reads.append(rRING[i])
                pairs.append((i, (bi % 32) * 128, hc))
            bk = bank()
            mm_group(PS[bk][:], [(RINGB[:, i, o:o + 128], G[:, hc, :]) for (i, o, hc) in pairs], reads, bk)
            sch.op("dve", lambda e, dc=dc, bk=bk: e.scalar_tensor_tensor(
                out=X[:, dc, :], in0=PS[bk][:], scalar=DER[:, l, jgate * 8 + dc:jgate * 8 + dc + 1], in1=X[:, dc, :],
                op0=ALU.mult, op1=ALU.add), reads=[rPS[bk], rDER[l], rX[dc]], writes=[rX[dc]])

    def mixer_stage(l, k0):
        b0 = l * LSP
        norm(l, 3, 4)
        i = wslot(l, k0 + 0)
        for m in range(4):
            bk = bank()
            mm_group(PS[bk][:], [(RINGB[:, i, kc * 512 + m * 128:kc * 512 + (m + 1) * 128], H[:, kc, :])
                                  for kc in range(NCH)], [rRING[i]] + rH, bk)
            sch.op("act", lambda e, m=m, bk=bk: e.activation(out=SA[:, m, :], in_=PS[bk][:], func=AF.Gelu),
                   reads=[rPS[bk]], writes=[rSA[m]])
        i = wslot(l, k0 + 1)
        mb = [bank() for _ in range(4)]
        for tb in range(4):
            bk = bank()
            mm_group(PS[bk][:], [(H[:, kc, tb * 128:(tb + 1) * 128], RINGB[:, i, kc * 512:(kc + 1) * 512])
                                  for kc in range(NCH)], [rRING[i]] + rH, bk)
            t = tmp()
            v = tb % 2
            sch.op("act", lambda e, t=t, bk=bk: e.activation(out=TMP[:, t, :], in_=PS[bk][:], func=AF.Gelu),
                   reads=[rPS[bk]], writes=[rTMP[t]])
            sch.op("dve", lambda e, t=t, v=v: e.bn_stats(out=ST6[:, v, 0:6], in_=TMP[:, t, :]),
                   reads=[rTMP[t]], writes=[rST[v]])
            sch.op("dve", lambda e, v=v: e.bn_aggr(out=MV[:, v, :], in_=ST6[:, v, 0:6]),
                   reads=[rST[v]], writes=[rST[v]])
            sch.op("act", lambda e, v=v: e.activation(out=MV[:, v, 1:2], in_=MV[:, v, 1:2], func=AF.Sqrt,
                                                     bias=CST[:, 1:2], scale=1.0),
                   reads=[rST[v], rCST], writes=[rST[v]])
            sch.op("dve", lambda e, v=v: e.reciprocal(out=MV[:, v, 1:2], in_=MV[:, v, 1:2]),
                   reads=[rST[v]], writes=[rST[v]])
            sch.op("dve", lambda e, t=t, v=v: e.tensor_scalar(out=VN[:, v, :], in0=TMP[:, t, :], scalar1=MV[:, v, 0:1],
                                                             scalar2=MV[:, v, 1:2], op0=ALU.subtract, op1=ALU.mult),
                   reads=[rTMP[t], rST[v]], writes=[rVN[v]])

            def f(e, v=v, tb=tb):
                ins = None
                for g in range(4):
                    ins = e.matmul(PS[mb[g]][:, tb * 128:(tb + 1) * 128], VN[:, v, g * 128:(g + 1) * 128],
                                   WMT[:, l, g, :], start=True, stop=True)
                return ins
            sch.op("pe", f, reads=[rVN[v], rWMT[l]], writes=[rPS[b] for b in mb])
        for g in range(4):
            t = tmp()
            sch.op("dve", lambda e, g=g, t=t: e.scalar_tensor_tensor(
                out=TMP[:, t, :].rearrange("p (a b) -> p a b", a=4),
                in0=PS[mb[g]][:].rearrange("p (a b) -> p a b", a=4),
                scalar=SP[:, b0 + 96 + g:b0 + 97 + g],
                in1=BM[:, l, g, :].unsqueeze(1).broadcast_to([128, 4, 128]),
                op0=ALU.mult, op1=ALU.add), reads=[rPS[mb[g]], rSP, rBM[l]], writes=[rTMP[t]])
            sch.op("dve", lambda e, g=g, t=t: e.tensor_tensor(out=G[:, g, :], in0=SA[:, g, :], in1=TMP[:, t, :], op=ALU.mult),
                   reads=[rSA[g], rTMP[t]], writes=[rG[g]])
        i = wslot(l, k0 + 2)
        for m in range(4):
            bk = bank()
            mm_group(PS[bk][:], [(RINGB[:, i, kc * 512 + m * 128:kc * 512 + (m + 1) * 128], H[:, kc, :])
                                  for kc in range(NCH)], [rRING[i]] + rH, bk)
            sch.op("act", lambda e, m=m, bk=bk: e.activation(out=SB_[:, m, :], in_=PS[bk][:], func=AF.Identity),
                   reads=[rPS[bk]], writes=[rSB[m]])
        i = wslot(l, k0 + 3)
        for m in range(4):
            bk = bank()
            mm_group(PS[bk][:], [(RINGB[:, i, kc * 512 + m * 128:kc * 512 + (m + 1) * 128], H[:, kc, :])
                                  for kc in range(NCH)], [rRING[i]] + rH, bk)
            sch.op("act", lambda e, m=m: e.activation(out=SC[:, m, HALO - 2:HALO], in_=TAILB[:, l, m, :], func=AF.Identity),
                   reads=[rTB[l]], writes=[rSC[m]])
            sch.op("dve", lambda e, m=m, bk=bk: e.tensor_tensor(out=SC[:, m, HALO:HALO + T], in0=SB_[:, m, :], in1=PS[bk][:], op=ALU.mult),
                   reads=[rSB[m], rPS[bk]], writes=[rSC[m]])
            sch.op("act", lambda e, m=m: e.activation(out=TAILB[:, l, m, :], in_=SC[:, m, HALO + T - 2:HALO + T], func=AF.Identity),
                   reads=[rSC[m]], writes=[rTB[l]])
            wb = b0 + 104 + m * 3
            sch.op("dve", lambda e, m=m, wb=wb: e.tensor_scalar(out=SD[:, m, :], in0=SC[:, m, HALO - 2:HALO - 2 + T],
                                                               scalar1=SP[:, wb:wb + 1], scalar2=None, op0=ALU.mult),
                   reads=[rSC[m], rSP], writes=[rSD[m]])
            for k in (1, 2):
                sch.op("dve", lambda e, m=m, wb=wb, k=k: e.scalar_tensor_tensor(
                    out=SD[:, m, :], in0=SC[:, m, HALO - 2 + k:HALO - 2 + k + T], scalar=SP[:, wb + k:wb + k + 1],
                    in1=SD[:, m, :], op0=ALU.mult, op1=ALU.add), reads=[rSC[m], rSP, rSD[m]], writes=[rSD[m]])
        i = wslot(l, k0 + 4)
        for m in range(4):
            bk = bank()
            mm_group(PS[bk][:], [(RINGB[:, i, kc * 512 + m * 128:kc * 512 + (m + 1) * 128], H[:, kc, :])
                                  for kc in range(NCH)], [rRING[i]] + rH, bk)
            sch.op("dve", lambda e, m=m, bk=bk: e.tensor_tensor(out=G[:, 4 + m, :], in0=SD[:, m, :], in1=PS[bk][:], op=ALU.mult),
                   reads=[rSD[m], rPS[bk]], writes=[rG[4 + m]])
        i = wslot(l, k0 + 5)
        for m in range(4):
            bk = bank()
            mm_group(PS[bk][:], [(RINGB[:, i, kc * 512 + m * 128:kc * 512 + (m + 1) * 128], H[:, kc, :])
                                  for kc in range(NCH)], [rRING[i]] + rH, bk)
            sch.op("act", lambda e, m=m, bk=bk: e.activation(out=SB_[:, m, :], in_=PS[bk][:], func=AF.Sigmoid),
                   reads=[rPS[bk]], writes=[rSB[m]])
        i = wslot(l, k0 + 6)
        for m in range(4):
            bk = bank()
            mm_group(PS[bk][:], [(RINGB[:, i, kc * 512 + m * 128:kc * 512 + (m + 1) * 128], H[:, kc, :])
                                  for kc in range(NCH)], [rRING[i]] + rH, bk)
            sch.op("act", lambda e, m=m: e.activation(out=SC[:, m, HALO - 30:HALO], in_=TAILC[:, l, m, :], func=AF.Identity),
                   reads=[rTC[l]], writes=[rSC[m]])
            sch.op("dve", lambda e, m=m, bk=bk: e.tensor_tensor(out=SC[:, m, HALO:HALO + T], in0=SB_[:, m, :], in1=PS[bk][:], op=ALU.mult),
                   reads=[rSB[m], rPS[bk]], writes=[rSC[m]])
            sch.op("act", lambda e, m=m: e.activation(out=TAILC[:, l, m, :], in_=SC[:, m, HALO + T - 30:HALO + T], func=AF.Identity),
                   reads=[rSC[m]], writes=[rTC[l]])
        for k in range(31):
            for m in range(4):
                wc = b0 + 116 + m * 31 + k
                src = SC[:, m, HALO - 30 + k:HALO - 30 + k + T]
                if k == 0:
                    sch.op("dve", lambda e, m=m, wc=wc, src=src: e.tensor_scalar(
                        out=SD[:, m, :], in0=src, scalar1=SP[:, wc:wc + 1], scalar2=SP[:, b0 + 240 + m:b0 + 241 + m],
                        op0=ALU.mult, op1=ALU.add), reads=[rSC[m], rSP], writes=[rSD[m]])
                else:
                    sch.op("dve", lambda e, m=m, wc=wc, src=src: e.scalar_tensor_tensor(
                        out=SD[:, m, :], in0=src, scalar=SP[:, wc:wc + 1], in1=SD[:, m, :],
                        op0=ALU.mult, op1=ALU.add), reads=[rSC[m], rSP, rSD[m]], writes=[rSD[m]])
        for m in range(4):
            sch.op("act", lambda e, m=m: e.activation(out=G[:, 12 + m, :], in_=SD[:, m, :], func=AF.Identity),
                   reads=[rSD[m]], writes=[rG[12 + m]])
            sch.op("act", lambda e, m=m: e.activation(out=G[:, 16 + m, :], in_=SD[:, m, :], func=AF.Square),
                   reads=[rSD[m]], writes=[rG[16 + m]])
        b1, b2 = bank(), bank()
        mm_group(PS[b1][:], [(ONES[:], G[:, 12 + m, :]) for m in range(4)], [rONES] + rG[12:16], b1)
        mm_group(PS[b2][:], [(ONES[:], G[:, 16 + m, :]) for m in range(4)], [rONES] + rG[16:20], b2)
        sch.op("dve", lambda e: e.tensor_scalar(out=STAT[:, 0, :], in0=PS[b1][:], scalar1=1.0 / 512, scalar2=None, op0=ALU.mult),
               reads=[rPS[b1]], writes=[rSTAT[0]])
        sch.op("dve", lambda e: e.tensor_tensor(out=STAT[:, 1, :], in0=STAT[:, 0, :], in1=STAT[:, 0, :], op=ALU.mult),
               reads=[rSTAT[0]], writes=[rSTAT[1]])
        sch.op("dve", lambda e: e.scalar_tensor_tensor(out=STAT[:, 1, :], in0=PS[b2][:], scalar=1.0 / 512, in1=STAT[:, 1, :],
                                                       op0=ALU.mult, op1=ALU.subtract),
               reads=[rPS[b2], rSTAT[1]], writes=[rSTAT[1]])
        sch.op("act", lambda e: e.activation(out=STAT[:, 1, :], in_=STAT[:, 1, :], func=AF.Sqrt, bias=CST[:, 1:2], scale=1.0),
               reads=[rSTAT[1], rCST], writes=[rSTAT[1]])
        sch.op("dve", lambda e: e.reciprocal(out=STAT[:, 1, :], in_=STAT[:, 1, :]), reads=[rSTAT[1]], writes=[rSTAT[1]])
        for m in range(4):
            t = tmp()
            sch.op("dve", lambda e, m=m, t=t: e.tensor_tensor(out=TMP[:, t, :], in0=SD[:, m, :], in1=STAT[:, 0, :], op=ALU.subtract),
                   reads=[rSD[m], rSTAT[0]], writes=[rTMP[t]])
            sch.op("dve", lambda e, t=t: e.tensor_tensor(out=TMP[:, t, :], in0=TMP[:, t, :], in1=STAT[:, 1, :], op=ALU.mult),
                   reads=[rTMP[t], rSTAT[1]], writes=[rTMP[t]])
            sch.op("act", lambda e, m=m, t=t: e.activation(
                out=G[:, 8 + m, :], in_=TMP[:, t, :], func=AF.Silu,
                bias=SP[:, b0 + 248 + m:b0 + 249 + m], scale=SP[:, b0 + 244 + m:b0 + 245 + m]),
                reads=[rTMP[t], rSP], writes=[rG[8 + m]])
        for dc in range(NCH):
            ig = wslot(l, k0 + 7 + 2 * dc)
            ib = wslot(l, k0 + 8 + 2 * dc)
            gbk, pbk = [], []
            for i3 in range(3):
                bk = bank()
                gbk.append(bk)
                mm_group(PS[bk][:], [(RINGB[:, ig, (i3 * 8 + kc) * 128:(i3 * 8 + kc + 1) * 128], H[:, kc, :])
                                      for kc in range(NCH)], [rRING[ig]] + rH, bk)
                sch.op("act", lambda e, i3=i3, bk=bk: e.activation(out=GT[:, i3, :], in_=PS[bk][:], func=AF.Sigmoid),
                       reads=[rPS[bk]], writes=[rGT[i3]])
            for i3 in range(3):
                bk = bank()
                pbk.append(bk)
                mm_group(PS[bk][:], [(RINGB[:, ib, (i3 * 4 + kc) * 128:(i3 * 4 + kc + 1) * 128], G[:, i3 * 4 + kc, :])
                                      for kc in range(4)], [rRING[ib]] + rG[i3 * 4:i3 * 4 + 4], bk)
            ta, tb_ = tmp(), tmp()
            sch.op("dve", lambda e, ta=ta, bk=pbk[0]: e.tensor_tensor(out=TMP[:, ta, :], in0=GT[:, 0, :], in1=PS[bk][:], op=ALU.mult),
                   reads=[rGT[0], rPS[pbk[0]]], writes=[rTMP[ta]])
            sch.op("dve", lambda e, tb_=tb_, bk=pbk[1]: e.tensor_tensor(out=TMP[:, tb_, :], in0=GT[:, 1, :], in1=PS[bk][:], op=ALU.mult),
                   reads=[rGT[1], rPS[pbk[1]]], writes=[rTMP[tb_]])
            sch.op("dve", lambda e, ta=ta, tb_=tb_: e.tensor_tensor(out=TMP[:, ta, :], in0=TMP[:, ta, :], in1=TMP[:, tb_, :], op=ALU.add),
                   reads=[rTMP[ta], rTMP[tb_]], writes=[rTMP[ta]])
            sch.op("dve", lambda e, tb_=tb_, bk=pbk[2]: e.tensor_tensor(out=TMP[:, tb_, :], in0=GT[:, 2, :], in1=PS[bk][:], op=ALU.mult),
                   reads=[rGT[2], rPS[pbk[2]]], writes=[rTMP[tb_]])
            sch.op("dve", lambda e, ta=ta, tb_=tb_, dc=dc: e.tensor_tensor(out=SQ[:, dc, :], in0=TMP[:, ta, :], in1=TMP[:, tb_, :], op=ALU.add),
                   reads=[rTMP[ta], rTMP[tb_]], writes=[rSQ[dc]])
        for dc in range(NCH):
            if dc % 4 == 0:
                i = wslot(l, k0 + 23 + dc // 4)
            bk = bank()
            mm_group(PS[bk][:], [(RINGB[:, i, kc * 512 + (dc % 4) * 128:kc * 512 + (dc % 4 + 1) * 128], SQ[:, kc, :])
                                  for kc in range(NCH)], [rRING[i]] + rSQ, bk)
            sch.op("dve", lambda e, dc=dc, bk=bk: e.scalar_tensor_tensor(
                out=X[:, dc, :], in0=PS[bk][:], scalar=DER[:, l, 5 * 8 + dc:5 * 8 + dc + 1], in1=X[:, dc, :],
                op0=ALU.mult, op1=ALU.add), reads=[rPS[bk], rDER[l], rX[dc]], writes=[rX[dc]])

    def load_x(ti):
        for c in range(NCH):
            sch.dma("sp", lambda e, c=c, ti=ti: e.dma_start(out=X[:, c, :], in_=xT[c * 128:(c + 1) * 128, ti * T:(ti + 1) * T]),
                    f"xld{c}", writes=[rX[c]])

    load_x(0)
    for ti in range(n_tiles):
        for l in range(n_layers):
            if ti == 0:
                ada_stage(l)
            ffn_stage(l, 0, 0, 1, 2)
            mixer_stage(l, 17)
            ffn_stage(l, 42, 6, 7, 8)
        if final_norm:
            rms_stats()
        for c in range(NCH):
            o = c % 2
            if final_norm:
                t = tmp()
                sch.op("dve", lambda e, c=c, t=t: e.tensor_tensor(out=TMP[:, t, :], in0=X[:, c, :], in1=RSTD[:], op=ALU.mult),
                       reads=[rX[c], rRSTD], writes=[rTMP[t]])
                sch.op("act", lambda e, c=c, t=t, o=o: e.activation(out=OST[:, o, :], in_=TMP[:, t, :], func=AF.Identity,
                                                                  scale=GF[:, c:c + 1]),
                       reads=[rTMP[t], rGF], writes=[rOST[o]])
            else:
                sch.op("act", lambda e, c=c, o=o: e.activation(out=OST[:, o, :], in_=X[:, c, :], func=AF.Identity),
                       reads=[rX[c]], writes=[rOST[o]])
            sch.dma("sp", lambda e, c=c, ti=ti, o=o: e.dma_start(out=outT[c * 128:(c + 1) * 128, ti * T:(ti + 1) * T], in_=OST[:, o, :]),
                    f"ost{o}", reads=[rOST[o]])
            if ti + 1 < n_tiles:
                sch.dma("sp", lambda e, c=c, ti=ti: e.dma_start(out=X[:, c, :], in_=xT[c * 128:(c + 1) * 128, (ti + 1) * T:(ti + 2) * T]),
                        f"xld{c}", writes=[rX[c]])
    sch.final_wait("sp", "ost0")
    sch.final_wait("sp", "ost1")

    semnames = list(Sched.ENGS) + sorted(sch.dma_cnt.keys())
    sems = {n: es.enter_context(nc.semaphore("s_" + n)) for n in semnames}
    block = es.enter_context(nc.Block())

    def emit(ename):
        def body(eng):
            for waits, fn, inc in sch.ops[ename]:
                for k, v in waits:
                    eng.wait_ge(sems[k], v)
                if fn is None:
                    continue
                ins = fn(eng)
                ins.then_inc(sems[inc[0]], inc[1])
        return body

    block.tensor(emit("pe"))
    block.scalar(emit("act"))
    block.vector(emit("dve"))
    block.gpsimd(emit("pool"))
    block.sync(emit("sp"))
    es.close()
    return nc


def kernel(**inputs):
    ins = {k: np.asarray(v) for k, v in inputs.items()}
    nc = build_program(DEPTH, SEQ // T, True)
    maps = prep_inputs(ins, DEPTH, SEQ // T, list(range(NCORES)))
    res = run_bass_kernel_spmd(nc, maps, core_ids=list(range(NCORES)))
    out = np.empty((NCORES, SEQ, D), np.float32)
    for b in range(NCORES):
        out[b] = res.results[b]["outT"].T
    return out
```

```python
import contextlib
import numpy as np
import concourse.bass as bass
import concourse.mybir as mybir
from concourse.bass_utils import run_bass_kernel_spmd

F32 = mybir.dt.float32
BF16 = mybir.dt.bfloat16
AF = mybir.ActivationFunctionType
ALU = mybir.AluOpType

D = 1024
NCH = 8
T = 512
DFF = 2816
NHC = 22
IN_COLS = 6656
N_ADA = 9
EPS = 1e-6
SLOTW = 4096
RING = 5
HALO = 32
LSP = 252
SEQ = 4096
DEPTH = 4
NCORES = 8


def _kc_tile(w, cols):
    K = w.shape[0]
    sub = w[:, cols]
    sub = sub.reshape(K // 128, 128, len(cols))
    return np.ascontiguousarray(sub.transpose(1, 0, 2)).reshape(128, -1)


def _layer_slots(l, ins):
    slots = []

    def ffn(w13, w2):
        for j2 in range(NHC // 2):
            cols = np.concatenate([
                np.arange((2 * j2) * 128, (2 * j2 + 2) * 128),
                DFF + np.arange((2 * j2) * 128, (2 * j2 + 2) * 128)])
            slots.append(_kc_tile(w13, cols))
        blk = w2.reshape(NHC, 128, NCH, 128).transpose(1, 2, 0, 3)
        blk = np.ascontiguousarray(blk).reshape(128, NCH * NHC * 128)
        for s in range(0, NCH * NHC * 128, SLOTW):
            slots.append(np.ascontiguousarray(blk[:, s:s + SLOTW]))

    ffn(ins["ffn1_w13"][l], ins["ffn1_w2"][l])
    w_in = ins["w_in"][l]
    for base in (3072, 2560, 0, 512, 1536, 2048, 1024):
        slots.append(_kc_tile(w_in, np.arange(base, base + 512)))
    wb = [ins["w_branch_a"][l], ins["w_branch_b"][l], ins["w_branch_c"][l]]
    for dc in range(NCH):
        g = [_kc_tile(w_in, 3584 + i * 1024 + dc * 128 + np.arange(128)) for i in range(3)]
        slots.append(np.concatenate(g, axis=1))
    for dc in range(NCH):
        b = [_kc_tile(wb[i], dc * 128 + np.arange(128)) for i in range(3)]
        slots.append(np.concatenate(b, axis=1))
    w_o = ins["w_o"][l]
    for s in range(2):
        slots.append(_kc_tile(w_o, np.arange(s * 512, (s + 1) * 512)))
    ffn(ins["ffn2_w13"][l], ins["ffn2_w2"][l])
    return slots


def _slot_widths():
    ffn = [SLOTW] * 11 + [SLOTW] * 5 + [NCH * NHC * 128 - 5 * SLOTW]
    mix = [SLOTW] * 7 + [3072] * NCH + [1536] * NCH + [SLOTW] * 2
    return ffn + mix + ffn


def _fm(v, n):
    return np.ascontiguousarray(v.reshape(n, 128).T)


def prep_inputs(ins, n_layers, n_tiles, batch_ids):
    widths = _slot_widths()
    wst = np.empty((n_layers, 128 * sum(widths)), np.float32)
    for l in range(n_layers):
        off = 0
        for s, w in zip(_layer_slots(l, ins), widths):
            assert s.shape == (128, w), (s.shape, w)
            wst[l, off:off + 128 * w] = s.reshape(-1)
            off += 128 * w
    wada = np.empty((n_layers, 18, 128, SLOTW), np.float32)
    for l in range(n_layers):
        for s in range(18):
            wada[l, s] = _kc_tile(ins["w_ada"][l], np.arange(s * 512, (s + 1) * 512))
    nsp = n_layers * LSP + 16
    spw = np.empty((n_layers, 128, 1024), np.float32)
    sp_common = np.zeros((128, nsp), np.float32)
    for l in range(n_layers):
        b = l * LSP
        sp_common[:, b + 0:b + 8] = _fm(ins["g_ffn1"][l], 8)
        sp_common[:, b + 8:b + 16] = _fm(ins["g_mix"][l], 8)
        sp_common[:, b + 16:b + 24] = _fm(ins["g_ffn2"][l], 8)
        sp_common[:, b + 24:b + 96] = _fm(ins["b_ada"][l], 72)
        sp_common[:, b + 96:b + 100] = _fm(ins["a_ln_g"][l], 4)
        sp_common[:, b + 100:b + 104] = _fm(ins["a_ln_b"][l], 4)
        bc = ins["b_conv"][l].reshape(3, 4, 128).transpose(2, 1, 0).reshape(128, 12)
        cc = ins["c_conv"][l].reshape(31, 4, 128).transpose(2, 1, 0).reshape(128, 124)
        sp_common[:, b + 104:b + 116] = bc
        sp_common[:, b + 116:b + 240] = cc
        sp_common[:, b + 240:b + 244] = _fm(ins["c_conv_b"][l], 4)
        sp_common[:, b + 244:b + 248] = _fm(ins["c_ln_g"][l], 4)
        sp_common[:, b + 248:b + 252] = _fm(ins["c_ln_b"][l], 4)
        spw[l, :, 0:512] = ins["a_ws"][l].transpose(2, 0, 1).reshape(128, 512)
        spw[l, :, 512:1024] = np.broadcast_to(ins["a_bs"][l].reshape(1, 512), (128, 512))
    gb = n_layers * LSP
    sp_common[:, gb:gb + 8] = _fm(ins["g_final"], 8)
    maps = []
    for bi in batch_ids:
        sp = sp_common.copy()
        sp[:, gb + 8:gb + 16] = _fm(ins["c"][bi], 8)
        xT = np.ascontiguousarray(ins["x"][bi, :n_tiles * T, :].T)
        maps.append({"xT": xT, "wst": wst, "wada": wada, "sp": sp, "spw": spw})
    return maps


class Res:
    __slots__ = ("w", "r", "name")

    def __init__(self, name=""):
        self.w = None
        self.r = {}
        self.name = name


class Sched:
    ENGS = ("pe", "act", "dve", "pool", "sp")

    def __init__(self):
        self.ops = {e: [] for e in self.ENGS}
        self.cnt = {e: 0 for e in self.ENGS}
        self.seen = {e: {} for e in self.ENGS}
        self.clock = {e: [None] for e in self.ENGS}
        self.dma_cnt = {}

    def _deps(self, e, reads, writes):
        raw, other = {}, {}

        def add(dst, kv):
            k, v = kv
            if dst.get(k, 0) < v:
                dst[k] = v
        for r in reads:
            if r.w is not None:
                add(raw, r.w)
        for r in writes:
            if r.w is not None:
                add(other, r.w)
            for kv in r.r.items():
                add(other, kv)
        waits = []
        seen = self.seen[e]
        changed = False
        for src, is_raw in ((raw, True), (other, False)):
            for k, v in src.items():
                if k == e:
                    if e in ("pe", "pool", "sp") or not is_raw:
                        continue
                if seen.get(k, 0) >= v:
                    continue
                if not changed:
                    seen = dict(seen)
                    changed = True
                seen[k] = v
                waits.append((k, v))
                if k in self.clock:
                    snap = self.clock[k][v]
                    for k2, v2 in snap.items():
                        if seen.get(k2, 0) < v2:
                            seen[k2] = v2
        if changed:
            self.seen[e] = seen
        return waits

    def op(self, e, fn, reads=(), writes=()):
        waits = self._deps(e, reads, writes)
        self.cnt[e] += 1
        n = self.cnt[e]
        self.clock[e].append(self.seen[e])
        self.ops[e].append((waits, fn, (e, 1)))
        for r in writes:
            r.w = (e, n)
            r.r = {}
        for r in reads:
            if r.r.get(e, 0) < n:
                r.r[e] = n

    def dma(self, q, fn, semname, reads=(), writes=()):
        waits = self._deps(q, reads, writes)
        prev = self.dma_cnt.get(semname, 0)
        if prev and self.seen[q].get(semname, 0) < prev:
            s = dict(self.seen[q])
            s[semname] = prev
            self.seen[q] = s
            waits.append((semname, prev))
        val = prev + 16
        self.dma_cnt[semname] = val
        self.ops[q].append((waits, fn, (semname, 16)))
        for r in writes:
            r.w = (semname, val)
            r.r = {}
        for r in reads:
            r.r[semname] = val

    def final_wait(self, e, semname):
        v = self.dma_cnt.get(semname, 0)
        if v and self.seen[e].get(semname, 0) < v:
            self.ops[e].append(([(semname, v)], None, None))


def build_program(n_layers, n_tiles, final_norm=True):
    nc = bass.Bass("TRN2", target_bir_lowering=False)
    S = n_tiles * T
    widths = _slot_widths()
    offs = np.concatenate([[0], np.cumsum([128 * w for w in widths])]).astype(np.int64)
    nsp = n_layers * LSP + 16
    xT = nc.dram_tensor("xT", [D, S], F32, kind="ExternalInput").ap()
    wst = nc.dram_tensor("wst", [n_layers, int(offs[-1])], F32, kind="ExternalInput").ap()
    wada = nc.dram_tensor("wada", [n_layers, 18, 128, SLOTW], F32, kind="ExternalInput").ap()
    spd = nc.dram_tensor("sp", [128, nsp], F32, kind="ExternalInput").ap()
    spwd = nc.dram_tensor("spw", [n_layers, 128, 1024], F32, kind="ExternalInput").ap()
    outT = nc.dram_tensor("outT", [D, S], F32, kind="ExternalOutput").ap()

    sch = Sched()
    es = contextlib.ExitStack()

    def sb(name, shape, dt):
        return es.enter_context(nc.sbuf_tensor(name, shape, dt))

    X = sb("X", [128, NCH, T], F32)
    OST = sb("OST", [128, 2, T], F32)
    SQ = sb("SQ", [128, NCH, T], BF16)
    RSTD = sb("RSTD", [128, T], F32)
    H = sb("H", [128, NCH, T], BF16)
    G = sb("G", [128, NHC, T], BF16)
    NTMP = 6
    TMP = sb("TMP", [128, NTMP, T], F32)
    SA = sb("SA", [128, 4, T], F32)
    SB_ = sb("SBb", [128, 4, T], F32)
    SC = sb("SC", [128, 4, HALO + T], F32)
    SD = sb("SD", [128, 4, T], F32)
    VN = sb("VN", [128, 4, 512], BF16)
    ST6 = sb("ST6", [128, 4, 8], F32)
    MV = sb("MV", [128, 4, 2], F32)
    STAT = sb("STAT", [128, 2, T], F32)
    GS = sb("GS", [128, 24, T], BF16)
    SCB = sb("SCB", [128, 4, T + 2], F32)
    CW = sb("CW", [128, n_layers, 124], F32)
    RINGB = sb("RINGB", [128, RING, SLOTW], BF16)
    SP = sb("SP", [128, nsp], F32)
    SCR = TMP[:, 0:2, :].rearrange("p a b -> p (a b)")
    DER = sb("DER", [128, n_layers, 72], F32)
    GF = sb("GF", [128, 8], F32)
    WMT = sb("WMT", [128, n_layers, 4, 128], BF16)
    BM = sb("BM", [128, n_layers, 4, 128], F32)
    ONES = sb("ONES", [128, 128], BF16)
    CACT = sb("CACT", [128, 8], BF16)
    CST = sb("CST", [128, 2], F32)
    TAILB = sb("TAILB", [128, n_layers, 4, 2], F32)
    TAILC = sb("TAILC", [128, n_layers, 4, 30], F32)
    PS = [es.enter_context(nc.psum_tensor(f"ps{i}", [128, 512], F32)) for i in range(8)]

    rX = [Res(f"X{c}") for c in range(NCH)]
    rOST = [Res() for _ in range(2)]
    rSQ = [Res() for _ in range(NCH)]
    rRSTD = Res()
    rH = [Res() for _ in range(NCH)]
    rG = [Res() for _ in range(NHC)]
    rTMP = [Res() for _ in range(NTMP)]
    rSA = [Res() for _ in range(4)]
    rSB = [Res() for _ in range(4)]
    rSC = [Res() for _ in range(4)]
    rSD = [Res() for _ in range(4)]
    rVN = [Res() for _ in range(4)]
    rST = [Res() for _ in range(4)]
    rSTAT = [Res() for _ in range(2)]
    rGS = [Res() for _ in range(24)]
    rSCB = [Res() for _ in range(4)]
    rCW = [Res() for _ in range(n_layers)]
    rRING = [Res() for _ in range(RING)]
    rSP = Res()
    rDER = [Res() for _ in range(n_layers)]
    rGF = Res()
    rWMT = [Res() for _ in range(n_layers)]
    rBM = [Res() for _ in range(n_layers)]
    rONES = Res()
    rCACT = Res()
    rCST = Res()
    rTB = [Res() for _ in range(n_layers)]
    rTC = [Res() for _ in range(n_layers)]
    rPS = [Res() for _ in range(8)]

    state = {"bank": 0, "tmp": 0, "slot": 0}

    def bank():
        b = state["bank"]
        if state.get("reserve7") and b == 7:
            b = 0
        state["bank"] = (b + 1) % 8
        return b

    def tmp():
        t = state["tmp"]
        state["tmp"] = (t + 1) % NTMP
        return t

    def load_slot(src_ap, width):
        n = state["slot"]
        state["slot"] = n + 1
        i = n % RING
        dst = RINGB[:, i, 0:width]
        sch.dma("pool", lambda e, dst=dst, src=src_ap: e.dma_start(out=dst, in_=src), f"ring{i}",
                writes=[rRING[i]])
        return i

    def wslot(l, k):
        if state.get("ada_q"):
            state["ada_ctr"] = state.get("ada_ctr", 0) + 1
            if state["ada_ctr"] % 3 == 0:
                state["ada_q"].pop(0)()
        w = widths[k]
        src = wst[l, int(offs[k]):int(offs[k + 1])].rearrange("(p w) -> p w", p=128)
        return load_slot(src, w)

    sch.dma("sp", lambda e: e.dma_start(out=SP[:], in_=spd), "spld", writes=[rSP])
    sch.op("dve", lambda e: e.memset(ONES[:], 1.0), writes=[rONES])
    sch.op("dve", lambda e: e.memset(CST[:, 0:1], float(EPS * D)), writes=[rCST])
    sch.op("dve", lambda e: e.memset(CST[:, 1:2], float(EPS)), writes=[rCST])
    sch.op("dve", lambda e: e.memset(TAILB[:], 0.0), writes=rTB)
    sch.op("dve", lambda e: e.memset(TAILC[:], 0.0), writes=rTC)
    gb = n_layers * LSP
    sch.op("act", lambda e: e.activation(out=CACT[:], in_=SP[:, gb + 8:gb + 16], func=AF.Silu),
           reads=[rSP], writes=[rCACT])
    sch.op("dve", lambda e: e.tensor_scalar(out=GF[:], in0=SP[:, gb:gb + 8], scalar1=float(np.sqrt(D)),
                                            scalar2=None, op0=ALU.mult), reads=[rSP], writes=[rGF])
    for l in range(n_layers):
        b0 = l * LSP
        sch.op("dve", lambda e, l=l, b0=b0: e.tensor_scalar(out=CW[:, l, :], in0=SP[:, b0 + 116:b0 + 240], scalar1=0.5,
                                                         scalar2=None, op0=ALU.mult), reads=[rSP], writes=[rCW[l]])
        sch.dma("sp", lambda e, l=l: e.dma_start(out=SCR[:], in_=spwd[l]), "scrld", writes=[rTMP[0], rTMP[1]])
        sch.op("dve", lambda e, l=l: e.tensor_copy(out=WMT[:, l].rearrange("p g q -> p (g q)"), in_=SCR[:, 0:512]),
               reads=[rTMP[0], rTMP[1]], writes=[rWMT[l]])
        sch.op("dve", lambda e, l=l: e.memset(WMT[64:128, l, :, 0:64], 0.0), writes=[rWMT[l]])
        bk = bank()
        sch.op("pe", lambda e, l=l, bk=bk: e.matmul(PS[bk][:], ONES[:], WMT[:, l].rearrange("p g q -> p (g q)"),
                                                   start=True, stop=True),
               reads=[rONES, rWMT[l]], writes=[rPS[bk]])
        for g in range(4):
            sch.op("dve", lambda e, l=l, g=g, bk=bk, b0=b0: e.scalar_tensor_tensor(
                out=BM[:, l, g, :], in0=PS[bk][:, g * 128:(g + 1) * 128], scalar=SP[:, b0 + 100 + g:b0 + 101 + g],
                in1=SCR[:, 512 + g * 128:512 + (g + 1) * 128], op0=ALU.mult, op1=ALU.add),
                reads=[rPS[bk], rSP, rTMP[0], rTMP[1]], writes=[rBM[l]])

    def ada_begin(l):
        q = state.setdefault("ada_q", [])
        for s in range(18):
            q.append(lambda s=s: ada_slot(l, s))
        q.append(lambda: ada_finish(l))

    def ada_flush():
        q = state.get("ada_q", [])
        while q:
            q.pop(0)()

    def ada_slot(l, s):
        bk = 7
        if True:
            i = load_slot(wada[l, s], SLOTW)

            def f(e, i=i, s=s, bk=bk):
                ins = None
                for cc in range(4):
                    col = s * 4 + cc
                    for kc in range(NCH):
                        ins = e.matmul(PS[bk][:, col:col + 1],
                                       RINGB[:, i, kc * 512 + cc * 128:kc * 512 + (cc + 1) * 128],
                                       CACT[:, kc:kc + 1], start=(kc == 0), stop=(kc == NCH - 1))
                return ins
            sch.op("pe", f, reads=[rRING[i], rCACT], writes=[rPS[bk]])

    def ada_finish(l):
        b0 = l * LSP
        bk = 7
        dl = DER[:, l, :]
        sch.op("dve", lambda e: e.tensor_tensor(out=dl, in0=PS[bk][:, 0:72], in1=SP[:, b0 + 24:b0 + 96], op=ALU.add),
               reads=[rPS[bk], rSP], writes=[rDER[l]])
        for j, gcol in ((1, 0), (4, 8), (7, 16)):
            sch.op("dve", lambda e, j=j, gcol=gcol: e.scalar_tensor_tensor(
                out=DER[:, l, j * 8:j * 8 + 8], in0=DER[:, l, j * 8:j * 8 + 8], scalar=1.0,
                in1=SP[:, b0 + gcol:b0 + gcol + 8], op0=ALU.add, op1=ALU.mult),
                reads=[rDER[l], rSP], writes=[rDER[l]])
            sch.op("dve", lambda e, j=j: e.tensor_scalar(
                out=DER[:, l, j * 8:j * 8 + 8], in0=DER[:, l, j * 8:j * 8 + 8], scalar1=float(np.sqrt(D)),
                scalar2=None, op0=ALU.mult), reads=[rDER[l]], writes=[rDER[l]])
        for j in (2, 5, 8):
            sch.op("dve", lambda e, j=j: e.tensor_scalar(
                out=DER[:, l, j * 8:j * 8 + 8], in0=DER[:, l, j * 8:j * 8 + 8], scalar1=0.5,
                scalar2=None, op0=ALU.mult), reads=[rDER[l]], writes=[rDER[l]])

    def rms_stats():
        for c in range(NCH):
            sch.op("act", lambda e, c=c: e.activation(out=SQ[:, c, :], in_=X[:, c, :], func=AF.Square),
                   reads=[rX[c]], writes=[rSQ[c]])
        bk = bank()

        def f(e):
            ins = None
            for c in range(NCH):
                ins = e.matmul(PS[bk][:], ONES[:], SQ[:, c, :], start=(c == 0), stop=(c == NCH - 1))
            return ins
        sch.op("pe", f, reads=rSQ + [rONES], writes=[rPS[bk]])
        sch.op("act", lambda e: e.activation(out=RSTD[:], in_=PS[bk][:], func=AF.Sqrt, bias=CST[:, 0:1], scale=1.0),
               reads=[rPS[bk], rCST], writes=[rRSTD])
        sch.op("dve", lambda e: e.reciprocal(out=RSTD[:], in_=RSTD[:]), reads=[rRSTD], writes=[rRSTD])

    def norm(l, jshift, jscale):
        rms_stats()
        for c in range(NCH):
            t = tmp()
            sch.op("dve", lambda e, c=c, t=t: e.tensor_tensor(out=TMP[:, t, :], in0=X[:, c, :], in1=RSTD[:], op=ALU.mult),
                   reads=[rX[c], rRSTD], writes=[rTMP[t]])
            sch.op("act", lambda e, c=c, t=t: e.activation(
                out=H[:, c, :], in_=TMP[:, t, :], func=AF.Identity,
                bias=DER[:, l, jshift * 8 + c:jshift * 8 + c + 1], scale=DER[:, l, jscale * 8 + c:jscale * 8 + c + 1]),
                reads=[rTMP[t], rDER[l]], writes=[rH[c]])

    def mm_group(out_ap, pairs, reads, bk):
        def f(e):
            ins = None
            n = len(pairs)
            for k, (lhsT, rhs) in enumerate(pairs):
                ins = e.matmul(out_ap, lhsT, rhs, start=(k == 0), stop=(k == n - 1))
            return ins
        sch.op("pe", f, reads=reads, writes=[rPS[bk]])

    def ffn_stage(l, k0, jshift, jscale, jgate):
        norm(l, jshift, jscale)
        for j in range(NHC):
            if j % 2 == 0:
                i = wslot(l, k0 + j // 2)
            jj = j % 2
            ba, bb = bank(), bank()
            for q, bk in ((jj, ba), (2 + jj, bb)):
                mm_group(PS[bk][:], [(RINGB[:, i, kc * 512 + q * 128:kc * 512 + (q + 1) * 128], H[:, kc, :])
                                      for kc in range(NCH)], [rRING[i]] + rH, bk)
            t = tmp()
            sch.op("act", lambda e, t=t, ba=ba: e.activation(out=TMP[:, t, :], in_=PS[ba][:], func=AF.Silu),
                   reads=[rPS[ba]], writes=[rTMP[t]])
            sch.op("dve", lambda e, t=t, bb=bb, j=j: e.tensor_tensor(out=G[:, j, :], in0=TMP[:, t, :], in1=PS[bb][:], op=ALU.mult),
                   reads=[rTMP[t], rPS[bb]], writes=[rG[j]])
        cur = {}
        for dc in range(NCH):
            pairs, reads = [], list(rG)
            for hc in range(NHC):
                bi = dc * NHC + hc
                s = bi // 32
                if s not in cur:
                    cur.clear()
                    cur[s] = wslot(l, k0 + 11 + s)
                i = cur[s]
                if rRING[i] not in reads:
                    reads.append(rRING[i])
                pairs.append((i, (bi % 32) * 128, hc))
            bk = bank()
            mm_group(PS[bk][:], [(RINGB[:, i, o:o + 128], G[:, hc, :]) for (i, o, hc) in pairs], reads, bk)
            sch.op("dve", lambda e, dc=dc, bk=bk: e.scalar_tensor_tensor(
                out=X[:, dc, :], in0=PS[bk][:], scalar=DER[:, l, jgate * 8 + dc:jgate * 8 + dc + 1], in1=X[:, dc, :],
                op0=ALU.mult, op1=ALU.add), reads=[rPS[bk], rDER[l], rX[dc]], writes=[rX[dc]])

    def mixer_stage(l, k0):
        b0 = l * LSP
        norm(l, 3, 4)
        pend = []

        def drip(n):
            for _ in range(min(n, len(pend))):
                pend.pop(0)()

        def std_groups(i, banks=None):
            for m in range(4):
                bk = banks[m] if banks is not None else bank()
                mm_group(PS[bk][:], [(RINGB[:, i, kc * 512 + m * 128:kc * 512 + (m + 1) * 128], H[:, kc, :])
                                      for kc in range(NCH)], [rRING[i]] + rH, bk)
                yield m, bk

        i = wslot(l, k0 + 0)
        for m, bk in std_groups(i):
            sch.op("act", lambda e, m=m, bk=bk: e.activation(out=SB_[:, m, :], in_=PS[bk][:], func=AF.Tanh, scale=0.5),
                   reads=[rPS[bk]], writes=[rSB[m]])
        i = wslot(l, k0 + 1)
        for m, bk in std_groups(i):
            sch.op("act", lambda e, m=m: e.activation(out=SC[:, m, HALO - 30:HALO], in_=TAILC[:, l, m, :], func=AF.Identity),
                   reads=[rTC[l]], writes=[rSC[m]])
            sch.op("dve", lambda e, m=m, bk=bk: e.scalar_tensor_tensor(
                out=SC[:, m, HALO:HALO + T], in0=SB_[:, m, :], scalar=1.0, in1=PS[bk][:], op0=ALU.add, op1=ALU.mult),
                reads=[rSB[m], rPS[bk]], writes=[rSC[m]])
            sch.op("act", lambda e, m=m: e.activation(out=TAILC[:, l, m, :], in_=SC[:, m, HALO + T - 30:HALO + T], func=AF.Identity),
                   reads=[rSC[m]], writes=[rTC[l]])
        for k in range(31):
            for m in range(4):
                src = SC[:, m, HALO - 30 + k:HALO - 30 + k + T]
                wv = CW[:, l, m * 31 + k:m * 31 + k + 1]
                if k == 0:
                    pend.append(lambda m=m, wv=wv, src=src: sch.op("dve", lambda e: e.tensor_scalar(
                        out=SD[:, m, :], in0=src, scalar1=wv, scalar2=SP[:, b0 + 240 + m:b0 + 241 + m],
                        op0=ALU.mult, op1=ALU.add), reads=[rSC[m], rCW[l], rSP], writes=[rSD[m]]))
                else:
                    pend.append(lambda m=m, wv=wv, src=src: sch.op("dve", lambda e: e.scalar_tensor_tensor(
                        out=SD[:, m, :], in0=src, scalar=wv, in1=SD[:, m, :],
                        op0=ALU.mult, op1=ALU.add), reads=[rSC[m], rCW[l], rSD[m]], writes=[rSD[m]]))
        NDRIP = 4
        i = wslot(l, k0 + 2)
        for m, bk in std_groups(i):
            sch.op("act", lambda e, m=m, bk=bk: e.activation(out=SA[:, m, :], in_=PS[bk][:], func=AF.Gelu),
                   reads=[rPS[bk]], writes=[rSA[m]])
            drip(NDRIP)
        i = wslot(l, k0 + 3)
        mb = [bank() for _ in range(4)]
        sgu_pending = []
        vbanks = []
        for tb in range(4):
            bk = bank()
            while bk in mb:
                bk = bank()
            vbanks.append(bk)
            mm_group(PS[bk][:], [(H[:, kc, tb * 128:(tb + 1) * 128], RINGB[:, i, kc * 512:(kc + 1) * 512])
                                  for kc in range(NCH)], [rRING[i]] + rH, bk)
            t = tmp()
            v = tb
            sch.op("act", lambda e, t=t, bk=bk: e.activation(out=TMP[:, t, :], in_=PS[bk][:], func=AF.Gelu),
                   reads=[rPS[bk]], writes=[rTMP[t]])
            sch.op("dve", lambda e, t=t, v=v: e.bn_stats(out=ST6[:, v, 0:6], in_=TMP[:, t, :]),
                   reads=[rTMP[t]], writes=[rST[v]])
            sch.op("dve", lambda e, v=v: e.bn_aggr(out=MV[:, v, :], in_=ST6[:, v, 0:6]),
                   reads=[rST[v]], writes=[rST[v]])
            sch.op("act", lambda e, v=v: e.activation(out=MV[:, v, 1:2], in_=MV[:, v, 1:2], func=AF.Sqrt,
                                                     bias=CST[:, 1:2], scale=1.0),
                   reads=[rST[v], rCST], writes=[rST[v]])
            sch.op("dve", lambda e, v=v: e.reciprocal(out=MV[:, v, 1:2], in_=MV[:, v, 1:2]),
                   reads=[rST[v]], writes=[rST[v]])
            sch.op("dve", lambda e, t=t, v=v: e.tensor_scalar(out=VN[:, v, :], in0=TMP[:, t, :], scalar1=MV[:, v, 0:1],
                                                             scalar2=MV[:, v, 1:2], op0=ALU.subtract, op1=ALU.mult),
                   reads=[rTMP[t], rST[v]], writes=[rVN[v]])

            def sgu(v=v, tb=tb):
                def f(e):
                    ins = None
                    for g in range(4):
                        ins = e.matmul(PS[mb[g]][:, tb * 128:(tb + 1) * 128], VN[:, v, g * 128:(g + 1) * 128],
                                       WMT[:, l, g, :], start=True, stop=True)
                    return ins
                sch.op("pe", f, reads=[rVN[v], rWMT[l]], writes=[rPS[b] for b in mb])
            sgu_pending.append(sgu)
            drip(NDRIP)
        i = wslot(l, k0 + 4)
        for m, bk in std_groups(i, vbanks):
            if sgu_pending:
                sgu_pending.pop(0)()
            sch.op("act", lambda e, m=m, bk=bk: e.activation(out=SB_[:, m, :], in_=PS[bk][:], func=AF.Identity),
                   reads=[rPS[bk]], writes=[rSB[m]])
            drip(NDRIP)
        for g in range(4):
            t = tmp()
            sch.op("dve", lambda e, g=g, t=t: e.scalar_tensor_tensor(
                out=TMP[:, t, :].rearrange("p (a b) -> p a b", a=4),
                in0=PS[mb[g]][:].rearrange("p (a b) -> p a b", a=4),
                scalar=SP[:, b0 + 96 + g:b0 + 97 + g],
                in1=BM[:, l, g, :].unsqueeze(1).broadcast_to([128, 4, 128]),
                op0=ALU.mult, op1=ALU.add), reads=[rPS[mb[g]], rSP, rBM[l]], writes=[rTMP[t]])
            sch.op("dve", lambda e, g=g, t=t: e.tensor_tensor(out=G[:, g, :], in0=SA[:, g, :], in1=TMP[:, t, :], op=ALU.mult),
                   reads=[rSA[g], rTMP[t]], writes=[rG[g]])
        i = wslot(l, k0 + 5)
        for m, bk in std_groups(i):
            sch.op("act", lambda e, m=m: e.activation(out=SCB[:, m, 0:2], in_=TAILB[:, l, m, :], func=AF.Identity),
                   reads=[rTB[l]], writes=[rSCB[m]])
            sch.op("dve", lambda e, m=m, bk=bk: e.tensor_tensor(out=SCB[:, m, 2:2 + T], in0=SB_[:, m, :], in1=PS[bk][:], op=ALU.mult),
                   reads=[rSB[m], rPS[bk]], writes=[rSCB[m]])
            sch.op("act", lambda e, m=m: e.activation(out=TAILB[:, l, m, :], in_=SCB[:, m, T:T + 2], func=AF.Identity),
                   reads=[rSCB[m]], writes=[rTB[l]])
            wb = b0 + 104 + m * 3
            sch.op("dve", lambda e, m=m, wb=wb: e.tensor_scalar(out=SB_[:, m, :], in0=SCB[:, m, 0:T],
                                                               scalar1=SP[:, wb:wb + 1], scalar2=None, op0=ALU.mult),
                   reads=[rSCB[m], rSP], writes=[rSB[m]])
            for k in (1, 2):
                sch.op("dve", lambda e, m=m, wb=wb, k=k: e.scalar_tensor_tensor(
                    out=SB_[:, m, :], in0=SCB[:, m, k:k + T], scalar=SP[:, wb + k:wb + k + 1],
                    in1=SB_[:, m, :], op0=ALU.mult, op1=ALU.add), reads=[rSCB[m], rSP, rSB[m]], writes=[rSB[m]])
            drip(NDRIP)
        i = wslot(l, k0 + 6)
        for m, bk in std_groups(i):
            sch.op("dve", lambda e, m=m, bk=bk: e.tensor_tensor(out=G[:, 4 + m, :], in0=SB_[:, m, :], in1=PS[bk][:], op=ALU.mult),
                   reads=[rSB[m], rPS[bk]], writes=[rG[4 + m]])
            drip(NDRIP)

        def gate_slot(dc):
            ig = wslot(l, k0 + 7 + dc)
            for i3 in range(3):
                bk = bank()
                mm_group(PS[bk][:], [(RINGB[:, ig, (i3 * 8 + kc) * 128:(i3 * 8 + kc + 1) * 128], H[:, kc, :])
                                      for kc in range(NCH)], [rRING[ig]] + rH, bk)
                sch.op("act", lambda e, i3=i3, bk=bk, dc=dc: e.activation(out=GS[:, dc * 3 + i3, :], in_=PS[bk][:],
                                                                       func=AF.Tanh, scale=0.5),
                       reads=[rPS[bk]], writes=[rGS[dc * 3 + i3]])
                drip(NDRIP + 2)
        for dc in range(5):
            gate_slot(dc)
        drip(len(pend))
        for m in range(4):
            sch.op("act", lambda e, m=m: e.activation(out=G[:, 12 + m, :], in_=SD[:, m, :], func=AF.Identity),
                   reads=[rSD[m]], writes=[rG[12 + m]])
            sch.op("act", lambda e, m=m: e.activation(out=G[:, 16 + m, :], in_=SD[:, m, :], func=AF.Square),
                   reads=[rSD[m]], writes=[rG[16 + m]])
        gate_slot(5)
        b1, b2 = bank(), bank()
        mm_group(PS[b1][:], [(ONES[:], G[:, 12 + m, :]) for m in range(4)], [rONES] + rG[12:16], b1)
        mm_group(PS[b2][:], [(ONES[:], G[:, 16 + m, :]) for m in range(4)], [rONES] + rG[16:20], b2)
        sch.op("dve", lambda e: e.tensor_scalar(out=STAT[:, 0, :], in0=PS[b1][:], scalar1=1.0 / 512, scalar2=None, op0=ALU.mult),
               reads=[rPS[b1]], writes=[rSTAT[0]])
        sch.op("dve", lambda e: e.tensor_tensor(out=STAT[:, 1, :], in0=STAT[:, 0, :], in1=STAT[:, 0, :], op=ALU.mult),
               reads=[rSTAT[0]], writes=[rSTAT[1]])
        sch.op("dve", lambda e: e.scalar_tensor_tensor(out=STAT[:, 1, :], in0=PS[b2][:], scalar=1.0 / 512, in1=STAT[:, 1, :],
                                                       op0=ALU.mult, op1=ALU.subtract),
               reads=[rPS[b2], rSTAT[1]], writes=[rSTAT[1]])
        sch.op("act", lambda e: e.activation(out=STAT[:, 1, :], in_=STAT[:, 1, :], func=AF.Sqrt, bias=CST[:, 1:2], scale=1.0),
               reads=[rSTAT[1], rCST], writes=[rSTAT[1]])
        sch.op("dve", lambda e: e.reciprocal(out=STAT[:, 1, :], in_=STAT[:, 1, :]), reads=[rSTAT[1]], writes=[rSTAT[1]])
        gate_slot(6)
        for m in range(4):
            t = tmp()
            sch.op("dve", lambda e, m=m, t=t: e.tensor_tensor(out=TMP[:, t, :], in0=SD[:, m, :], in1=STAT[:, 0, :], op=ALU.subtract),
                   reads=[rSD[m], rSTAT[0]], writes=[rTMP[t]])
            sch.op("dve", lambda e, t=t: e.tensor_tensor(out=TMP[:, t, :], in0=TMP[:, t, :], in1=STAT[:, 1, :], op=ALU.mult),
                   reads=[rTMP[t], rSTAT[1]], writes=[rTMP[t]])
            sch.op("act", lambda e, m=m, t=t: e.activation(
                out=G[:, 8 + m, :], in_=TMP[:, t, :], func=AF.Silu,
                bias=SP[:, b0 + 248 + m:b0 + 249 + m], scale=SP[:, b0 + 244 + m:b0 + 245 + m]),
                reads=[rTMP[t], rSP], writes=[rG[8 + m]])
        gate_slot(7)
        for dc in range(NCH):
            ib = wslot(l, k0 + 15 + dc)
            pbk = []
            for i3 in range(3):
                bk = bank()
                pbk.append(bk)
                mm_group(PS[bk][:], [(RINGB[:, ib, (i3 * 4 + kc) * 128:(i3 * 4 + kc + 1) * 128], G[:, i3 * 4 + kc, :])
                                      for kc in range(4)], [rRING[ib]] + rG[i3 * 4:i3 * 4 + 4], bk)
            ta, tb_ = tmp(), tmp()

            def gp(e, out, i3, bk):
                return e.scalar_tensor_tensor(out=out, in0=GS[:, dc * 3 + i3, :], scalar=1.0, in1=PS[bk][:],
                                              op0=ALU.add, op1=ALU.mult)
            sch.op("dve", lambda e, ta=ta, bk=pbk[0], dc=dc: e.scalar_tensor_tensor(
                out=TMP[:, ta, :], in0=GS[:, dc * 3 + 0, :], scalar=1.0, in1=PS[bk][:], op0=ALU.add, op1=ALU.mult),
                reads=[rGS[dc * 3 + 0], rPS[pbk[0]]], writes=[rTMP[ta]])
            sch.op("dve", lambda e, tb_=tb_, bk=pbk[1], dc=dc: e.scalar_tensor_tensor(
                out=TMP[:, tb_, :], in0=GS[:, dc * 3 + 1, :], scalar=1.0, in1=PS[bk][:], op0=ALU.add, op1=ALU.mult),
                reads=[rGS[dc * 3 + 1], rPS[pbk[1]]], writes=[rTMP[tb_]])
            sch.op("dve", lambda e, ta=ta, tb_=tb_: e.tensor_tensor(out=TMP[:, ta, :], in0=TMP[:, ta, :], in1=TMP[:, tb_, :], op=ALU.add),
                   reads=[rTMP[ta], rTMP[tb_]], writes=[rTMP[ta]])
            sch.op("dve", lambda e, tb_=tb_, bk=pbk[2], dc=dc: e.scalar_tensor_tensor(
                out=TMP[:, tb_, :], in0=GS[:, dc * 3 + 2, :], scalar=1.0, in1=PS[bk][:], op0=ALU.add, op1=ALU.mult),
                reads=[rGS[dc * 3 + 2], rPS[pbk[2]]], writes=[rTMP[tb_]])
            sch.op("dve", lambda e, ta=ta, tb_=tb_, dc=dc: e.tensor_tensor(out=SQ[:, dc, :], in0=TMP[:, ta, :], in1=TMP[:, tb_, :], op=ALU.add),
                   reads=[rTMP[ta], rTMP[tb_]], writes=[rSQ[dc]])
        for dc in range(NCH):
            if dc % 4 == 0:
                i = wslot(l, k0 + 23 + dc // 4)
            bk = bank()
            mm_group(PS[bk][:], [(RINGB[:, i, kc * 512 + (dc % 4) * 128:kc * 512 + (dc % 4 + 1) * 128], SQ[:, kc, :])
                                  for kc in range(NCH)], [rRING[i]] + rSQ, bk)
            sch.op("dve", lambda e, dc=dc, bk=bk: e.scalar_tensor_tensor(
                out=X[:, dc, :], in0=PS[bk][:], scalar=DER[:, l, 5 * 8 + dc:5 * 8 + dc + 1], in1=X[:, dc, :],
                op0=ALU.mult, op1=ALU.add), reads=[rPS[bk], rDER[l], rX[dc]], writes=[rX[dc]])

    def load_x(ti):
        for c in range(NCH):
            sch.dma("sp", lambda e, c=c, ti=ti: e.dma_start(out=X[:, c, :], in_=xT[c * 128:(c + 1) * 128, ti * T:(ti + 1) * T]),
                    f"xld{c}", writes=[rX[c]])

    load_x(0)
    for ti in range(n_tiles):
        for l in range(n_layers):
            if ti == 0:
                state["reserve7"] = True
                if l == 0:
                    ada_begin(0)
                ada_flush()
                if l + 1 < n_layers:
                    ada_begin(l + 1)
            else:
                state["reserve7"] = False
            ffn_stage(l, 0, 0, 1, 2)
            mixer_stage(l, 17)
            ffn_stage(l, 42, 6, 7, 8)
        if final_norm:
            rms_stats()
        for c in range(NCH):
            o = c % 2
            if final_norm:
                t = tmp()
                sch.op("dve", lambda e, c=c, t=t: e.tensor_tensor(out=TMP[:, t, :], in0=X[:, c, :], in1=RSTD[:], op=ALU.mult),
                       reads=[rX[c], rRSTD], writes=[rTMP[t]])
                sch.op("act", lambda e, c=c, t=t, o=o: e.activation(out=OST[:, o, :], in_=TMP[:, t, :], func=AF.Identity,
                                                                  scale=GF[:, c:c + 1]),
                       reads=[rTMP[t], rGF], writes=[rOST[o]])
            else:
                sch.op("act", lambda e, c=c, o=o: e.activation(out=OST[:, o, :], in_=X[:, c, :], func=AF.Identity),
                       reads=[rX[c]], writes=[rOST[o]])
            sch.dma("sp", lambda e, c=c, ti=ti, o=o: e.dma_start(out=outT[c * 128:(c + 1) * 128, ti * T:(ti + 1) * T], in_=OST[:, o, :]),
                    f"ost{o}", reads=[rOST[o]])
            if ti + 1 < n_tiles:
                sch.dma("sp", lambda e, c=c, ti=ti: e.dma_start(out=X[:, c, :], in_=xT[c * 128:(c + 1) * 128, (ti + 1) * T:(ti + 2) * T]),
                        f"xld{c}", writes=[rX[c]])
    sch.final_wait("sp", "ost0")
    sch.final_wait("sp", "ost1")

    semnames = list(Sched.ENGS) + sorted(sch.dma_cnt.keys())
    sems = {n: es.enter_context(nc.semaphore("s_" + n)) for n in semnames}
    block = es.enter_context(nc.Block())

    def emit(ename):
        def body(eng):
            for waits, fn, inc in sch.ops[ename]:
                for k, v in waits:
                    eng.wait_ge(sems[k], v)
                if fn is None:
                    continue
                ins = fn(eng)
                ins.then_inc(sems[inc[0]], inc[1])
        return body

    block.tensor(emit("pe"))
    block.scalar(emit("act"))
    block.vector(emit("dve"))
    block.gpsimd(emit("pool"))
    block.sync(emit("sp"))
    try:
        print("sbuf bytes remaining/partition:", nc.sbuf_bytes_remaining)
    except Exception:
        pass
    es.close()
    return nc


def kernel(**inputs):
    ins = {k: np.asarray(v) for k, v in inputs.items()}
    nc = build_program(DEPTH, SEQ // T, True)
    maps = prep_inputs(ins, DEPTH, SEQ // T, list(range(NCORES)))
    res = run_bass_kernel_spmd(nc, maps, core_ids=list(range(NCORES)))
    out = np.empty((NCORES, SEQ, D), np.float32)
    for b in range(NCORES):
        out[b] = res.results[b]["outT"].T
    return out
```

```python
import contextlib
import numpy as np
import concourse.bass as bass
import concourse.mybir as mybir
from concourse.bass_utils import run_bass_kernel_spmd

F32 = mybir.dt.float32
BF16 = mybir.dt.bfloat16
AF = mybir.ActivationFunctionType
ALU = mybir.AluOpType

D = 1024
NCH = 8
T = 512
DFF = 2816
NHC = 22
IN_COLS = 6656
N_ADA = 9
EPS = 1e-6
SLOTW = 4096
RING = 5
HALO = 32
LSP = 252
SEQ = 4096
DEPTH = 4
NCORES = 8
USE_ARSQRT = False


def _kc_tile(w, cols):
    K = w.shape[0]
    sub = w[:, cols]
    sub = sub.reshape(K // 128, 128, len(cols))
    return np.ascontiguousarray(sub.transpose(1, 0, 2)).reshape(128, -1)


def _layer_slots(l, ins):
    slots = []

    def ffn(w13, w2):
        for j2 in range(NHC // 2):
            cols = np.concatenate([
                np.arange((2 * j2) * 128, (2 * j2 + 2) * 128),
                DFF + np.arange((2 * j2) * 128, (2 * j2 + 2) * 128)])
            slots.append(_kc_tile(w13, cols))
        blk = w2.reshape(NHC, 128, NCH, 128).transpose(1, 2, 0, 3)
        blk = np.ascontiguousarray(blk).reshape(128, NCH * NHC * 128)
        for s in range(0, NCH * NHC * 128, SLOTW):
            slots.append(np.ascontiguousarray(blk[:, s:s + SLOTW]))

    ffn(ins["ffn1_w13"][l], ins["ffn1_w2"][l])
    w_in = ins["w_in"][l]
    for base in (3072, 2560, 512, 0, 1536, 2048, 1024):
        slots.append(_kc_tile(w_in, np.arange(base, base + 512)))
    wb = [ins["w_branch_a"][l], ins["w_branch_b"][l], ins["w_branch_c"][l]]
    for dc in range(NCH):
        g = [_kc_tile(w_in, 3584 + i * 1024 + dc * 128 + np.arange(128)) for i in range(3)]
        slots.append(np.concatenate(g, axis=1))
    for dc in range(NCH):
        b = [_kc_tile(wb[i], dc * 128 + np.arange(128)) for i in range(3)]
        slots.append(np.concatenate(b, axis=1))
    w_o = ins["w_o"][l]
    for s in range(2):
        slots.append(_kc_tile(w_o, np.arange(s * 512, (s + 1) * 512)))
    ffn(ins["ffn2_w13"][l], ins["ffn2_w2"][l])
    return slots


def _slot_widths():
    ffn = [SLOTW] * 11 + [SLOTW] * 5 + [NCH * NHC * 128 - 5 * SLOTW]
    mix = [SLOTW] * 7 + [3072] * NCH + [1536] * NCH + [SLOTW] * 2
    return ffn + mix + ffn


def _fm(v, n):
    return np.ascontiguousarray(v.reshape(n, 128).T)


def prep_inputs(ins, n_layers, n_tiles, batch_ids):
    widths = _slot_widths()
    wst = np.empty((n_layers, 128 * sum(widths)), np.float32)
    for l in range(n_layers):
        off = 0
        for s, w in zip(_layer_slots(l, ins), widths):
            assert s.shape == (128, w), (s.shape, w)
            wst[l, off:off + 128 * w] = s.reshape(-1)
            off += 128 * w
    wada = np.empty((n_layers, 18, 128, SLOTW), np.float32)
    for l in range(n_layers):
        for s in range(18):
            wada[l, s] = _kc_tile(ins["w_ada"][l], np.arange(s * 512, (s + 1) * 512))
    nsp = n_layers * LSP + 16
    spw = np.empty((n_layers, 128, 1024), np.float32)
    sp_common = np.zeros((128, nsp), np.float32)
    for l in range(n_layers):
        b = l * LSP
        sp_common[:, b + 0:b + 8] = _fm(ins["g_ffn1"][l], 8)
        sp_common[:, b + 8:b + 16] = _fm(ins["g_mix"][l], 8)
        sp_common[:, b + 16:b + 24] = _fm(ins["g_ffn2"][l], 8)
        sp_common[:, b + 24:b + 96] = _fm(ins["b_ada"][l], 72)
        sp_common[:, b + 96:b + 100] = _fm(ins["a_ln_g"][l], 4)
        sp_common[:, b + 100:b + 104] = _fm(ins["a_ln_b"][l], 4)
        bc = ins["b_conv"][l].reshape(3, 4, 128).transpose(2, 1, 0).reshape(128, 12)
        cc = ins["c_conv"][l].reshape(31, 4, 128).transpose(2, 1, 0).reshape(128, 124)
        sp_common[:, b + 104:b + 116] = bc
        sp_common[:, b + 116:b + 240] = cc
        sp_common[:, b + 240:b + 244] = _fm(ins["c_conv_b"][l], 4)
        sp_common[:, b + 244:b + 248] = _fm(ins["c_ln_g"][l], 4)
        sp_common[:, b + 248:b + 252] = _fm(ins["c_ln_b"][l], 4)
        spw[l, :, 0:512] = ins["a_ws"][l].transpose(2, 0, 1).reshape(128, 512)
        spw[l, :, 512:1024] = np.broadcast_to(ins["a_bs"][l].reshape(1, 512), (128, 512))
    gb = n_layers * LSP
    sp_common[:, gb:gb + 8] = _fm(ins["g_final"], 8)
    maps = []
    for bi in batch_ids:
        sp = sp_common.copy()
        sp[:, gb + 8:gb + 16] = _fm(ins["c"][bi], 8)
        xT = np.ascontiguousarray(ins["x"][bi, :n_tiles * T, :].T)
        maps.append({"xT": xT, "wst": wst, "wada": wada, "sp": sp, "spw": spw})
    return maps


class Res:
    __slots__ = ("w", "r", "name")

    def __init__(self, name=""):
        self.w = None
        self.r = {}
        self.name = name


class Sched:
    ENGS = ("pe", "act", "dve", "pool", "sp")

    def __init__(self):
        self.ops = {e: [] for e in self.ENGS}
        self.cnt = {e: 0 for e in self.ENGS}
        self.seen = {e: {} for e in self.ENGS}
        self.clock = {e: [None] for e in self.ENGS}
        self.dma_cnt = {}

    def _deps(self, e, reads, writes):
        raw, other = {}, {}

        def add(dst, kv):
            k, v = kv
            if dst.get(k, 0) < v:
                dst[k] = v
        for r in reads:
            if r.w is not None:
                add(raw, r.w)
        for r in writes:
            if r.w is not None:
                add(other, r.w)
            for kv in r.r.items():
                add(other, kv)
        waits = []
        seen = self.seen[e]
        changed = False
        for src, is_raw in ((raw, True), (other, False)):
            for k, v in src.items():
                if k == e:
                    if e in ("pe", "sp") or not is_raw:
                        continue
                if seen.get(k, 0) >= v:
                    continue
                if not changed:
                    seen = dict(seen)
                    changed = True
                seen[k] = v
                waits.append((k, v))
                if k in self.clock:
                    snap = self.clock[k][v]
                    for k2, v2 in snap.items():
                        if seen.get(k2, 0) < v2:
                            seen[k2] = v2
        if changed:
            self.seen[e] = seen
        return waits

    def op(self, e, fn, reads=(), writes=()):
        waits = self._deps(e, reads, writes)
        self.cnt[e] += 1
        n = self.cnt[e]
        self.clock[e].append(self.seen[e])
        self.ops[e].append((waits, fn, (e, 1)))
        for r in writes:
            r.w = (e, n)
            r.r = {}
        for r in reads:
            if r.r.get(e, 0) < n:
                r.r[e] = n

    def dma(self, q, fn, semname, reads=(), writes=()):
        waits = self._deps(q, reads, writes)
        prev = self.dma_cnt.get(semname, 0)
        if prev and self.seen[q].get(semname, 0) < prev:
            s = dict(self.seen[q])
            s[semname] = prev
            self.seen[q] = s
            waits.append((semname, prev))
        val = prev + 16
        self.dma_cnt[semname] = val
        self.ops[q].append((waits, fn, (semname, 16)))
        for r in writes:
            r.w = (semname, val)
            r.r = {}
        for r in reads:
            r.r[semname] = val

    def final_wait(self, e, semname):
        v = self.dma_cnt.get(semname, 0)
        if v and self.seen[e].get(semname, 0) < v:
            self.ops[e].append(([(semname, v)], None, None))


def build_program(n_layers, n_tiles, final_norm=True, stream=None):
    nc = bass.Bass("TRN2", target_bir_lowering=False)
    S = n_tiles * T
    widths = _slot_widths()
    offs = np.concatenate([[0], np.cumsum([128 * w for w in widths])]).astype(np.int64)
    nsp = n_layers * LSP + 16
    xT = nc.dram_tensor("xT", [D, S], F32, kind="ExternalInput").ap()
    wst = nc.dram_tensor("wst", [n_layers, int(offs[-1])], F32, kind="ExternalInput").ap()
    wada = nc.dram_tensor("wada", [n_layers, 18, 128, SLOTW], F32, kind="ExternalInput").ap()
    spd = nc.dram_tensor("sp", [128, nsp], F32, kind="ExternalInput").ap()
    spwd = nc.dram_tensor("spw", [n_layers, 128, 1024], F32, kind="ExternalInput").ap()
    outT = nc.dram_tensor("outT", [D, S], F32, kind="ExternalOutput").ap()

    sch = Sched()
    es = contextlib.ExitStack()

    def sb(name, shape, dt):
        return es.enter_context(nc.sbuf_tensor(name, shape, dt))

    X = sb("X", [128, NCH, T], F32)
    OST = sb("OST", [128, 2, T], F32)
    SQ = sb("SQ", [128, NCH, T], BF16)
    RSTD = sb("RSTD", [128, T], F32)
    H = sb("H", [128, NCH, T], BF16)
    G = sb("G", [128, NHC, T], BF16)
    NTMP = 6
    TMP = sb("TMP", [128, NTMP, T], F32)
    SA = sb("SA", [128, 4, T], F32)
    SB_ = sb("SBb", [128, 4, T], F32)
    SC = sb("SC", [128, 4, HALO + T], F32)
    SD = sb("SD", [128, 4, T], F32)
    VN = sb("VN", [128, 4, 512], BF16)
    ST6 = sb("ST6", [128, 4, 8], F32)
    MV = sb("MV", [128, 4, 2], F32)
    STAT = sb("STAT", [128, 2, T], F32)
    GS = sb("GS", [128, 24, T], BF16)
    SCB = sb("SCB", [128, 4, T + 2], F32)
    CW = sb("CW", [128, n_layers, 124], F32)
    RINGB = sb("RINGB", [128, RING, SLOTW], BF16)
    SP = sb("SP", [128, nsp], F32)
    SCR = TMP[:, 0:2, :].rearrange("p a b -> p (a b)")
    DER = sb("DER", [128, n_layers, 72], F32)
    GF = sb("GF", [128, 8], F32)
    WMT = sb("WMT", [128, n_layers, 4, 128], BF16)
    BM = sb("BM", [128, n_layers, 4, 128], F32)
    ONES = sb("ONES", [128, 128], BF16)
    CACT = sb("CACT", [128, 8], BF16)
    CST = sb("CST", [128, 2], F32)
    DUM = sb("DUM", [128, 2], F32)
    TAILB = sb("TAILB", [128, n_layers, 4, 2], F32)
    TAILC = sb("TAILC", [128, n_layers, 4, 30], F32)
    PS = [es.enter_context(nc.psum_tensor(f"ps{i}", [128, 512], F32)) for i in range(8)]

    rX = [Res(f"X{c}") for c in range(NCH)]
    rOST = [Res() for _ in range(2)]
    rSQ = [Res() for _ in range(NCH)]
    rRSTD = Res()
    rH = [Res() for _ in range(NCH)]
    rG = [Res() for _ in range(NHC)]
    rTMP = [Res() for _ in range(NTMP)]
    rSA = [Res() for _ in range(4)]
    rSB = [Res() for _ in range(4)]
    rSC = [Res() for _ in range(4)]
    rSD = [Res() for _ in range(4)]
    rVN = [Res() for _ in range(4)]
    rST = [Res() for _ in range(4)]
    rSTAT = [Res() for _ in range(2)]
    rGS = [Res() for _ in range(24)]
    rSCB = [Res() for _ in range(4)]
    rCW = [Res() for _ in range(n_layers)]
    rRING = [Res() for _ in range(RING)]
    rSP = Res()
    rDER = [Res() for _ in range(n_layers)]
    rGF = Res()
    rWMT = [Res() for _ in range(n_layers)]
    rBM = [Res() for _ in range(n_layers)]
    rONES = Res()
    rCACT = Res()
    rCST = Res()
    rDUM = Res()
    rTB = [Res() for _ in range(n_layers)]
    rTC = [Res() for _ in range(n_layers)]
    rPS = [Res() for _ in range(8)]

    state = {"bank": 0, "tmp": 0, "slot": 0}

    def bank():
        b = state["bank"]
        while (state.get("reserve7") and b == 7) or b in state.get("hold", ()):
            b = (b + 1) % 8
        state["bank"] = (b + 1) % 8
        return b

    def tmp():
        t = state["tmp"]
        state["tmp"] = (t + 1) % NTMP
        return t

    recorded = []

    def src_of(key):
        if key[0] == "ada":
            return wada[key[1], key[2]], SLOTW
        _, l, k = key
        return wst[l, int(offs[k]):int(offs[k + 1])].rearrange("(p w) -> p w", p=128), widths[k]

    def emit_slot_dma(n, key):
        src, width = src_of(key)
        i = n % RING
        dst = RINGB[:, i, 0:width]
        sch.dma("pool", lambda e, dst=dst, src=src: e.dma_start(out=dst, in_=src), f"ring{i}",
                writes=[rRING[i]])

    def load_slot(key):
        n = state["slot"]
        state["slot"] = n + 1
        recorded.append(key)
        if stream is None:
            emit_slot_dma(n, key)
        else:
            assert stream[n] == key, (n, stream[n], key)
            hi = min(n + RING - 2, len(stream) - 1)
            while state.get("issued", -1) < hi:
                state["issued"] = state.get("issued", -1) + 1
                emit_slot_dma(state["issued"], stream[state["issued"]])
        return n % RING

    def wslot(l, k):
        if state.get("ada_q"):
            state["ada_ctr"] = state.get("ada_ctr", 0) + 1
            if state["ada_ctr"] % 3 == 0:
                state["ada_q"].pop(0)()
        return load_slot(("w", l, k))

    sch.dma("sp", lambda e: e.dma_start(out=SP[:], in_=spd), "spld", writes=[rSP])
    sch.op("dve", lambda e: e.memset(ONES[:], 1.0), writes=[rONES])
    sch.op("dve", lambda e: e.memset(CST[:, 0:1], float(EPS * D)), writes=[rCST])
    sch.op("dve", lambda e: e.memset(CST[:, 1:2], float(EPS)), writes=[rCST])
    sch.op("dve", lambda e: e.memset(TAILB[:], 0.0), writes=rTB)
    sch.op("dve", lambda e: e.memset(TAILC[:], 0.0), writes=rTC)
    gb = n_layers * LSP
    sch.op("act", lambda e: e.activation(out=CACT[:], in_=SP[:, gb + 8:gb + 16], func=AF.Silu),
           reads=[rSP], writes=[rCACT])
    sch.op("dve", lambda e: e.tensor_scalar(out=GF[:], in0=SP[:, gb:gb + 8], scalar1=float(np.sqrt(D)),
                                            scalar2=None, op0=ALU.mult), reads=[rSP], writes=[rGF])
    for l in range(n_layers):
        b0 = l * LSP
        sch.op("dve", lambda e, l=l, b0=b0: e.tensor_scalar(out=CW[:, l, :], in0=SP[:, b0 + 116:b0 + 240], scalar1=0.5,
                                                         scalar2=None, op0=ALU.mult), reads=[rSP], writes=[rCW[l]])
        sch.dma("sp", lambda e, l=l: e.dma_start(out=SCR[:], in_=spwd[l]), "scrld", writes=[rTMP[0], rTMP[1]])
        sch.op("dve", lambda e, l=l: e.tensor_copy(out=WMT[:, l].rearrange("p g q -> p (g q)"), in_=SCR[:, 0:512]),
               reads=[rTMP[0], rTMP[1]], writes=[rWMT[l]])
        sch.op("dve", lambda e, l=l: e.memset(WMT[64:128, l, :, 0:64], 0.0), writes=[rWMT[l]])
        bk = bank()
        sch.op("pe", lambda e, l=l, bk=bk: e.matmul(PS[bk][:], ONES[:], WMT[:, l].rearrange("p g q -> p (g q)"),
                                                   start=True, stop=True),
               reads=[rONES, rWMT[l]], writes=[rPS[bk]])
        for g in range(4):
            sch.op("dve", lambda e, l=l, g=g, bk=bk, b0=b0: e.scalar_tensor_tensor(
                out=BM[:, l, g, :], in0=PS[bk][:, g * 128:(g + 1) * 128], scalar=SP[:, b0 + 100 + g:b0 + 101 + g],
                in1=SCR[:, 512 + g * 128:512 + (g + 1) * 128], op0=ALU.mult, op1=ALU.add),
                reads=[rPS[bk], rSP, rTMP[0], rTMP[1]], writes=[rBM[l]])

    def ada_begin(l):
        q = state.setdefault("ada_q", [])
        for s in range(18):
            q.append(lambda s=s: ada_slot(l, s))
        q.append(lambda: ada_finish(l))

    def ada_flush():
        q = state.get("ada_q", [])
        while q:
            q.pop(0)()

    def ada_slot(l, s):
        bk = 7
        if True:
            i = load_slot(("ada", l, s))

            def f(e, i=i, s=s, bk=bk):
                ins = None
                for cc in range(4):
                    col = s * 4 + cc
                    for kc in range(NCH):
                        ins = e.matmul(PS[bk][:, col:col + 1],
                                       RINGB[:, i, kc * 512 + cc * 128:kc * 512 + (cc + 1) * 128],
                                       CACT[:, kc:kc + 1], start=(kc == 0), stop=(kc == NCH - 1))
                return ins
            sch.op("pe", f, reads=[rRING[i], rCACT], writes=[rPS[bk]])

    def ada_finish(l):
        b0 = l * LSP
        bk = 7
        dl = DER[:, l, :]
        sch.op("dve", lambda e: e.tensor_tensor(out=dl, in0=PS[bk][:, 0:72], in1=SP[:, b0 + 24:b0 + 96], op=ALU.add),
               reads=[rPS[bk], rSP], writes=[rDER[l]])
        for j, gcol in ((1, 0), (4, 8), (7, 16)):
            sch.op("dve", lambda e, j=j, gcol=gcol: e.scalar_tensor_tensor(
                out=DER[:, l, j * 8:j * 8 + 8], in0=DER[:, l, j * 8:j * 8 + 8], scalar=1.0,
                in1=SP[:, b0 + gcol:b0 + gcol + 8], op0=ALU.add, op1=ALU.mult),
                reads=[rDER[l], rSP], writes=[rDER[l]])
            sch.op("dve", lambda e, j=j: e.tensor_scalar(
                out=DER[:, l, j * 8:j * 8 + 8], in0=DER[:, l, j * 8:j * 8 + 8], scalar1=float(np.sqrt(D)),
                scalar2=None, op0=ALU.mult), reads=[rDER[l]], writes=[rDER[l]])
        for j in (2, 5, 8):
            sch.op("dve", lambda e, j=j: e.tensor_scalar(
                out=DER[:, l, j * 8:j * 8 + 8], in0=DER[:, l, j * 8:j * 8 + 8], scalar1=0.5,
                scalar2=None, op0=ALU.mult), reads=[rDER[l]], writes=[rDER[l]])

    def rms_stats():
        sch.op("act", lambda e: e.activation(out=DUM[:, 0:1], in_=CST[:, 0:1],
                                             func=(AF.Abs_reciprocal_sqrt if USE_ARSQRT else AF.Sqrt)), reads=[rCST], writes=[rDUM])
        for c in range(NCH):
            sch.op("act", lambda e, c=c: e.activation(out=SQ[:, c, :], in_=X[:, c, :], func=AF.Square),
                   reads=[rX[c]], writes=[rSQ[c]])
        bk = bank()

        for c in range(NCH):
            sch.op("pe", lambda e, c=c: e.matmul(PS[bk][:], ONES[:], SQ[:, c, :], start=(c == 0), stop=(c == NCH - 1)),
                   reads=[rSQ[c], rONES], writes=[rPS[bk]])
        if USE_ARSQRT:
            sch.op("act", lambda e: e.activation(out=RSTD[:], in_=PS[bk][:], func=AF.Abs_reciprocal_sqrt, bias=CST[:, 0:1], scale=1.0),
                   reads=[rPS[bk], rCST], writes=[rRSTD])
        else:
            sch.op("act", lambda e: e.activation(out=RSTD[:], in_=PS[bk][:], func=AF.Sqrt, bias=CST[:, 0:1], scale=1.0),
                   reads=[rPS[bk], rCST], writes=[rRSTD])
            sch.op("dve", lambda e: e.reciprocal(out=RSTD[:], in_=RSTD[:]), reads=[rRSTD], writes=[rRSTD])

    def norm(l, jshift, jscale):
        rms_stats()
        for c in range(NCH):
            t = tmp()
            sch.op("dve", lambda e, c=c, t=t: e.tensor_tensor(out=TMP[:, t, :], in0=X[:, c, :], in1=RSTD[:], op=ALU.mult),
                   reads=[rX[c], rRSTD], writes=[rTMP[t]])
            sch.op("act", lambda e, c=c, t=t: e.activation(
                out=H[:, c, :], in_=TMP[:, t, :], func=AF.Identity,
                bias=DER[:, l, jshift * 8 + c:jshift * 8 + c + 1], scale=DER[:, l, jscale * 8 + c:jscale * 8 + c + 1]),
                reads=[rTMP[t], rDER[l]], writes=[rH[c]])

    def mm_group(out_ap, pairs, reads, bk):
        def f(e):
            ins = None
            n = len(pairs)
            for k, (lhsT, rhs) in enumerate(pairs):
                ins = e.matmul(out_ap, lhsT, rhs, start=(k == 0), stop=(k == n - 1))
            return ins
        sch.op("pe", f, reads=reads, writes=[rPS[bk]])

    def mm_groups_splitk(groups, ring_res):
        banks = [rPS[bk] for (_, bk, _) in groups]

        def half(lo, hi):
            def f(e):
                ins = None
                for out_ap, bk, pairs in groups:
                    for kc in range(lo, hi):
                        ins = e.matmul(out_ap, pairs[kc][0], pairs[kc][1], start=(kc == 0), stop=(kc == NCH - 1))
                return ins
            return f
        sch.op("pe", half(0, 4), reads=[ring_res] + rH[0:4], writes=banks)
        sch.op("pe", half(4, 8), reads=[ring_res] + rH[4:8], writes=banks)

    def ffn_stage(l, k0, jshift, jscale, jgate):
        norm(l, jshift, jscale)
        pre = {}
        for j in range(NHC):
            if j % 2 == 0:
                i = wslot(l, k0 + j // 2)
            jj = j % 2
            if j == 0:
                grp = []
                for j2 in (0, 1):
                    pre[j2] = (bank(), bank())
                    for q, bk in ((j2, pre[j2][0]), (2 + j2, pre[j2][1])):
                        grp.append((PS[bk][:], bk, [(RINGB[:, i, kc * 512 + q * 128:kc * 512 + (q + 1) * 128], H[:, kc, :])
                                                    for kc in range(NCH)]))
                mm_groups_splitk(grp, rRING[i])
            if j in pre:
                ba, bb = pre[j]
            else:
                ba, bb = bank(), bank()
                for q, bk in ((jj, ba), (2 + jj, bb)):
                    mm_group(PS[bk][:], [(RINGB[:, i, kc * 512 + q * 128:kc * 512 + (q + 1) * 128], H[:, kc, :])
                                          for kc in range(NCH)], [rRING[i]] + rH, bk)
            t = tmp()
            sch.op("act", lambda e, t=t, ba=ba: e.activation(out=TMP[:, t, :], in_=PS[ba][:], func=AF.Silu),
                   reads=[rPS[ba]], writes=[rTMP[t]])
            sch.op("dve", lambda e, t=t, bb=bb, j=j: e.tensor_tensor(out=G[:, j, :], in0=TMP[:, t, :], in1=PS[bb][:], op=ALU.mult),
                   reads=[rTMP[t], rPS[bb]], writes=[rG[j]])
        cur = {}
        for dc in range(NCH):
            pairs, reads = [], list(rG)
            for hc in range(NHC):
                bi = dc * NHC + hc
                s = bi // 32
                if s not in cur:
                    cur.clear()
                    cur[s] = wslot(l, k0 + 11 + s)
                i = cur[s]
                if rRING[i] not in reads:
                    reads.append(rRING[i])
                pairs.append((i, (bi % 32) * 128, hc))
            bk = bank()
            mm_group(PS[bk][:], [(RINGB[:, i, o:o + 128], G[:, hc, :]) for (i, o, hc) in pairs], reads, bk)
            sch.op("dve", lambda e, dc=dc, bk=bk: e.scalar_tensor_tensor(
                out=X[:, dc, :], in0=PS[bk][:], scalar=DER[:, l, jgate * 8 + dc:jgate * 8 + dc + 1], in1=X[:, dc, :],
                op0=ALU.mult, op1=ALU.add), reads=[rPS[bk], rDER[l], rX[dc]], writes=[rX[dc]])

    def mixer_stage(l, k0):
        b0 = l * LSP
        norm(l, 3, 4)
        pend = []

        ppend = []

        def drip(n):
            for _ in range(min(n, len(pend))):
                pend.pop(0)()
            for _ in range(min(max(1, (2 * n + 2) // 3), len(ppend))):
                ppend.pop(0)()

        def std_groups(i, banks=None, split=False):
            if split:
                bks = [bank() for _ in range(4)]
                mm_groups_splitk([(PS[bks[m]][:], bks[m],
                                   [(RINGB[:, i, kc * 512 + m * 128:kc * 512 + (m + 1) * 128], H[:, kc, :]) for kc in range(NCH)])
                                  for m in range(4)], rRING[i])
                for m in range(4):
                    yield m, bks[m]
                return
            for m in range(4):
                bk = banks[m] if banks is not None else bank()
                mm_group(PS[bk][:], [(RINGB[:, i, kc * 512 + m * 128:kc * 512 + (m + 1) * 128], H[:, kc, :])
                                      for kc in range(NCH)], [rRING[i]] + rH, bk)
                yield m, bk

        i = wslot(l, k0 + 0)
        for m, bk in std_groups(i, split=True):
            sch.op("act", lambda e, m=m, bk=bk: e.activation(out=SB_[:, m, :], in_=PS[bk][:], func=AF.Tanh, scale=0.5),
                   reads=[rPS[bk]], writes=[rSB[m]])
        i = wslot(l, k0 + 1)
        for m, bk in std_groups(i):
            sch.op("act", lambda e, m=m: e.activation(out=SC[:, m, HALO - 30:HALO], in_=TAILC[:, l, m, :], func=AF.Identity),
                   reads=[rTC[l]], writes=[rSC[m]])
            sch.op("dve", lambda e, m=m, bk=bk: e.scalar_tensor_tensor(
                out=SC[:, m, HALO:HALO + T], in0=SB_[:, m, :], scalar=1.0, in1=PS[bk][:], op0=ALU.add, op1=ALU.mult),
                reads=[rSB[m], rPS[bk]], writes=[rSC[m]])
            sch.op("act", lambda e, m=m: e.activation(out=TAILC[:, l, m, :], in_=SC[:, m, HALO + T - 30:HALO + T], func=AF.Identity),
                   reads=[rSC[m]], writes=[rTC[l]])
        for k in range(31):
            for m in range(4):
                src = SC[:, m, HALO - 30 + k:HALO - 30 + k + T]
                wv = CW[:, l, m * 31 + k:m * 31 + k + 1]
                if k == 0:
                    sch.op("act", lambda e, m=m, wv=wv, src=src: e.activation(
                        out=SD[:, m, :], in_=src, func=AF.Identity, scale=wv, bias=SP[:, b0 + 240 + m:b0 + 241 + m]),
                        reads=[rSC[m], rCW[l], rSP], writes=[rSD[m]])
                else:
                    pend.append(lambda m=m, wv=wv, src=src: sch.op("dve", lambda e: e.scalar_tensor_tensor(
                        out=SD[:, m, :], in0=src, scalar=wv, in1=SD[:, m, :],
                        op0=ALU.mult, op1=ALU.add), reads=[rSC[m], rCW[l], rSD[m]], writes=[rSD[m]]))
        NDRIP = 4
        i = wslot(l, k0 + 2)
        mb = [bank() for _ in range(4)]
        state["hold"] = set(mb)
        sgu_pending = []
        vt = []
        for tb in range(4):
            bk = bank()
            mm_group(PS[bk][:], [(H[:, kc, tb * 128:(tb + 1) * 128], RINGB[:, i, kc * 512:(kc + 1) * 512])
                                  for kc in range(NCH)], [rRING[i]] + rH, bk)
            t = tmp()
            vt.append(t)
            v = tb
            sch.op("act", lambda e, t=t, bk=bk: e.activation(out=TMP[:, t, :], in_=PS[bk][:], func=AF.Gelu),
                   reads=[rPS[bk]], writes=[rTMP[t]])
            sch.op("dve", lambda e, t=t, v=v: e.bn_stats(out=ST6[:, v, 0:6], in_=TMP[:, t, :]),
                   reads=[rTMP[t]], writes=[rST[v]])
            sch.op("dve", lambda e, v=v: e.bn_aggr(out=MV[:, v, :], in_=ST6[:, v, 0:6]),
                   reads=[rST[v]], writes=[rST[v]])
        for tb in range(4):
            t = vt[tb]
            v = tb
            sch.op("act", lambda e, v=v: e.activation(out=MV[:, v, 1:2], in_=MV[:, v, 1:2], func=AF.Sqrt,
                                                     bias=CST[:, 1:2], scale=1.0),
                   reads=[rST[v], rCST], writes=[rST[v]])
            sch.op("dve", lambda e, v=v: e.reciprocal(out=MV[:, v, 1:2], in_=MV[:, v, 1:2]),
                   reads=[rST[v]], writes=[rST[v]])
            sch.op("dve", lambda e, t=t, v=v: e.tensor_scalar(out=VN[:, v, :], in0=TMP[:, t, :], scalar1=MV[:, v, 0:1],
                                                             scalar2=MV[:, v, 1:2], op0=ALU.subtract, op1=ALU.mult),
                   reads=[rTMP[t], rST[v]], writes=[rVN[v]])

            def sgu(v=v, tb=tb):
                def f(e):
                    ins = None
                    for g in range(4):
                        ins = e.matmul(PS[mb[g]][:, tb * 128:(tb + 1) * 128], VN[:, v, g * 128:(g + 1) * 128],
                                       WMT[:, l, g, :], start=True, stop=True)
                    return ins
                sch.op("pe", f, reads=[rVN[v], rWMT[l]], writes=[rPS[b] for b in mb])
            sgu_pending.append(sgu)
        i = wslot(l, k0 + 3)
        for m, bk in std_groups(i):
            if m >= 1:
                sgu_pending.pop(0)()
            if m == 3:
                sgu_pending.pop(0)()
            sch.op("act", lambda e, m=m, bk=bk: e.activation(out=SA[:, m, :], in_=PS[bk][:], func=AF.Gelu),
                   reads=[rPS[bk]], writes=[rSA[m]])
            drip(NDRIP)
        for g in range(4):
            t = tmp()
            sch.op("dve", lambda e, g=g, t=t: e.scalar_tensor_tensor(
                out=TMP[:, t, :].rearrange("p (a b) -> p a b", a=4),
                in0=PS[mb[g]][:].rearrange("p (a b) -> p a b", a=4),
                scalar=SP[:, b0 + 96 + g:b0 + 97 + g],
                in1=BM[:, l, g, :].unsqueeze(1).broadcast_to([128, 4, 128]),
                op0=ALU.mult, op1=ALU.add), reads=[rPS[mb[g]], rSP, rBM[l]], writes=[rTMP[t]])
            sch.op("dve", lambda e, g=g, t=t: e.tensor_tensor(out=G[:, g, :], in0=SA[:, g, :], in1=TMP[:, t, :], op=ALU.mult),
                   reads=[rSA[g], rTMP[t]], writes=[rG[g]])
            drip(2)
        state["hold"] = set()
        i = wslot(l, k0 + 4)
        for m, bk in std_groups(i):
            sch.op("act", lambda e, m=m, bk=bk: e.activation(out=SB_[:, m, :], in_=PS[bk][:], func=AF.Identity),
                   reads=[rPS[bk]], writes=[rSB[m]])
            drip(NDRIP)
        i = wslot(l, k0 + 5)
        for m, bk in std_groups(i):
            sch.op("act", lambda e, m=m: e.activation(out=SCB[:, m, 0:2], in_=TAILB[:, l, m, :], func=AF.Identity),
                   reads=[rTB[l]], writes=[rSCB[m]])
            sch.op("dve", lambda e, m=m, bk=bk: e.tensor_tensor(out=SCB[:, m, 2:2 + T], in0=SB_[:, m, :], in1=PS[bk][:], op=ALU.mult),
                   reads=[rSB[m], rPS[bk]], writes=[rSCB[m]])
            sch.op("act", lambda e, m=m: e.activation(out=TAILB[:, l, m, :], in_=SCB[:, m, T:T + 2], func=AF.Identity),
                   reads=[rSCB[m]], writes=[rTB[l]])
            wb = b0 + 104 + m * 3
            sch.op("act", lambda e, m=m, wb=wb: e.activation(out=SB_[:, m, :], in_=SCB[:, m, 0:T], func=AF.Identity,
                                                            scale=SP[:, wb:wb + 1]),
                   reads=[rSCB[m], rSP], writes=[rSB[m]])
            for k in (1, 2):
                sch.op("dve", lambda e, m=m, wb=wb, k=k: e.scalar_tensor_tensor(
                    out=SB_[:, m, :], in0=SCB[:, m, k:k + T], scalar=SP[:, wb + k:wb + k + 1],
                    in1=SB_[:, m, :], op0=ALU.mult, op1=ALU.add), reads=[rSCB[m], rSP, rSB[m]], writes=[rSB[m]])
            drip(NDRIP)
        i = wslot(l, k0 + 6)
        for m, bk in std_groups(i):
            sch.op("dve", lambda e, m=m, bk=bk: e.tensor_tensor(out=G[:, 4 + m, :], in0=SB_[:, m, :], in1=PS[bk][:], op=ALU.mult),
                   reads=[rSB[m], rPS[bk]], writes=[rG[4 + m]])
            drip(NDRIP)

        def gate_slot(dc):
            ig = wslot(l, k0 + 7 + dc)
            for i3 in range(3):
                bk = bank()
                mm_group(PS[bk][:], [(RINGB[:, ig, (i3 * 8 + kc) * 128:(i3 * 8 + kc + 1) * 128], H[:, kc, :])
                                      for kc in range(NCH)], [rRING[ig]] + rH, bk)
                sch.op("act", lambda e, i3=i3, bk=bk, dc=dc: e.activation(out=GS[:, dc * 3 + i3, :], in_=PS[bk][:],
                                                                       func=AF.Tanh, scale=0.5),
                       reads=[rPS[bk]], writes=[rGS[dc * 3 + i3]])
                drip(NDRIP + 2)
        for dc in range(5):
            gate_slot(dc)
        drip(len(pend))
        while ppend:
            ppend.pop(0)()
        for m in range(4):
            sch.op("act", lambda e, m=m: e.activation(out=G[:, 12 + m, :], in_=SD[:, m, :], func=AF.Identity),
                   reads=[rSD[m]], writes=[rG[12 + m]])
            sch.op("act", lambda e, m=m: e.activation(out=G[:, 16 + m, :], in_=SD[:, m, :], func=AF.Square),
                   reads=[rSD[m]], writes=[rG[16 + m]])
        gate_slot(5)
        b1, b2 = bank(), bank()
        mm_group(PS[b1][:], [(ONES[:], G[:, 12 + m, :]) for m in range(4)], [rONES] + rG[12:16], b1)
        mm_group(PS[b2][:], [(ONES[:], G[:, 16 + m, :]) for m in range(4)], [rONES] + rG[16:20], b2)
        sch.op("dve", lambda e: e.tensor_scalar(out=STAT[:, 0, :], in0=PS[b1][:], scalar1=1.0 / 512, scalar2=None, op0=ALU.mult),
               reads=[rPS[b1]], writes=[rSTAT[0]])
        sch.op("dve", lambda e: e.tensor_tensor(out=STAT[:, 1, :], in0=STAT[:, 0, :], in1=STAT[:, 0, :], op=ALU.mult),
               reads=[rSTAT[0]], writes=[rSTAT[1]])
        sch.op("dve", lambda e: e.scalar_tensor_tensor(out=STAT[:, 1, :], in0=PS[b2][:], scalar=1.0 / 512, in1=STAT[:, 1, :],
                                                       op0=ALU.mult, op1=ALU.subtract),
               reads=[rPS[b2], rSTAT[1]], writes=[rSTAT[1]])
        sch.op("act", lambda e: e.activation(out=STAT[:, 1, :], in_=STAT[:, 1, :], func=AF.Sqrt, bias=CST[:, 1:2], scale=1.0),
               reads=[rSTAT[1], rCST], writes=[rSTAT[1]])
        sch.op("dve", lambda e: e.reciprocal(out=STAT[:, 1, :], in_=STAT[:, 1, :]), reads=[rSTAT[1]], writes=[rSTAT[1]])
        gate_slot(6)
        for m in range(4):
            t = tmp()
            sch.op("dve", lambda e, m=m, t=t: e.tensor_tensor(out=TMP[:, t, :], in0=SD[:, m, :], in1=STAT[:, 0, :], op=ALU.subtract),
                   reads=[rSD[m], rSTAT[0]], writes=[rTMP[t]])
            sch.op("dve", lambda e, t=t: e.tensor_tensor(out=TMP[:, t, :], in0=TMP[:, t, :], in1=STAT[:, 1, :], op=ALU.mult),
                   reads=[rTMP[t], rSTAT[1]], writes=[rTMP[t]])
            sch.op("act", lambda e, m=m, t=t: e.activation(
                out=G[:, 8 + m, :], in_=TMP[:, t, :], func=AF.Silu,
                bias=SP[:, b0 + 248 + m:b0 + 249 + m], scale=SP[:, b0 + 244 + m:b0 + 245 + m]),
                reads=[rTMP[t], rSP], writes=[rG[8 + m]])
        gate_slot(7)
        for dc in range(NCH):
            ib = wslot(l, k0 + 15 + dc)
            pbk = []
            for i3 in range(3):
                bk = bank()
                pbk.append(bk)
                mm_group(PS[bk][:], [(RINGB[:, ib, (i3 * 4 + kc) * 128:(i3 * 4 + kc + 1) * 128], G[:, i3 * 4 + kc, :])
                                      for kc in range(4)], [rRING[ib]] + rG[i3 * 4:i3 * 4 + 4], bk)
            ta, tb_ = tmp(), tmp()
            sch.op("dve", lambda e, ta=ta, bk=pbk[0], dc=dc: e.scalar_tensor_tensor(
                out=TMP[:, ta, :], in0=GS[:, dc * 3 + 0, :], scalar=1.0, in1=PS[bk][:], op0=ALU.add, op1=ALU.mult),
                reads=[rGS[dc * 3 + 0], rPS[pbk[0]]], writes=[rTMP[ta]])
            sch.op("dve", lambda e, tb_=tb_, bk=pbk[1], dc=dc: e.scalar_tensor_tensor(
                out=TMP[:, tb_, :], in0=GS[:, dc * 3 + 1, :], scalar=1.0, in1=PS[bk][:], op0=ALU.add, op1=ALU.mult),
                reads=[rGS[dc * 3 + 1], rPS[pbk[1]]], writes=[rTMP[tb_]])
            sch.op("dve", lambda e, ta=ta, tb_=tb_: e.tensor_tensor(out=TMP[:, ta, :], in0=TMP[:, ta, :], in1=TMP[:, tb_, :], op=ALU.add),
                   reads=[rTMP[ta], rTMP[tb_]], writes=[rTMP[ta]])
            sch.op("dve", lambda e, tb_=tb_, bk=pbk[2], dc=dc: e.scalar_tensor_tensor(
                out=TMP[:, tb_, :], in0=GS[:, dc * 3 + 2, :], scalar=1.0, in1=PS[bk][:], op0=ALU.add, op1=ALU.mult),
                reads=[rGS[dc * 3 + 2], rPS[pbk[2]]], writes=[rTMP[tb_]])
            sch.op("dve", lambda e, ta=ta, tb_=tb_, dc=dc: e.tensor_tensor(out=SQ[:, dc, :], in0=TMP[:, ta, :], in1=TMP[:, tb_, :], op=ALU.add),
                   reads=[rTMP[ta], rTMP[tb_]], writes=[rSQ[dc]])
        for dc in range(NCH):
            if dc % 4 == 0:
                i = wslot(l, k0 + 23 + dc // 4)
            bk = bank()
            mm_group(PS[bk][:], [(RINGB[:, i, kc * 512 + (dc % 4) * 128:kc * 512 + (dc % 4 + 1) * 128], SQ[:, kc, :])
                                  for kc in range(NCH)], [rRING[i]] + rSQ, bk)
            sch.op("dve", lambda e, dc=dc, bk=bk: e.scalar_tensor_tensor(
                out=X[:, dc, :], in0=PS[bk][:], scalar=DER[:, l, 5 * 8 + dc:5 * 8 + dc + 1], in1=X[:, dc, :],
                op0=ALU.mult, op1=ALU.add), reads=[rPS[bk], rDER[l], rX[dc]], writes=[rX[dc]])

    def load_x(ti):
        for c in range(NCH):
            sch.dma("sp", lambda e, c=c, ti=ti: e.dma_start(out=X[:, c, :], in_=xT[c * 128:(c + 1) * 128, ti * T:(ti + 1) * T]),
                    f"xld{c}", writes=[rX[c]])

    load_x(0)
    for ti in range(n_tiles):
        for l in range(n_layers):
            if ti == 0:
                state["reserve7"] = True
                if l == 0:
                    ada_begin(0)
                ada_flush()
                if l + 1 < n_layers:
                    ada_begin(l + 1)
            else:
                state["reserve7"] = False
            ffn_stage(l, 0, 0, 1, 2)
            mixer_stage(l, 17)
            ffn_stage(l, 42, 6, 7, 8)
        if final_norm:
            rms_stats()
        for c in range(NCH):
            o = c % 2
            if final_norm:
                t = tmp()
                sch.op("dve", lambda e, c=c, t=t: e.tensor_tensor(out=TMP[:, t, :], in0=X[:, c, :], in1=RSTD[:], op=ALU.mult),
                       reads=[rX[c], rRSTD], writes=[rTMP[t]])
                sch.op("act", lambda e, c=c, t=t, o=o: e.activation(out=OST[:, o, :], in_=TMP[:, t, :], func=AF.Identity,
                                                                  scale=GF[:, c:c + 1]),
                       reads=[rTMP[t], rGF], writes=[rOST[o]])
            else:
                sch.op("act", lambda e, c=c, o=o: e.activation(out=OST[:, o, :], in_=X[:, c, :], func=AF.Identity),
                       reads=[rX[c]], writes=[rOST[o]])
            sch.dma("sp", lambda e, c=c, ti=ti, o=o: e.dma_start(out=outT[c * 128:(c + 1) * 128, ti * T:(ti + 1) * T], in_=OST[:, o, :]),
                    f"ost{o}", reads=[rOST[o]])
            if ti + 1 < n_tiles:
                sch.dma("sp", lambda e, c=c, ti=ti: e.dma_start(out=X[:, c, :], in_=xT[c * 128:(c + 1) * 128, (ti + 1) * T:(ti + 2) * T]),
                        f"xld{c}", writes=[rX[c]])
    sch.final_wait("sp", "ost0")
    sch.final_wait("sp", "ost1")

    semnames = list(Sched.ENGS) + sorted(sch.dma_cnt.keys())
    sems = {n: es.enter_context(nc.semaphore("s_" + n)) for n in semnames}
    block = es.enter_context(nc.Block())

    def emit(ename):
        def body(eng):
            for waits, fn, inc in sch.ops[ename]:
                for k, v in waits:
                    eng.wait_ge(sems[k], v)
                if fn is None:
                    continue
                ins = fn(eng)
                ins.then_inc(sems[inc[0]], inc[1])
        return body

    block.tensor(emit("pe"))
    block.scalar(emit("act"))
    block.vector(emit("dve"))
    block.gpsimd(emit("pool"))
    block.sync(emit("sp"))
    try:
        print("sbuf bytes remaining/partition:", nc.sbuf_bytes_remaining)
    except Exception:
        pass
    es.close()
    return nc, recorded


def kernel(**inputs):
    ins = {k: np.asarray(v) for k, v in inputs.items()}
    nc, _ = build_program(DEPTH, SEQ // T, True)
    maps = prep_inputs(ins, DEPTH, SEQ // T, list(range(NCORES)))
    res = run_bass_kernel_spmd(nc, maps, core_ids=list(range(NCORES)))
    out = np.empty((NCORES, SEQ, D), np.float32)
    for b in range(NCORES):
        out[b] = res.results[b]["outT"].T
    return out
```

```python
import contextlib
import numpy as np
import concourse.bass as bass
import concourse.mybir as mybir
from concourse.bass_utils import run_bass_kernel_spmd

F32 = mybir.dt.float32
BF16 = mybir.dt.bfloat16
AF = mybir.ActivationFunctionType
ALU = mybir.AluOpType

D = 1024
NCH = 8
T = 512
DFF = 2816
NHC = 22
IN_COLS = 6656
N_ADA = 9
EPS = 1e-6
SLOTW = 4096
RING = 5
HALO = 32
LSP = 252
SEQ = 4096
DEPTH = 4
NCORES = 8
USE_ARSQRT = False


def _kc_tile(w, cols):
    K = w.shape[0]
    sub = w[:, cols]
    sub = sub.reshape(K // 128, 128, len(cols))
    return np.ascontiguousarray(sub.transpose(1, 0, 2)).reshape(128, -1)


def _layer_slots(l, ins):
    slots = []

    def ffn(w13, w2):
        for j2 in range(NHC // 2):
            cols = np.concatenate([
                np.arange((2 * j2) * 128, (2 * j2 + 2) * 128),
                DFF + np.arange((2 * j2) * 128, (2 * j2 + 2) * 128)])
            slots.append(_kc_tile(w13, cols))
        blk = w2.reshape(NHC, 128, NCH, 128).transpose(1, 2, 0, 3)
        blk = np.ascontiguousarray(blk).reshape(128, NCH * NHC * 128)
        for s in range(0, NCH * NHC * 128, SLOTW):
            slots.append(np.ascontiguousarray(blk[:, s:s + SLOTW]))

    ffn(ins["ffn1_w13"][l], ins["ffn1_w2"][l])
    w_in = ins["w_in"][l]
    for base in (3072, 2560, 512, 0, 1536, 2048, 1024):
        slots.append(_kc_tile(w_in, np.arange(base, base + 512)))
    wb = [ins["w_branch_a"][l], ins["w_branch_b"][l], ins["w_branch_c"][l]]
    for dc in range(NCH):
        g = [_kc_tile(w_in, 3584 + i * 1024 + dc * 128 + np.arange(128)) for i in range(3)]
        slots.append(np.concatenate(g, axis=1))
    for dc in range(NCH):
        b = [_kc_tile(wb[i], dc * 128 + np.arange(128)) for i in range(3)]
        slots.append(np.concatenate(b, axis=1))
    w_o = ins["w_o"][l]
    for s in range(2):
        slots.append(_kc_tile(w_o, np.arange(s * 512, (s + 1) * 512)))
    ffn(ins["ffn2_w13"][l], ins["ffn2_w2"][l])
    return slots


def _slot_widths():
    ffn = [SLOTW] * 11 + [SLOTW] * 5 + [NCH * NHC * 128 - 5 * SLOTW]
    mix = [SLOTW] * 7 + [3072] * NCH + [1536] * NCH + [SLOTW] * 2
    return ffn + mix + ffn


def _fm(v, n):
    return np.ascontiguousarray(v.reshape(n, 128).T)


def prep_inputs(ins, n_layers, n_tiles, batch_ids):
    widths = _slot_widths()
    wst = np.empty((n_layers, 128 * sum(widths)), np.float32)
    for l in range(n_layers):
        off = 0
        for s, w in zip(_layer_slots(l, ins), widths):
            assert s.shape == (128, w), (s.shape, w)
            wst[l, off:off + 128 * w] = s.reshape(-1)
            off += 128 * w
    wada = np.empty((n_layers, 18, 128, SLOTW), np.float32)
    for l in range(n_layers):
        for s in range(18):
            wada[l, s] = _kc_tile(ins["w_ada"][l], np.arange(s * 512, (s + 1) * 512))
    nsp = n_layers * LSP + 16
    spw = np.empty((n_layers, 128, 1024), np.float32)
    sp_common = np.zeros((128, nsp), np.float32)
    for l in range(n_layers):
        b = l * LSP
        sp_common[:, b + 0:b + 8] = _fm(ins["g_ffn1"][l], 8)
        sp_common[:, b + 8:b + 16] = _fm(ins["g_mix"][l], 8)
        sp_common[:, b + 16:b + 24] = _fm(ins["g_ffn2"][l], 8)
        sp_common[:, b + 24:b + 96] = _fm(ins["b_ada"][l], 72)
        sp_common[:, b + 96:b + 100] = _fm(ins["a_ln_g"][l], 4)
        sp_common[:, b + 100:b + 104] = _fm(ins["a_ln_b"][l], 4)
        bc = ins["b_conv"][l].reshape(3, 4, 128).transpose(2, 1, 0).reshape(128, 12)
        cc = ins["c_conv"][l].reshape(31, 4, 128).transpose(2, 1, 0).reshape(128, 124)
        sp_common[:, b + 104:b + 116] = bc
        sp_common[:, b + 116:b + 240] = cc
        sp_common[:, b + 240:b + 244] = _fm(ins["c_conv_b"][l], 4)
        sp_common[:, b + 244:b + 248] = _fm(ins["c_ln_g"][l], 4)
        sp_common[:, b + 248:b + 252] = _fm(ins["c_ln_b"][l], 4)
        spw[l, :, 0:512] = ins["a_ws"][l].transpose(2, 0, 1).reshape(128, 512)
        spw[l, :, 512:1024] = np.broadcast_to(ins["a_bs"][l].reshape(1, 512), (128, 512))
    gb = n_layers * LSP
    sp_common[:, gb:gb + 8] = _fm(ins["g_final"], 8)
    maps = []
    for bi in batch_ids:
        sp = sp_common.copy()
        sp[:, gb + 8:gb + 16] = _fm(ins["c"][bi], 8)
        xT = np.ascontiguousarray(ins["x"][bi, :n_tiles * T, :].T)
        maps.append({"xT": xT, "wst": wst, "wada": wada, "sp": sp, "spw": spw})
    return maps


class Res:
    __slots__ = ("w", "r", "name")

    def __init__(self, name=""):
        self.w = None
        self.r = {}
        self.name = name


class Sched:
    ENGS = ("pe", "act", "dve", "pool", "sp")

    def __init__(self):
        self.ops = {e: [] for e in self.ENGS}
        self.cnt = {e: 0 for e in self.ENGS}
        self.seen = {e: {} for e in self.ENGS}
        self.clock = {e: [None] for e in self.ENGS}
        self.dma_cnt = {}

    def _deps(self, e, reads, writes):
        raw, other = {}, {}

        def add(dst, kv):
            k, v = kv
            if dst.get(k, 0) < v:
                dst[k] = v
        for r in reads:
            if r.w is not None:
                add(raw, r.w)
        for r in writes:
            if r.w is not None:
                add(other, r.w)
            for kv in r.r.items():
                add(other, kv)
        waits = []
        seen = self.seen[e]
        changed = False
        for src, is_raw in ((raw, True), (other, False)):
            for k, v in src.items():
                if k == e:
                    if e in ("pe", "sp") or not is_raw:
                        continue
                if seen.get(k, 0) >= v:
                    continue
                if not changed:
                    seen = dict(seen)
                    changed = True
                seen[k] = v
                waits.append((k, v))
                if k in self.clock:
                    snap = self.clock[k][v]
                    for k2, v2 in snap.items():
                        if seen.get(k2, 0) < v2:
                            seen[k2] = v2
        if changed:
            self.seen[e] = seen
        return waits

    def op(self, e, fn, reads=(), writes=()):
        waits = self._deps(e, reads, writes)
        self.cnt[e] += 1
        n = self.cnt[e]
        self.clock[e].append(self.seen[e])
        self.ops[e].append((waits, fn, (e, 1)))
        for r in writes:
            r.w = (e, n)
            r.r = {}
        for r in reads:
            if r.r.get(e, 0) < n:
                r.r[e] = n

    def dma(self, q, fn, semname, reads=(), writes=()):
        waits = self._deps(q, reads, writes)
        prev = self.dma_cnt.get(semname, 0)
        if prev and self.seen[q].get(semname, 0) < prev:
            s = dict(self.seen[q])
            s[semname] = prev
            self.seen[q] = s
            waits.append((semname, prev))
        val = prev + 16
        self.dma_cnt[semname] = val
        self.ops[q].append((waits, fn, (semname, 16)))
        for r in writes:
            r.w = (semname, val)
            r.r = {}
        for r in reads:
            r.r[semname] = val

    def final_wait(self, e, semname):
        v = self.dma_cnt.get(semname, 0)
        if v and self.seen[e].get(semname, 0) < v:
            self.ops[e].append(([(semname, v)], None, None))


def build_program(n_layers, n_tiles, final_norm=True, stream=None):
    nc = bass.Bass("TRN2", target_bir_lowering=False)
    S = n_tiles * T
    widths = _slot_widths()
    offs = np.concatenate([[0], np.cumsum([128 * w for w in widths])]).astype(np.int64)
    nsp = n_layers * LSP + 16
    xT = nc.dram_tensor("xT", [D, S], F32, kind="ExternalInput").ap()
    wst = nc.dram_tensor("wst", [n_layers, int(offs[-1])], F32, kind="ExternalInput").ap()
    wada = nc.dram_tensor("wada", [n_layers, 18, 128, SLOTW], F32, kind="ExternalInput").ap()
    spd = nc.dram_tensor("sp", [128, nsp], F32, kind="ExternalInput").ap()
    spwd = nc.dram_tensor("spw", [n_layers, 128, 1024], F32, kind="ExternalInput").ap()
    outT = nc.dram_tensor("outT", [D, S], F32, kind="ExternalOutput").ap()

    sch = Sched()
    es = contextlib.ExitStack()

    def sb(name, shape, dt):
        return es.enter_context(nc.sbuf_tensor(name, shape, dt))

    X = sb("X", [128, NCH, T], F32)
    OST = sb("OST", [128, 2, T], F32)
    SQ = sb("SQ", [128, NCH, T], BF16)
    RSTD = sb("RSTD", [128, T], F32)
    H = sb("H", [128, NCH, T], BF16)
    G = sb("G", [128, NHC, T], BF16)
    NTMP = 6
    TMP = sb("TMP", [128, NTMP, T], F32)
    SA = sb("SA", [128, 4, T], F32)
    SB_ = sb("SBb", [128, 4, T], F32)
    SC = sb("SC", [128, 4, HALO + T], F32)
    SD = sb("SD", [128, 4, T], F32)
    VN = sb("VN", [128, 4, 512], BF16)
    ST6 = sb("ST6", [128, 4, 8], F32)
    MV = sb("MV", [128, 4, 2], F32)
    STAT = sb("STAT", [128, 2, T], F32)
    GS = sb("GS", [128, 24, T], BF16)
    SCB = sb("SCB", [128, 4, T + 2], F32)
    CW = sb("CW", [128, n_layers, 124], F32)
    RINGB = sb("RINGB", [128, RING, SLOTW], BF16)
    SP = sb("SP", [128, nsp], F32)
    SCR = TMP[:, 0:2, :].rearrange("p a b -> p (a b)")
    DER = sb("DER", [128, n_layers, 72], F32)
    GF = sb("GF", [128, 8], F32)
    WMT = sb("WMT", [128, n_layers, 4, 128], BF16)
    BM = sb("BM", [128, n_layers, 4, 128], F32)
    ONES = sb("ONES", [128, 128], BF16)
    CACT = sb("CACT", [128, 8], BF16)
    CST = sb("CST", [128, 2], F32)
    DUM = sb("DUM", [128, 2], F32)
    TAILB = sb("TAILB", [128, n_layers, 4, 2], F32)
    TAILC = sb("TAILC", [128, n_layers, 4, 30], F32)
    PS = [es.enter_context(nc.psum_tensor(f"ps{i}", [128, 512], F32)) for i in range(8)]

    rX = [Res(f"X{c}") for c in range(NCH)]
    rOST = [Res() for _ in range(2)]
    rSQ = [Res() for _ in range(NCH)]
    rRSTD = Res()
    rH = [Res() for _ in range(NCH)]
    rG = [Res() for _ in range(NHC)]
    rTMP = [Res() for _ in range(NTMP)]
    rSA = [Res() for _ in range(4)]
    rSB = [Res() for _ in range(4)]
    rSC = [Res() for _ in range(4)]
    rSD = [Res() for _ in range(4)]
    rVN = [Res() for _ in range(4)]
    rST = [Res() for _ in range(4)]
    rSTAT = [Res() for _ in range(2)]
    rGS = [Res() for _ in range(24)]
    rSCB = [Res() for _ in range(4)]
    rCW = [Res() for _ in range(n_layers)]
    rRING = [Res() for _ in range(RING)]
    rSP = Res()
    rDER = [Res() for _ in range(n_layers)]
    rGF = Res()
    rWMT = [Res() for _ in range(n_layers)]
    rBM = [Res() for _ in range(n_layers)]
    rONES = Res()
    rCACT = Res()
    rCST = Res()
    rDUM = Res()
    rTB = [Res() for _ in range(n_layers)]
    rTC = [Res() for _ in range(n_layers)]
    rPS = [Res() for _ in range(8)]

    state = {"bank": 0, "tmp": 0, "slot": 0}

    def bank():
        b = state["bank"]
        while (state.get("reserve7") and b == 7) or b in state.get("hold", ()):
            b = (b + 1) % 8
        state["bank"] = (b + 1) % 8
        return b

    def tmp():
        t = state["tmp"]
        state["tmp"] = (t + 1) % NTMP
        return t

    recorded = []

    def src_of(key):
        if key[0] == "ada":
            return wada[key[1], key[2]], SLOTW
        _, l, k = key
        return wst[l, int(offs[k]):int(offs[k + 1])].rearrange("(p w) -> p w", p=128), widths[k]

    def emit_slot_dma(n, key):
        src, width = src_of(key)
        i = n % RING
        dst = RINGB[:, i, 0:width]
        sch.dma("pool", lambda e, dst=dst, src=src: e.dma_start(out=dst, in_=src), f"ring{i}",
                writes=[rRING[i]])

    def load_slot(key):
        n = state["slot"]
        state["slot"] = n + 1
        recorded.append(key)
        if stream is None:
            emit_slot_dma(n, key)
        else:
            assert stream[n] == key, (n, stream[n], key)
            hi = min(n + RING - 2, len(stream) - 1)
            while state.get("issued", -1) < hi:
                state["issued"] = state.get("issued", -1) + 1
                emit_slot_dma(state["issued"], stream[state["issued"]])
        return n % RING

    def wslot(l, k):
        if state.get("ada_q"):
            state["ada_ctr"] = state.get("ada_ctr", 0) + 1
            if state["ada_ctr"] % 3 == 0:
                state["ada_q"].pop(0)()
        return load_slot(("w", l, k))

    sch.dma("sp", lambda e: e.dma_start(out=SP[:], in_=spd), "spld", writes=[rSP])
    sch.op("dve", lambda e: e.memset(ONES[:], 1.0), writes=[rONES])
    sch.op("dve", lambda e: e.memset(CST[:, 0:1], float(EPS * D)), writes=[rCST])
    sch.op("dve", lambda e: e.memset(CST[:, 1:2], float(EPS)), writes=[rCST])
    sch.op("dve", lambda e: e.memset(TAILB[:], 0.0), writes=rTB)
    sch.op("dve", lambda e: e.memset(TAILC[:], 0.0), writes=rTC)
    gb = n_layers * LSP
    sch.op("act", lambda e: e.activation(out=CACT[:], in_=SP[:, gb + 8:gb + 16], func=AF.Silu),
           reads=[rSP], writes=[rCACT])
    sch.op("dve", lambda e: e.tensor_scalar(out=GF[:], in0=SP[:, gb:gb + 8], scalar1=float(np.sqrt(D)),
                                            scalar2=None, op0=ALU.mult), reads=[rSP], writes=[rGF])
    for l in range(n_layers):
        b0 = l * LSP
        sch.op("dve", lambda e, l=l, b0=b0: e.tensor_scalar(out=CW[:, l, :], in0=SP[:, b0 + 116:b0 + 240], scalar1=0.5,
                                                         scalar2=None, op0=ALU.mult), reads=[rSP], writes=[rCW[l]])
        sch.dma("sp", lambda e, l=l: e.dma_start(out=SCR[:], in_=spwd[l]), "scrld", writes=[rTMP[0], rTMP[1]])
        sch.op("dve", lambda e, l=l: e.tensor_copy(out=WMT[:, l].rearrange("p g q -> p (g q)"), in_=SCR[:, 0:512]),
               reads=[rTMP[0], rTMP[1]], writes=[rWMT[l]])
        sch.op("dve", lambda e, l=l: e.memset(WMT[64:128, l, :, 0:64], 0.0), writes=[rWMT[l]])
        bk = bank()
        sch.op("pe", lambda e, l=l, bk=bk: e.matmul(PS[bk][:], ONES[:], WMT[:, l].rearrange("p g q -> p (g q)"),
                                                   start=True, stop=True),
               reads=[rONES, rWMT[l]], writes=[rPS[bk]])
        for g in range(4):
            sch.op("dve", lambda e, l=l, g=g, bk=bk, b0=b0: e.scalar_tensor_tensor(
                out=BM[:, l, g, :], in0=PS[bk][:, g * 128:(g + 1) * 128], scalar=SP[:, b0 + 100 + g:b0 + 101 + g],
                in1=SCR[:, 512 + g * 128:512 + (g + 1) * 128], op0=ALU.mult, op1=ALU.add),
                reads=[rPS[bk], rSP, rTMP[0], rTMP[1]], writes=[rBM[l]])

    def ada_begin(l):
        q = state.setdefault("ada_q", [])
        for s in range(18):
            q.append(lambda s=s: ada_slot(l, s))
        q.append(lambda: ada_finish(l))

    def ada_flush():
        q = state.get("ada_q", [])
        while q:
            q.pop(0)()

    def ada_slot(l, s):
        bk = 7
        if True:
            i = load_slot(("ada", l, s))

            def f(e, i=i, s=s, bk=bk):
                ins = None
                for cc in range(4):
                    col = s * 4 + cc
                    for kc in range(NCH):
                        ins = e.matmul(PS[bk][:, col:col + 1],
                                       RINGB[:, i, kc * 512 + cc * 128:kc * 512 + (cc + 1) * 128],
                                       CACT[:, kc:kc + 1], start=(kc == 0), stop=(kc == NCH - 1))
                return ins
            sch.op("pe", f, reads=[rRING[i], rCACT], writes=[rPS[bk]])

    def ada_finish(l):
        b0 = l * LSP
        bk = 7
        dl = DER[:, l, :]
        sch.op("dve", lambda e: e.tensor_tensor(out=dl, in0=PS[bk][:, 0:72], in1=SP[:, b0 + 24:b0 + 96], op=ALU.add),
               reads=[rPS[bk], rSP], writes=[rDER[l]])
        for j, gcol in ((1, 0), (4, 8), (7, 16)):
            sch.op("dve", lambda e, j=j, gcol=gcol: e.scalar_tensor_tensor(
                out=DER[:, l, j * 8:j * 8 + 8], in0=DER[:, l, j * 8:j * 8 + 8], scalar=1.0,
                in1=SP[:, b0 + gcol:b0 + gcol + 8], op0=ALU.add, op1=ALU.mult),
                reads=[rDER[l], rSP], writes=[rDER[l]])
            sch.op("dve", lambda e, j=j: e.tensor_scalar(
                out=DER[:, l, j * 8:j * 8 + 8], in0=DER[:, l, j * 8:j * 8 + 8], scalar1=float(np.sqrt(D)),
                scalar2=None, op0=ALU.mult), reads=[rDER[l]], writes=[rDER[l]])
        for j in (2, 5, 8):
            sch.op("dve", lambda e, j=j: e.tensor_scalar(
                out=DER[:, l, j * 8:j * 8 + 8], in0=DER[:, l, j * 8:j * 8 + 8], scalar1=0.5,
                scalar2=None, op0=ALU.mult), reads=[rDER[l]], writes=[rDER[l]])

    def rms_stats(dst=None, dres=None, pre=None):
        dst = RSTD[:] if dst is None else dst
        dres = rRSTD if dres is None else dres
        sch.op("act", lambda e: e.activation(out=DUM[:, 0:1], in_=CST[:, 0:1],
                                             func=(AF.Abs_reciprocal_sqrt if USE_ARSQRT else AF.Sqrt)), reads=[rCST], writes=[rDUM])
        for c in range(NCH):
            sch.op("act", lambda e, c=c: e.activation(out=SQ[:, c, :], in_=X[:, c, :], func=AF.Square),
                   reads=[rX[c]], writes=[rSQ[c]])
            if pre is not None:
                pre(c)
        bk = bank()

        for c in range(NCH):
            sch.op("pe", lambda e, c=c: e.matmul(PS[bk][:], ONES[:], SQ[:, c, :], start=(c == 0), stop=(c == NCH - 1)),
                   reads=[rSQ[c], rONES], writes=[rPS[bk]])
        if USE_ARSQRT:
            sch.op("act", lambda e: e.activation(out=RSTD[:], in_=PS[bk][:], func=AF.Abs_reciprocal_sqrt, bias=CST[:, 0:1], scale=1.0),
                   reads=[rPS[bk], rCST], writes=[rRSTD])
        else:
            sch.op("act", lambda e: e.activation(out=dst, in_=PS[bk][:], func=AF.Sqrt, bias=CST[:, 0:1], scale=1.0),
                   reads=[rPS[bk], rCST], writes=[dres])
            sch.op("dve", lambda e: e.reciprocal(out=dst, in_=dst), reads=[dres], writes=[dres])

    def norm(l, jshift, jscale):
        rms_stats()
        for c in range(NCH):
            t = tmp()
            sch.op("dve", lambda e, c=c, t=t: e.tensor_tensor(out=TMP[:, t, :], in0=X[:, c, :], in1=RSTD[:], op=ALU.mult),
                   reads=[rX[c], rRSTD], writes=[rTMP[t]])
            sch.op("act", lambda e, c=c, t=t: e.activation(
                out=H[:, c, :], in_=TMP[:, t, :], func=AF.Identity,
                bias=DER[:, l, jshift * 8 + c:jshift * 8 + c + 1], scale=DER[:, l, jscale * 8 + c:jscale * 8 + c + 1]),
                reads=[rTMP[t], rDER[l]], writes=[rH[c]])

    def mm_group(out_ap, pairs, reads, bk):
        def f(e):
            ins = None
            n = len(pairs)
            for k, (lhsT, rhs) in enumerate(pairs):
                ins = e.matmul(out_ap, lhsT, rhs, start=(k == 0), stop=(k == n - 1))
            return ins
        sch.op("pe", f, reads=reads, writes=[rPS[bk]])

    def mm_groups_splitk(groups, ring_res, split=4, lo_reads=None, hi_reads=None):
        banks = [rPS[bk] for (_, bk, _) in groups]
        ring_res = ring_res if isinstance(ring_res, list) else [ring_res]
        lo_reads = rH[0:4] if lo_reads is None else lo_reads
        hi_reads = rH[4:8] if hi_reads is None else hi_reads

        def half(lo, hi):
            def f(e):
                ins = None
                for out_ap, bk, pairs in groups:
                    n = len(pairs)
                    for kc in range(lo, n if hi is None else hi):
                        ins = e.matmul(out_ap, pairs[kc][0], pairs[kc][1], start=(kc == 0), stop=(kc == n - 1))
                return ins
            return f
        sch.op("pe", half(0, split), reads=ring_res + lo_reads, writes=banks)
        sch.op("pe", half(split, None), reads=ring_res + hi_reads, writes=banks)

    def ffn_stage(l, k0, jshift, jscale, jgate):
        norm(l, jshift, jscale)
        for f in state.pop("after_norm", []):
            f()
        pre = {}
        for j in range(NHC):
            if j % 2 == 0:
                i = wslot(l, k0 + j // 2)
            jj = j % 2
            if j == 0:
                grp = []
                for j2 in (0, 1):
                    pre[j2] = (bank(), bank())
                    for q, bk in ((j2, pre[j2][0]), (2 + j2, pre[j2][1])):
                        grp.append((PS[bk][:], bk, [(RINGB[:, i, kc * 512 + q * 128:kc * 512 + (q + 1) * 128], H[:, kc, :])
                                                    for kc in range(NCH)]))
                mm_groups_splitk(grp, rRING[i])
            if j in pre:
                ba, bb = pre[j]
            else:
                ba, bb = bank(), bank()
                for q, bk in ((jj, ba), (2 + jj, bb)):
                    mm_group(PS[bk][:], [(RINGB[:, i, kc * 512 + q * 128:kc * 512 + (q + 1) * 128], H[:, kc, :])
                                          for kc in range(NCH)], [rRING[i]] + rH, bk)
            t = tmp()
            sch.op("act", lambda e, t=t, ba=ba: e.activation(out=TMP[:, t, :], in_=PS[ba][:], func=AF.Silu),
                   reads=[rPS[ba]], writes=[rTMP[t]])
            sch.op("dve", lambda e, t=t, bb=bb, j=j: e.tensor_tensor(out=G[:, j, :], in0=TMP[:, t, :], in1=PS[bb][:], op=ALU.mult),
                   reads=[rTMP[t], rPS[bb]], writes=[rG[j]])
        cur = {}
        for dc in range(NCH):
            pairs, reads = [], list(rG)
            for hc in range(NHC):
                bi = dc * NHC + hc
                s = bi // 32
                if s not in cur:
                    cur.clear()
                    cur[s] = wslot(l, k0 + 11 + s)
                i = cur[s]
                if rRING[i] not in reads:
                    reads.append(rRING[i])
                pairs.append((i, (bi % 32) * 128, hc))
            bk = bank()
            if dc == 0:
                rr = [r for r in reads if r not in rG]
                mm_groups_splitk([(PS[bk][:], bk, [(RINGB[:, i, o:o + 128], G[:, hc, :]) for (i, o, hc) in pairs])],
                                 rr, split=NHC - 2, lo_reads=rG[0:NHC - 2], hi_reads=rG[NHC - 2:NHC])
            else:
                mm_group(PS[bk][:], [(RINGB[:, i, o:o + 128], G[:, hc, :]) for (i, o, hc) in pairs], reads, bk)
            sch.op("dve", lambda e, dc=dc, bk=bk: e.scalar_tensor_tensor(
                out=X[:, dc, :], in0=PS[bk][:], scalar=DER[:, l, jgate * 8 + dc:jgate * 8 + dc + 1], in1=X[:, dc, :],
                op0=ALU.mult, op1=ALU.add), reads=[rPS[bk], rDER[l], rX[dc]], writes=[rX[dc]])

    def mixer_stage(l, k0):
        b0 = l * LSP
        norm(l, 3, 4)
        pend = []

        ppend = []

        def drip(n):
            for _ in range(min(n, len(pend))):
                pend.pop(0)()
            for _ in range(min(max(1, (2 * n + 2) // 3), len(ppend))):
                ppend.pop(0)()

        def std_groups(i, banks=None, split=False):
            if split:
                bks = [bank() for _ in range(4)]
                mm_groups_splitk([(PS[bks[m]][:], bks[m],
                                   [(RINGB[:, i, kc * 512 + m * 128:kc * 512 + (m + 1) * 128], H[:, kc, :]) for kc in range(NCH)])
                                  for m in range(4)], rRING[i])
                for m in range(4):
                    yield m, bks[m]
                return
            for m in range(4):
                bk = banks[m] if banks is not None else bank()
                mm_group(PS[bk][:], [(RINGB[:, i, kc * 512 + m * 128:kc * 512 + (m + 1) * 128], H[:, kc, :])
                                      for kc in range(NCH)], [rRING[i]] + rH, bk)
                yield m, bk

        i = wslot(l, k0 + 0)
        for m, bk in std_groups(i, split=True):
            sch.op("act", lambda e, m=m, bk=bk: e.activation(out=SB_[:, m, :], in_=PS[bk][:], func=AF.Tanh, scale=0.5),
                   reads=[rPS[bk]], writes=[rSB[m]])
        i = wslot(l, k0 + 1)
        for m, bk in std_groups(i):
            sch.op("act", lambda e, m=m: e.activation(out=SC[:, m, HALO - 30:HALO], in_=TAILC[:, l, m, :], func=AF.Identity),
                   reads=[rTC[l]], writes=[rSC[m]])
            sch.op("dve", lambda e, m=m, bk=bk: e.scalar_tensor_tensor(
                out=SC[:, m, HALO:HALO + T], in0=SB_[:, m, :], scalar=1.0, in1=PS[bk][:], op0=ALU.add, op1=ALU.mult),
                reads=[rSB[m], rPS[bk]], writes=[rSC[m]])
            sch.op("act", lambda e, m=m: e.activation(out=TAILC[:, l, m, :], in_=SC[:, m, HALO + T - 30:HALO + T], func=AF.Identity),
                   reads=[rSC[m]], writes=[rTC[l]])
        order = [(k, m) for k in range(27) for m in range(4)] + [(k, m) for m in range(4) for k in range(27, 31)]
        for k, m in order:
            if True:
                src = SC[:, m, HALO - 30 + k:HALO - 30 + k + T]
                wv = CW[:, l, m * 31 + k:m * 31 + k + 1]
                if k == 0:
                    sch.op("act", lambda e, m=m, wv=wv, src=src: e.activation(
                        out=SD[:, m, :], in_=src, func=AF.Identity, scale=wv, bias=SP[:, b0 + 240 + m:b0 + 241 + m]),
                        reads=[rSC[m], rCW[l], rSP], writes=[rSD[m]])
                else:
                    pend.append(lambda m=m, wv=wv, src=src: sch.op("dve", lambda e: e.scalar_tensor_tensor(
                        out=SD[:, m, :], in0=src, scalar=wv, in1=SD[:, m, :],
                        op0=ALU.mult, op1=ALU.add), reads=[rSC[m], rCW[l], rSD[m]], writes=[rSD[m]]))
        NDRIP = 4
        i = wslot(l, k0 + 2)
        mb = [bank() for _ in range(4)]
        state["hold"] = set(mb)
        sgu_pending = []
        vt = []
        for tb in range(4):
            bk = bank()
            mm_group(PS[bk][:], [(H[:, kc, tb * 128:(tb + 1) * 128], RINGB[:, i, kc * 512:(kc + 1) * 512])
                                  for kc in range(NCH)], [rRING[i]] + rH, bk)
            t = tmp()
            vt.append(t)
            v = tb
            sch.op("act", lambda e, t=t, bk=bk: e.activation(out=TMP[:, t, :], in_=PS[bk][:], func=AF.Gelu),
                   reads=[rPS[bk]], writes=[rTMP[t]])
            sch.op("dve", lambda e, t=t, v=v: e.bn_stats(out=ST6[:, v, 0:6], in_=TMP[:, t, :]),
                   reads=[rTMP[t]], writes=[rST[v]])
            sch.op("dve", lambda e, v=v: e.bn_aggr(out=MV[:, v, :], in_=ST6[:, v, 0:6]),
                   reads=[rST[v]], writes=[rST[v]])
        for tb in range(4):
            t = vt[tb]
            v = tb
            sch.op("act", lambda e, v=v: e.activation(out=MV[:, v, 1:2], in_=MV[:, v, 1:2], func=AF.Sqrt,
                                                     bias=CST[:, 1:2], scale=1.0),
                   reads=[rST[v], rCST], writes=[rST[v]])
            sch.op("dve", lambda e, v=v: e.reciprocal(out=MV[:, v, 1:2], in_=MV[:, v, 1:2]),
                   reads=[rST[v]], writes=[rST[v]])
            sch.op("dve", lambda e, t=t, v=v: e.tensor_scalar(out=VN[:, v, :], in0=TMP[:, t, :], scalar1=MV[:, v, 0:1],
                                                             scalar2=MV[:, v, 1:2], op0=ALU.subtract, op1=ALU.mult),
                   reads=[rTMP[t], rST[v]], writes=[rVN[v]])

            def sgu(v=v, tb=tb):
                def f(e):
                    ins = None
                    for g in range(4):
                        ins = e.matmul(PS[mb[g]][:, tb * 128:(tb + 1) * 128], VN[:, v, g * 128:(g + 1) * 128],
                                       WMT[:, l, g, :], start=True, stop=True)
                    return ins
                sch.op("pe", f, reads=[rVN[v], rWMT[l]], writes=[rPS[b] for b in mb])
            sgu_pending.append(sgu)
        i = wslot(l, k0 + 3)
        for m, bk in std_groups(i):
            if m >= 2:
                sgu_pending.pop(0)()
            sch.op("act", lambda e, m=m, bk=bk: e.activation(out=SA[:, m, :], in_=PS[bk][:], func=AF.Gelu),
                   reads=[rPS[bk]], writes=[rSA[m]])
            drip(NDRIP)
        i = wslot(l, k0 + 4)
        for m, bk in std_groups(i):
            if sgu_pending:
                sgu_pending.pop(0)()
            sch.op("act", lambda e, m=m, bk=bk: e.activation(out=SB_[:, m, :], in_=PS[bk][:], func=AF.Identity),
                   reads=[rPS[bk]], writes=[rSB[m]])
            drip(NDRIP)
        for g in range(4):
            t = tmp()
            sch.op("dve", lambda e, g=g, t=t: e.scalar_tensor_tensor(
                out=TMP[:, t, :].rearrange("p (a b) -> p a b", a=4),
                in0=PS[mb[g]][:].rearrange("p (a b) -> p a b", a=4),
                scalar=SP[:, b0 + 96 + g:b0 + 97 + g],
                in1=BM[:, l, g, :].unsqueeze(1).broadcast_to([128, 4, 128]),
                op0=ALU.mult, op1=ALU.add), reads=[rPS[mb[g]], rSP, rBM[l]], writes=[rTMP[t]])
            sch.op("dve", lambda e, g=g, t=t: e.tensor_tensor(out=G[:, g, :], in0=SA[:, g, :], in1=TMP[:, t, :], op=ALU.mult),
                   reads=[rSA[g], rTMP[t]], writes=[rG[g]])
            drip(2)
        i = wslot(l, k0 + 5)
        for m, bk in std_groups(i):
            sch.op("act", lambda e, m=m: e.activation(out=SCB[:, m, 0:2], in_=TAILB[:, l, m, :], func=AF.Identity),
                   reads=[rTB[l]], writes=[rSCB[m]])
            sch.op("dve", lambda e, m=m, bk=bk: e.tensor_tensor(out=SCB[:, m, 2:2 + T], in0=SB_[:, m, :], in1=PS[bk][:], op=ALU.mult),
                   reads=[rSB[m], rPS[bk]], writes=[rSCB[m]])
            sch.op("act", lambda e, m=m: e.activation(out=TAILB[:, l, m, :], in_=SCB[:, m, T:T + 2], func=AF.Identity),
                   reads=[rSCB[m]], writes=[rTB[l]])
            wb = b0 + 104 + m * 3
            sch.op("act", lambda e, m=m, wb=wb: e.activation(out=SB_[:, m, :], in_=SCB[:, m, 0:T], func=AF.Identity,
                                                            scale=SP[:, wb:wb + 1]),
                   reads=[rSCB[m], rSP], writes=[rSB[m]])
            for k in (1, 2):
                sch.op("dve", lambda e, m=m, wb=wb, k=k: e.scalar_tensor_tensor(
                    out=SB_[:, m, :], in0=SCB[:, m, k:k + T], scalar=SP[:, wb + k:wb + k + 1],
                    in1=SB_[:, m, :], op0=ALU.mult, op1=ALU.add), reads=[rSCB[m], rSP, rSB[m]], writes=[rSB[m]])
            drip(NDRIP)
        i = wslot(l, k0 + 6)
        for m, bk in std_groups(i):
            sch.op("dve", lambda e, m=m, bk=bk: e.tensor_tensor(out=G[:, 4 + m, :], in0=SB_[:, m, :], in1=PS[bk][:], op=ALU.mult),
                   reads=[rSB[m], rPS[bk]], writes=[rG[4 + m]])
            drip(NDRIP)
        state["hold"] = set()

        def gate_slot(dc):
            ig = wslot(l, k0 + 7 + dc)
            for i3 in range(3):
                bk = bank()
                mm_group(PS[bk][:], [(RINGB[:, ig, (i3 * 8 + kc) * 128:(i3 * 8 + kc + 1) * 128], H[:, kc, :])
                                      for kc in range(NCH)], [rRING[ig]] + rH, bk)
                sch.op("act", lambda e, i3=i3, bk=bk, dc=dc: e.activation(out=GS[:, dc * 3 + i3, :], in_=PS[bk][:],
                                                                       func=AF.Tanh, scale=0.5),
                       reads=[rPS[bk]], writes=[rGS[dc * 3 + i3]])
                drip(NDRIP + 2)
        for dc in range(6):
            gate_slot(dc)
        drip(len(pend))
        gate_slot(6)
        while ppend:
            ppend.pop(0)()
        for m in range(4):
            sch.op("act", lambda e, m=m: e.activation(out=G[:, 12 + m, :], in_=SD[:, m, :], func=AF.Identity),
                   reads=[rSD[m]], writes=[rG[12 + m]])
            sch.op("act", lambda e, m=m: e.activation(out=G[:, 16 + m, :], in_=SD[:, m, :], func=AF.Square),
                   reads=[rSD[m]], writes=[rG[16 + m]])
        gate_slot(7)
        b1, b2 = bank(), bank()
        mm_group(PS[b1][:], [(ONES[:], G[:, 12 + m, :]) for m in range(4)], [rONES] + rG[12:16], b1)
        mm_group(PS[b2][:], [(ONES[:], G[:, 16 + m, :]) for m in range(4)], [rONES] + rG[16:20], b2)
        sch.op("dve", lambda e: e.tensor_scalar(out=STAT[:, 0, :], in0=PS[b1][:], scalar1=1.0 / 512, scalar2=None, op0=ALU.mult),
               reads=[rPS[b1]], writes=[rSTAT[0]])
        sch.op("dve", lambda e: e.tensor_tensor(out=STAT[:, 1, :], in0=STAT[:, 0, :], in1=STAT[:, 0, :], op=ALU.mult),
               reads=[rSTAT[0]], writes=[rSTAT[1]])
        sch.op("dve", lambda e: e.scalar_tensor_tensor(out=STAT[:, 1, :], in0=PS[b2][:], scalar=1.0 / 512, in1=STAT[:, 1, :],
                                                       op0=ALU.mult, op1=ALU.subtract),
               reads=[rPS[b2], rSTAT[1]], writes=[rSTAT[1]])
        sch.op("act", lambda e: e.activation(out=STAT[:, 1, :], in_=STAT[:, 1, :], func=AF.Sqrt, bias=CST[:, 1:2], scale=1.0),
               reads=[rSTAT[1], rCST], writes=[rSTAT[1]])
        sch.op("dve", lambda e: e.reciprocal(out=STAT[:, 1, :], in_=STAT[:, 1, :]), reads=[rSTAT[1]], writes=[rSTAT[1]])
        for m in range(4):
            t = tmp()
            sch.op("dve", lambda e, m=m, t=t: e.tensor_tensor(out=TMP[:, t, :], in0=SD[:, m, :], in1=STAT[:, 0, :], op=ALU.subtract),
                   reads=[rSD[m], rSTAT[0]], writes=[rTMP[t]])
            sch.op("dve", lambda e, t=t: e.tensor_tensor(out=TMP[:, t, :], in0=TMP[:, t, :], in1=STAT[:, 1, :], op=ALU.mult),
                   reads=[rTMP[t], rSTAT[1]], writes=[rTMP[t]])
            sch.op("act", lambda e, m=m, t=t: e.activation(
                out=G[:, 8 + m, :], in_=TMP[:, t, :], func=AF.Silu,
                bias=SP[:, b0 + 248 + m:b0 + 249 + m], scale=SP[:, b0 + 244 + m:b0 + 245 + m]),
                reads=[rTMP[t], rSP], writes=[rG[8 + m]])
        for dc in range(NCH):
            ib = wslot(l, k0 + 15 + dc)
            pbk = []
            for i3 in range(3):
                bk = bank()
                pbk.append(bk)
                mm_group(PS[bk][:], [(RINGB[:, ib, (i3 * 4 + kc) * 128:(i3 * 4 + kc + 1) * 128], G[:, i3 * 4 + kc, :])
                                      for kc in range(4)], [rRING[ib]] + rG[i3 * 4:i3 * 4 + 4], bk)
            ta, tb_ = tmp(), tmp()
            sch.op("dve", lambda e, ta=ta, bk=pbk[0], dc=dc: e.scalar_tensor_tensor(
                out=TMP[:, ta, :], in0=GS[:, dc * 3 + 0, :], scalar=1.0, in1=PS[bk][:], op0=ALU.add, op1=ALU.mult),
                reads=[rGS[dc * 3 + 0], rPS[pbk[0]]], writes=[rTMP[ta]])
            sch.op("dve", lambda e, tb_=tb_, bk=pbk[1], dc=dc: e.scalar_tensor_tensor(
                out=TMP[:, tb_, :], in0=GS[:, dc * 3 + 1, :], scalar=1.0, in1=PS[bk][:], op0=ALU.add, op1=ALU.mult),
                reads=[rGS[dc * 3 + 1], rPS[pbk[1]]], writes=[rTMP[tb_]])
            sch.op("dve", lambda e, ta=ta, tb_=tb_: e.tensor_tensor(out=TMP[:, ta, :], in0=TMP[:, ta, :], in1=TMP[:, tb_, :], op=ALU.add),
                   reads=[rTMP[ta], rTMP[tb_]], writes=[rTMP[ta]])
            sch.op("dve", lambda e, tb_=tb_, bk=pbk[2], dc=dc: e.scalar_tensor_tensor(
                out=TMP[:, tb_, :], in0=GS[:, dc * 3 + 2, :], scalar=1.0, in1=PS[bk][:], op0=ALU.add, op1=ALU.mult),
                reads=[rGS[dc * 3 + 2], rPS[pbk[2]]], writes=[rTMP[tb_]])
            sch.op("dve", lambda e, ta=ta, tb_=tb_, dc=dc: e.tensor_tensor(out=SQ[:, dc, :], in0=TMP[:, ta, :], in1=TMP[:, tb_, :], op=ALU.add),
                   reads=[rTMP[ta], rTMP[tb_]], writes=[rSQ[dc]])
        wo_banks = {}
        for dc in range(NCH):
            if dc % 4 == 0:
                i = wslot(l, k0 + 23 + dc // 4)
            if dc == 0:
                grp = []
                for d2 in range(4):
                    wo_banks[d2] = bank()
                    grp.append((PS[wo_banks[d2]][:], wo_banks[d2],
                                [(RINGB[:, i, kc * 512 + d2 * 128:kc * 512 + (d2 + 1) * 128], SQ[:, kc, :]) for kc in range(NCH)]))
                mm_groups_splitk(grp, rRING[i], split=7, lo_reads=rSQ[0:7], hi_reads=[rSQ[7]])
            if dc in wo_banks:
                bk = wo_banks[dc]
            else:
                bk = bank()
                mm_group(PS[bk][:], [(RINGB[:, i, kc * 512 + (dc % 4) * 128:kc * 512 + (dc % 4 + 1) * 128], SQ[:, kc, :])
                                      for kc in range(NCH)], [rRING[i]] + rSQ, bk)
            sch.op("dve", lambda e, dc=dc, bk=bk: e.scalar_tensor_tensor(
                out=X[:, dc, :], in0=PS[bk][:], scalar=DER[:, l, 5 * 8 + dc:5 * 8 + dc + 1], in1=X[:, dc, :],
                op0=ALU.mult, op1=ALU.add), reads=[rPS[bk], rDER[l], rX[dc]], writes=[rX[dc]])

    def load_x(ti):
        for c in range(NCH):
            sch.dma("sp", lambda e, c=c, ti=ti: e.dma_start(out=X[:, c, :], in_=xT[c * 128:(c + 1) * 128, ti * T:(ti + 1) * T]),
                    f"xld{c}", writes=[rX[c]])

    load_x(0)
    for ti in range(n_tiles):
        for l in range(n_layers):
            if ti == 0:
                state["reserve7"] = True
                if l == 0:
                    ada_begin(0)
                ada_flush()
                if l + 1 < n_layers:
                    ada_begin(l + 1)
            else:
                state["reserve7"] = False
            ffn_stage(l, 0, 0, 1, 2)
            mixer_stage(l, 17)
            ffn_stage(l, 42, 6, 7, 8)
        def pre(c, ti=ti):
            yb = (SA if c < 4 else SB_)[:, c % 4, :]
            yres = (rSA if c < 4 else rSB)[c % 4]
            sch.op("act", lambda e: e.activation(out=yb, in_=X[:, c, :], func=AF.Identity, scale=GF[:, c:c + 1]),
                   reads=[rX[c], rGF], writes=[yres])
            if ti + 1 < n_tiles:
                sch.dma("sp", lambda e: e.dma_start(out=X[:, c, :], in_=xT[c * 128:(c + 1) * 128, (ti + 1) * T:(ti + 2) * T]),
                        f"xld{c}", writes=[rX[c]])
        rms_stats(dst=STAT[:, 0, :], dres=rSTAT[0], pre=pre)

        def out_ops(ti=ti):
            for c in range(NCH):
                o = c % 2
                yb = (SA if c < 4 else SB_)[:, c % 4, :]
                yres = (rSA if c < 4 else rSB)[c % 4]
                sch.op("dve", lambda e, yb=yb, o=o: e.tensor_tensor(out=OST[:, o, :], in0=yb, in1=STAT[:, 0, :], op=ALU.mult),
                       reads=[yres, rSTAT[0]], writes=[rOST[o]])
                sch.dma("sp", lambda e, c=c, o=o: e.dma_start(out=outT[c * 128:(c + 1) * 128, ti * T:(ti + 1) * T], in_=OST[:, o, :]),
                        f"ost{o}", reads=[rOST[o]])
        if ti + 1 < n_tiles:
            state["after_norm"] = [out_ops]
        else:
            out_ops()
    sch.final_wait("sp", "ost0")
    sch.final_wait("sp", "ost1")

    semnames = list(Sched.ENGS) + sorted(sch.dma_cnt.keys())
    sems = {n: es.enter_context(nc.semaphore("s_" + n)) for n in semnames}
    block = es.enter_context(nc.Block())

    def emit(ename):
        def body(eng):
            for waits, fn, inc in sch.ops[ename]:
                for k, v in waits:
                    eng.wait_ge(sems[k], v)
                if fn is None:
                    continue
                ins = fn(eng)
                ins.then_inc(sems[inc[0]], inc[1])
        return body

    block.tensor(emit("pe"))
    block.scalar(emit("act"))
    block.vector(emit("dve"))
    block.gpsimd(emit("pool"))
    block.sync(emit("sp"))
    try:
        print("sbuf bytes remaining/partition:", nc.sbuf_bytes_remaining)
    except Exception:
        pass
    es.close()
    return nc, recorded


def kernel(**inputs):
    ins = {k: np.asarray(v) for k, v in inputs.items()}
    nc, _ = build_program(DEPTH, SEQ // T, True)
    maps = prep_inputs(ins, DEPTH, SEQ // T, list(range(NCORES)))
    res = run_bass_kernel_spmd(nc, maps, core_ids=list(range(NCORES)))
    out = np.empty((NCORES, SEQ, D), np.float32)
    for b in range(NCORES):
        out[b] = res.results[b]["outT"].T
    return out
```

```python
import contextlib
import numpy as np
import concourse.bass as bass
import concourse.mybir as mybir
from concourse.bass_utils import run_bass_kernel_spmd

F32 = mybir.dt.float32
BF16 = mybir.dt.bfloat16
AF = mybir.ActivationFunctionType
ALU = mybir.AluOpType

D = 1024
NCH = 8
T = 512
DFF = 2816
NHC = 22
IN_COLS = 6656
N_ADA = 9
EPS = 1e-6
SLOTW = 4096
RING = 5
HALO = 32
LSP = 252
SEQ = 4096
DEPTH = 4
NCORES = 8
USE_ARSQRT = False


def _kc_tile(w, cols):
    K = w.shape[0]
    sub = w[:, cols]
    sub = sub.reshape(K // 128, 128, len(cols))
    return np.ascontiguousarray(sub.transpose(1, 0, 2)).reshape(128, -1)


def _layer_slots(l, ins):
    slots = []

    def ffn(w13, w2):
        for j2 in range(NHC // 2):
            cols = np.concatenate([
                np.arange((2 * j2) * 128, (2 * j2 + 2) * 128),
                DFF + np.arange((2 * j2) * 128, (2 * j2 + 2) * 128)])
            slots.append(_kc_tile(w13, cols))
        blk = w2.reshape(NHC, 128, NCH, 128).transpose(1, 2, 0, 3)
        blk = np.ascontiguousarray(blk).reshape(128, NCH * NHC * 128)
        for s in range(0, NCH * NHC * 128, SLOTW):
            slots.append(np.ascontiguousarray(blk[:, s:s + SLOTW]))

    ffn(ins["ffn1_w13"][l], ins["ffn1_w2"][l])
    w_in = ins["w_in"][l]
    for base in (3072, 2560, 512, 0, 1536, 2048, 1024):
        slots.append(_kc_tile(w_in, np.arange(base, base + 512)))
    wb = [ins["w_branch_a"][l], ins["w_branch_b"][l], ins["w_branch_c"][l]]
    for dc in range(NCH):
        g = [_kc_tile(w_in, 3584 + i * 1024 + dc * 128 + np.arange(128)) for i in range(3)]
        slots.append(np.concatenate(g, axis=1))
    for dc in range(NCH):
        b = [_kc_tile(wb[i], dc * 128 + np.arange(128)) for i in range(3)]
        slots.append(np.concatenate(b, axis=1))
    w_o = ins["w_o"][l]
    for s in range(2):
        slots.append(_kc_tile(w_o, np.arange(s * 512, (s + 1) * 512)))
    ffn(ins["ffn2_w13"][l], ins["ffn2_w2"][l])
    return slots


def _slot_widths():
    ffn = [SLOTW] * 11 + [SLOTW] * 5 + [NCH * NHC * 128 - 5 * SLOTW]
    mix = [SLOTW] * 7 + [3072] * NCH + [1536] * NCH + [SLOTW] * 2
    return ffn + mix + ffn


def _fm(v, n):
    return np.ascontiguousarray(v.reshape(n, 128).T)


def prep_inputs(ins, n_layers, n_tiles, batch_ids):
    widths = _slot_widths()
    wst = np.empty((n_layers, 128 * sum(widths)), np.float32)
    for l in range(n_layers):
        off = 0
        for s, w in zip(_layer_slots(l, ins), widths):
            assert s.shape == (128, w), (s.shape, w)
            wst[l, off:off + 128 * w] = s.reshape(-1)
            off += 128 * w
    wada = np.empty((n_layers, 18, 128, SLOTW), np.float32)
    for l in range(n_layers):
        for s in range(18):
            wada[l, s] = _kc_tile(ins["w_ada"][l], np.arange(s * 512, (s + 1) * 512))
    nsp = n_layers * LSP + 16
    spw = np.empty((n_layers, 128, 1024), np.float32)
    sp_common = np.zeros((128, nsp), np.float32)
    for l in range(n_layers):
        b = l * LSP
        sp_common[:, b + 0:b + 8] = _fm(ins["g_ffn1"][l], 8)
        sp_common[:, b + 8:b + 16] = _fm(ins["g_mix"][l], 8)
        sp_common[:, b + 16:b + 24] = _fm(ins["g_ffn2"][l], 8)
        sp_common[:, b + 24:b + 96] = _fm(ins["b_ada"][l], 72)
        sp_common[:, b + 96:b + 100] = _fm(ins["a_ln_g"][l], 4)
        sp_common[:, b + 100:b + 104] = _fm(ins["a_ln_b"][l], 4)
        bc = ins["b_conv"][l].reshape(3, 4, 128).transpose(2, 1, 0).reshape(128, 12)
        cc = ins["c_conv"][l].reshape(31, 4, 128).transpose(2, 1, 0).reshape(128, 124)
        sp_common[:, b + 104:b + 116] = bc
        sp_common[:, b + 116:b + 240] = cc
        sp_common[:, b + 240:b + 244] = _fm(ins["c_conv_b"][l], 4)
        sp_common[:, b + 244:b + 248] = _fm(ins["c_ln_g"][l], 4)
        sp_common[:, b + 248:b + 252] = _fm(ins["c_ln_b"][l], 4)
        spw[l, :, 0:512] = ins["a_ws"][l].transpose(2, 0, 1).reshape(128, 512)
        spw[l, :, 512:1024] = np.broadcast_to(ins["a_bs"][l].reshape(1, 512), (128, 512))
    gb = n_layers * LSP
    sp_common[:, gb:gb + 8] = _fm(ins["g_final"], 8)
    maps = []
    for bi in batch_ids:
        sp = sp_common.copy()
        sp[:, gb + 8:gb + 16] = _fm(ins["c"][bi], 8)
        xT = np.ascontiguousarray(ins["x"][bi, :n_tiles * T, :].T)
        maps.append({"xT": xT, "wst": wst, "wada": wada, "sp": sp, "spw": spw})
    return maps


class Res:
    __slots__ = ("w", "r", "name")

    def __init__(self, name=""):
        self.w = None
        self.r = {}
        self.name = name


class Sched:
    ENGS = ("pe", "act", "dve", "pool", "sp")

    def __init__(self):
        self.ops = {e: [] for e in self.ENGS}
        self.cnt = {e: 0 for e in self.ENGS}
        self.seen = {e: {} for e in self.ENGS}
        self.clock = {e: [None] for e in self.ENGS}
        self.dma_cnt = {}

    def _deps(self, e, reads, writes):
        raw, other = {}, {}

        def add(dst, kv):
            k, v = kv
            if dst.get(k, 0) < v:
                dst[k] = v
        for r in reads:
            if r.w is not None:
                add(raw, r.w)
        for r in writes:
            if r.w is not None:
                add(other, r.w)
            for kv in r.r.items():
                add(other, kv)
        waits = []
        seen = self.seen[e]
        changed = False
        for src, is_raw in ((raw, True), (other, False)):
            for k, v in src.items():
                if k == e:
                    if e in ("pe", "sp") or not is_raw:
                        continue
                if seen.get(k, 0) >= v:
                    continue
                if not changed:
                    seen = dict(seen)
                    changed = True
                seen[k] = v
                waits.append((k, v))
                if k in self.clock:
                    snap = self.clock[k][v]
                    for k2, v2 in snap.items():
                        if seen.get(k2, 0) < v2:
                            seen[k2] = v2
        if changed:
            self.seen[e] = seen
        return waits

    def op(self, e, fn, reads=(), writes=()):
        waits = self._deps(e, reads, writes)
        self.cnt[e] += 1
        n = self.cnt[e]
        self.clock[e].append(self.seen[e])
        self.ops[e].append((waits, fn, (e, 1)))
        for r in writes:
            r.w = (e, n)
            r.r = {}
        for r in reads:
            if r.r.get(e, 0) < n:
                r.r[e] = n

    def dma(self, q, fn, semname, reads=(), writes=()):
        waits = self._deps(q, reads, writes)
        prev = self.dma_cnt.get(semname, 0)
        if prev and self.seen[q].get(semname, 0) < prev:
            s = dict(self.seen[q])
            s[semname] = prev
            self.seen[q] = s
            waits.append((semname, prev))
        val = prev + 16
        self.dma_cnt[semname] = val
        self.ops[q].append((waits, fn, (semname, 16)))
        for r in writes:
            r.w = (semname, val)
            r.r = {}
        for r in reads:
            r.r[semname] = val

    def final_wait(self, e, semname):
        v = self.dma_cnt.get(semname, 0)
        if v and self.seen[e].get(semname, 0) < v:
            self.ops[e].append(([(semname, v)], None, None))


def build_program(n_layers, n_tiles, final_norm=True, stream=None):
    nc = bass.Bass("TRN2", target_bir_lowering=False)
    S = n_tiles * T
    widths = _slot_widths()
    offs = np.concatenate([[0], np.cumsum([128 * w for w in widths])]).astype(np.int64)
    nsp = n_layers * LSP + 16
    xT = nc.dram_tensor("xT", [D, S], F32, kind="ExternalInput").ap()
    wst = nc.dram_tensor("wst", [n_layers, int(offs[-1])], F32, kind="ExternalInput").ap()
    wada = nc.dram_tensor("wada", [n_layers, 18, 128, SLOTW], F32, kind="ExternalInput").ap()
    spd = nc.dram_tensor("sp", [128, nsp], F32, kind="ExternalInput").ap()
    spwd = nc.dram_tensor("spw", [n_layers, 128, 1024], F32, kind="ExternalInput").ap()
    outT = nc.dram_tensor("outT", [D, S], F32, kind="ExternalOutput").ap()

    sch = Sched()
    es = contextlib.ExitStack()

    def sb(name, shape, dt):
        return es.enter_context(nc.sbuf_tensor(name, shape, dt))

    X = sb("X", [128, NCH, T], F32)
    OST = sb("OST", [128, 2, T], F32)
    SQ = sb("SQ", [128, NCH, T], BF16)
    RSTD = sb("RSTD", [128, T], F32)
    H = sb("H", [128, NCH, T], BF16)
    G = sb("G", [128, NHC, T], BF16)
    NTMP = 6
    TMP = sb("TMP", [128, NTMP, T], F32)
    SA = sb("SA", [128, 4, T], F32)
    SB_ = sb("SBb", [128, 4, T], F32)
    SC = sb("SC", [128, 4, HALO + T], F32)
    SD = sb("SD", [128, 4, T], F32)
    VN = sb("VN", [128, 4, 512], BF16)
    ST6 = sb("ST6", [128, 4, 8], F32)
    MV = sb("MV", [128, 4, 2], F32)
    STAT = sb("STAT", [128, 2, T], F32)
    GS = sb("GS", [128, 24, T], BF16)
    SCB = sb("SCB", [128, 4, T + 2], F32)
    RINGB = sb("RINGB", [128, RING, SLOTW], BF16)
    SP = sb("SP", [128, nsp], F32)
    SCR = TMP[:, 0:2, :].rearrange("p a b -> p (a b)")
    DER = sb("DER", [128, n_layers, 72], F32)
    GF = sb("GF", [128, 8], F32)
    WMT = sb("WMT", [128, n_layers, 4, 128], BF16)
    BM = sb("BM", [128, n_layers, 4, 128], F32)
    ONES = sb("ONES", [128, 128], BF16)
    CACT = sb("CACT", [128, 8, 2], BF16)
    ROWH = sb("ROWH", [128, 1, 512], BF16)
    ROWL = sb("ROWL", [128, 1, 512], BF16)
    CST = sb("CST", [128, 2], F32)
    DUM = sb("DUM", [128, 2], F32)
    TAILB = sb("TAILB", [128, n_layers, 4, 2], F32)
    TAILC = sb("TAILC", [128, n_layers, 4, 30], F32)
    PS = [es.enter_context(nc.psum_tensor(f"ps{i}", [128, 512], F32)) for i in range(8)]

    rX = [Res(f"X{c}") for c in range(NCH)]
    rOST = [Res() for _ in range(2)]
    rSQ = [Res() for _ in range(NCH)]
    rRSTD = Res()
    rH = [Res() for _ in range(NCH)]
    rG = [Res() for _ in range(NHC)]
    rTMP = [Res() for _ in range(NTMP)]
    rSA = [Res() for _ in range(4)]
    rSB = [Res() for _ in range(4)]
    rSC = [Res() for _ in range(4)]
    rSD = [Res() for _ in range(4)]
    rVN = [Res() for _ in range(4)]
    rST = [Res() for _ in range(4)]
    rSTAT = [Res() for _ in range(2)]
    rGS = [Res() for _ in range(24)]
    rSCB = [Res() for _ in range(4)]
    rCW = [Res() for _ in range(n_layers)]
    rRING = [Res() for _ in range(RING)]
    rSP = Res()
    rDER = [Res() for _ in range(n_layers)]
    rGF = Res()
    rWMT = [Res() for _ in range(n_layers)]
    rBM = [Res() for _ in range(n_layers)]
    rONES = Res()
    rCACT = Res()
    rROWH = [Res() for _ in range(2)]
    rROWL = [Res() for _ in range(2)]
    rCST = Res()
    rDUM = Res()
    rTB = [Res() for _ in range(n_layers)]
    rTC = [Res() for _ in range(n_layers)]
    rPS = [Res() for _ in range(8)]

    state = {"bank": 0, "tmp": 0, "slot": 0}

    def bank():
        b = state["bank"]
        while (state.get("reserve7") and b == 7) or b in state.get("hold", ()):
            b = (b + 1) % 8
        state["bank"] = (b + 1) % 8
        return b

    def tmp():
        t = state["tmp"]
        state["tmp"] = (t + 1) % NTMP
        return t

    recorded = []

    def src_of(key):
        if key[0] == "ada":
            return wada[key[1], key[2]], SLOTW
        _, l, k = key
        return wst[l, int(offs[k]):int(offs[k + 1])].rearrange("(p w) -> p w", p=128), widths[k]

    def emit_slot_dma(n, key):
        src, width = src_of(key)
        i = n % RING
        dst = RINGB[:, i, 0:width]
        sch.dma("pool", lambda e, dst=dst, src=src: e.dma_start(out=dst, in_=src), f"ring{i}",
                writes=[rRING[i]])

    def load_slot(key):
        n = state["slot"]
        state["slot"] = n + 1
        recorded.append(key)
        if stream is None:
            emit_slot_dma(n, key)
        else:
            assert stream[n] == key, (n, stream[n], key)
            hi = min(n + RING - 2, len(stream) - 1)
            while state.get("issued", -1) < hi:
                state["issued"] = state.get("issued", -1) + 1
                emit_slot_dma(state["issued"], stream[state["issued"]])
        return n % RING

    def wslot(l, k):
        if state.get("ada_q"):
            state["ada_ctr"] = state.get("ada_ctr", 0) + 1
            if state["ada_ctr"] % 3 == 0:
                state["ada_q"].pop(0)()
        return load_slot(("w", l, k))

    sch.dma("sp", lambda e: e.dma_start(out=SP[:], in_=spd), "spld", writes=[rSP])
    sch.op("dve", lambda e: e.memset(ONES[:], 1.0), writes=[rONES])
    sch.op("dve", lambda e: e.memset(CST[:, 0:1], float(EPS * D)), writes=[rCST])
    sch.op("dve", lambda e: e.memset(CST[:, 1:2], float(EPS)), writes=[rCST])
    sch.op("dve", lambda e: e.memset(TAILB[:], 0.0), writes=rTB)
    sch.op("dve", lambda e: e.memset(TAILC[:], 0.0), writes=rTC)
    gb = n_layers * LSP
    sch.op("act", lambda e: e.activation(out=CACT[:, :, 0], in_=SP[:, gb + 8:gb + 16], func=AF.Silu),
           reads=[rSP], writes=[rCACT])
    sch.op("dve", lambda e: e.tensor_scalar(out=GF[:], in0=SP[:, gb:gb + 8], scalar1=float(np.sqrt(D)),
                                            scalar2=None, op0=ALU.mult), reads=[rSP], writes=[rGF])
    for l in range(n_layers):
        b0 = l * LSP
        sch.op("dve", lambda e, l=l, b0=b0: e.tensor_scalar(out=SP[:, b0 + 116:b0 + 240], in0=SP[:, b0 + 116:b0 + 240], scalar1=0.5,
                                                         scalar2=None, op0=ALU.mult), reads=[rSP], writes=[rSP, rCW[l]])
        sch.dma("sp", lambda e, l=l: e.dma_start(out=SCR[:], in_=spwd[l]), "scrld", writes=[rTMP[0], rTMP[1]])
        sch.op("dve", lambda e, l=l: e.tensor_copy(out=WMT[:, l].rearrange("p g q -> p (g q)"), in_=SCR[:, 0:512]),
               reads=[rTMP[0], rTMP[1]], writes=[rWMT[l]])
        sch.op("dve", lambda e, l=l: e.memset(WMT[64:128, l, :, 0:64], 0.0), writes=[rWMT[l]])
        bk = bank()
        sch.op("pe", lambda e, l=l, bk=bk: e.matmul(PS[bk][:], ONES[:], WMT[:, l].rearrange("p g q -> p (g q)"),
                                                   start=True, stop=True),
               reads=[rONES, rWMT[l]], writes=[rPS[bk]])
        for g in range(4):
            sch.op("dve", lambda e, l=l, g=g, bk=bk, b0=b0: e.scalar_tensor_tensor(
                out=BM[:, l, g, :], in0=PS[bk][:, g * 128:(g + 1) * 128], scalar=SP[:, b0 + 100 + g:b0 + 101 + g],
                in1=SCR[:, 512 + g * 128:512 + (g + 1) * 128], op0=ALU.mult, op1=ALU.add),
                reads=[rPS[bk], rSP, rTMP[0], rTMP[1]], writes=[rBM[l]])

    def ada_begin(l):
        q = state.setdefault("ada_q", [])
        for s in range(18):
            q.append(lambda s=s: ada_slot(l, s))
        q.append(lambda: ada_finish(l))

    def ada_flush():
        q = state.get("ada_q", [])
        while q:
            q.pop(0)()

    def ada_tr(s):
        par = 0

        def f(e):
            ins = None
            for cc in range(4):
                col = s * 4 + cc
                e.matmul(PS[7][:, col:col + 1], ROWH[0:1, par, cc * 128:(cc + 1) * 128], ONES[0:1, 0:1],
                         start=True, stop=False)
                ins = e.matmul(PS[7][:, col:col + 1], ROWL[0:1, par, cc * 128:(cc + 1) * 128], ONES[0:1, 0:1],
                               start=False, stop=True)
            return ins
        sch.op("pe", f, reads=[rROWH[par], rROWL[par], rONES], writes=[rPS[7]])

    def ada_slot(l, s):
        i = load_slot(("ada", l, s))
        if s > 0:
            ada_tr(s - 1)
        bk = bank()
        par = 0
        mm_group(PS[bk][0:1, :], [(CACT[:, kc, 0:1], RINGB[:, i, kc * 512:(kc + 1) * 512]) for kc in range(NCH)],
                 [rRING[i], rCACT], bk)
        sch.op("act", lambda e: e.activation(out=ROWH[0:1, par, :], in_=PS[bk][0:1, :], func=AF.Identity),
               reads=[rPS[bk]], writes=[rROWH[par]])
        sch.op("dve", lambda e: e.tensor_tensor(out=ROWL[0:1, par, :], in0=PS[bk][0:1, :], in1=ROWH[0:1, par, :],
                                                op=ALU.subtract),
               reads=[rPS[bk], rROWH[par]], writes=[rROWL[par]])

    def ada_finish(l):
        b0 = l * LSP
        bk = 7
        ada_tr(17)
        dl = DER[:, l, :]
        sch.op("dve", lambda e: e.tensor_tensor(out=dl, in0=PS[bk][:, 0:72], in1=SP[:, b0 + 24:b0 + 96], op=ALU.add),
               reads=[rPS[bk], rSP], writes=[rDER[l]])
        for j, gcol in ((1, 0), (4, 8), (7, 16)):
            sch.op("dve", lambda e, j=j, gcol=gcol: e.scalar_tensor_tensor(
                out=DER[:, l, j * 8:j * 8 + 8], in0=DER[:, l, j * 8:j * 8 + 8], scalar=1.0,
                in1=SP[:, b0 + gcol:b0 + gcol + 8], op0=ALU.add, op1=ALU.mult),
                reads=[rDER[l], rSP], writes=[rDER[l]])
            sch.op("dve", lambda e, j=j: e.tensor_scalar(
                out=DER[:, l, j * 8:j * 8 + 8], in0=DER[:, l, j * 8:j * 8 + 8], scalar1=float(np.sqrt(D)),
                scalar2=None, op0=ALU.mult), reads=[rDER[l]], writes=[rDER[l]])
        for j in (2, 5, 8):
            sch.op("dve", lambda e, j=j: e.tensor_scalar(
                out=DER[:, l, j * 8:j * 8 + 8], in0=DER[:, l, j * 8:j * 8 + 8], scalar1=0.5,
                scalar2=None, op0=ALU.mult), reads=[rDER[l]], writes=[rDER[l]])

    def rms_stats(dst=None, dres=None, pre=None):
        dst = RSTD[:] if dst is None else dst
        dres = rRSTD if dres is None else dres
        sch.op("act", lambda e: e.activation(out=DUM[:, 0:1], in_=CST[:, 0:1],
                                             func=(AF.Abs_reciprocal_sqrt if USE_ARSQRT else AF.Sqrt)), reads=[rCST], writes=[rDUM])
        for c in range(NCH):
            sch.op("act", lambda e, c=c: e.activation(out=SQ[:, c, :], in_=X[:, c, :], func=AF.Square),
                   reads=[rX[c]], writes=[rSQ[c]])
            if pre is not None:
                pre(c)
        bk = bank()

        for c in range(NCH):
            sch.op("pe", lambda e, c=c: e.matmul(PS[bk][:], ONES[:], SQ[:, c, :], start=(c == 0), stop=(c == NCH - 1)),
                   reads=[rSQ[c], rONES], writes=[rPS[bk]])
        if USE_ARSQRT:
            sch.op("act", lambda e: e.activation(out=RSTD[:], in_=PS[bk][:], func=AF.Abs_reciprocal_sqrt, bias=CST[:, 0:1], scale=1.0),
                   reads=[rPS[bk], rCST], writes=[rRSTD])
        else:
            sch.op("act", lambda e: e.activation(out=dst, in_=PS[bk][:], func=AF.Sqrt, bias=CST[:, 0:1], scale=1.0),
                   reads=[rPS[bk], rCST], writes=[dres])
            sch.op("dve", lambda e: e.reciprocal(out=dst, in_=dst), reads=[dres], writes=[dres])

    def norm(l, jshift, jscale):
        rms_stats()
        for c in range(NCH):
            t = tmp()
            sch.op("dve", lambda e, c=c, t=t: e.tensor_tensor(out=TMP[:, t, :], in0=X[:, c, :], in1=RSTD[:], op=ALU.mult),
                   reads=[rX[c], rRSTD], writes=[rTMP[t]])
            sch.op("act", lambda e, c=c, t=t: e.activation(
                out=H[:, c, :], in_=TMP[:, t, :], func=AF.Identity,
                bias=DER[:, l, jshift * 8 + c:jshift * 8 + c + 1], scale=DER[:, l, jscale * 8 + c:jscale * 8 + c + 1]),
                reads=[rTMP[t], rDER[l]], writes=[rH[c]])

    def mm_group(out_ap, pairs, reads, bk):
        def f(e):
            ins = None
            n = len(pairs)
            for k, (lhsT, rhs) in enumerate(pairs):
                ins = e.matmul(out_ap, lhsT, rhs, start=(k == 0), stop=(k == n - 1))
            return ins
        sch.op("pe", f, reads=reads, writes=[rPS[bk]])

    def mm_groups_splitk(groups, ring_res, split=4, lo_reads=None, hi_reads=None):
        banks = [rPS[bk] for (_, bk, _) in groups]
        ring_res = ring_res if isinstance(ring_res, list) else [ring_res]
        lo_reads = rH[0:4] if lo_reads is None else lo_reads
        hi_reads = rH[4:8] if hi_reads is None else hi_reads

        def half(lo, hi):
            def f(e):
                ins = None
                for out_ap, bk, pairs in groups:
                    n = len(pairs)
                    for kc in range(lo, n if hi is None else hi):
                        ins = e.matmul(out_ap, pairs[kc][0], pairs[kc][1], start=(kc == 0), stop=(kc == n - 1))
                return ins
            return f
        sch.op("pe", half(0, split), reads=ring_res + lo_reads, writes=banks)
        sch.op("pe", half(split, None), reads=ring_res + hi_reads, writes=banks)

    def ffn_stage(l, k0, jshift, jscale, jgate):
        norm(l, jshift, jscale)
        for f in state.pop("after_norm", []):
            f()
        pre = {}
        for j in range(NHC):
            if j % 2 == 0:
                i = wslot(l, k0 + j // 2)
            jj = j % 2
            if j == 0:
                grp = []
                for j2 in (0, 1):
                    pre[j2] = (bank(), bank())
                    for q, bk in ((j2, pre[j2][0]), (2 + j2, pre[j2][1])):
                        grp.append((PS[bk][:], bk, [(RINGB[:, i, kc * 512 + q * 128:kc * 512 + (q + 1) * 128], H[:, kc, :])
                                                    for kc in range(NCH)]))
                mm_groups_splitk(grp, rRING[i])
            if j in pre:
                ba, bb = pre[j]
            else:
                ba, bb = bank(), bank()
                for q, bk in ((jj, ba), (2 + jj, bb)):
                    mm_group(PS[bk][:], [(RINGB[:, i, kc * 512 + q * 128:kc * 512 + (q + 1) * 128], H[:, kc, :])
                                          for kc in range(NCH)], [rRING[i]] + rH, bk)
            t = tmp()
            sch.op("act", lambda e, t=t, ba=ba: e.activation(out=TMP[:, t, :], in_=PS[ba][:], func=AF.Silu),
                   reads=[rPS[ba]], writes=[rTMP[t]])
            sch.op("dve", lambda e, t=t, bb=bb, j=j: e.tensor_tensor(out=G[:, j, :], in0=TMP[:, t, :], in1=PS[bb][:], op=ALU.mult),
                   reads=[rTMP[t], rPS[bb]], writes=[rG[j]])
        cur = {}
        for dc in range(NCH):
            pairs, reads = [], list(rG)
            for hc in range(NHC):
                bi = dc * NHC + hc
                s = bi // 32
                if s not in cur:
                    cur.clear()
                    cur[s] = wslot(l, k0 + 11 + s)
                i = cur[s]
                if rRING[i] not in reads:
                    reads.append(rRING[i])
                pairs.append((i, (bi % 32) * 128, hc))
            bk = bank()
            if dc == 0:
                rr = [r for r in reads if r not in rG]
                mm_groups_splitk([(PS[bk][:], bk, [(RINGB[:, i, o:o + 128], G[:, hc, :]) for (i, o, hc) in pairs])],
                                 rr, split=NHC - 2, lo_reads=rG[0:NHC - 2], hi_reads=rG[NHC - 2:NHC])
            else:
                mm_group(PS[bk][:], [(RINGB[:, i, o:o + 128], G[:, hc, :]) for (i, o, hc) in pairs], reads, bk)
            sch.op("dve", lambda e, dc=dc, bk=bk: e.scalar_tensor_tensor(
                out=X[:, dc, :], in0=PS[bk][:], scalar=DER[:, l, jgate * 8 + dc:jgate * 8 + dc + 1], in1=X[:, dc, :],
                op0=ALU.mult, op1=ALU.add), reads=[rPS[bk], rDER[l], rX[dc]], writes=[rX[dc]])

    def mixer_stage(l, k0):
        b0 = l * LSP
        norm(l, 3, 4)
        pend = []

        ppend = []

        def drip(n):
            for _ in range(min(n, len(pend))):
                pend.pop(0)()
            for _ in range(min(max(1, (2 * n + 2) // 3), len(ppend))):
                ppend.pop(0)()

        def std_groups(i, banks=None, split=False):
            if split:
                bks = [bank() for _ in range(4)]
                mm_groups_splitk([(PS[bks[m]][:], bks[m],
                                   [(RINGB[:, i, kc * 512 + m * 128:kc * 512 + (m + 1) * 128], H[:, kc, :]) for kc in range(NCH)])
                                  for m in range(4)], rRING[i])
                for m in range(4):
                    yield m, bks[m]
                return
            for m in range(4):
                bk = banks[m] if banks is not None else bank()
                mm_group(PS[bk][:], [(RINGB[:, i, kc * 512 + m * 128:kc * 512 + (m + 1) * 128], H[:, kc, :])
                                      for kc in range(NCH)], [rRING[i]] + rH, bk)
                yield m, bk

        i = wslot(l, k0 + 0)
        for m, bk in std_groups(i, split=True):
            sch.op("act", lambda e, m=m, bk=bk: e.activation(out=SB_[:, m, :], in_=PS[bk][:], func=AF.Tanh, scale=0.5),
                   reads=[rPS[bk]], writes=[rSB[m]])
        i = wslot(l, k0 + 1)
        for m, bk in std_groups(i):
            sch.op("act", lambda e, m=m: e.activation(out=SC[:, m, HALO - 30:HALO], in_=TAILC[:, l, m, :], func=AF.Identity),
                   reads=[rTC[l]], writes=[rSC[m]])
            sch.op("dve", lambda e, m=m, bk=bk: e.scalar_tensor_tensor(
                out=SC[:, m, HALO:HALO + T], in0=SB_[:, m, :], scalar=1.0, in1=PS[bk][:], op0=ALU.add, op1=ALU.mult),
                reads=[rSB[m], rPS[bk]], writes=[rSC[m]])
            sch.op("act", lambda e, m=m: e.activation(out=TAILC[:, l, m, :], in_=SC[:, m, HALO + T - 30:HALO + T], func=AF.Identity),
                   reads=[rSC[m]], writes=[rTC[l]])
        order = [(k, m) for k in range(27) for m in range(4)] + [(k, m) for m in range(4) for k in range(27, 31)]
        for k, m in order:
            if True:
                src = SC[:, m, HALO - 30 + k:HALO - 30 + k + T]
                wv = SP[:, b0 + 116 + m * 31 + k:b0 + 116 + m * 31 + k + 1]
                if k == 0:
                    sch.op("act", lambda e, m=m, wv=wv, src=src: e.activation(
                        out=SD[:, m, :], in_=src, func=AF.Identity, scale=wv, bias=SP[:, b0 + 240 + m:b0 + 241 + m]),
                        reads=[rSC[m], rCW[l], rSP], writes=[rSD[m]])
                else:
                    pend.append(lambda m=m, wv=wv, src=src: sch.op("dve", lambda e: e.scalar_tensor_tensor(
                        out=SD[:, m, :], in0=src, scalar=wv, in1=SD[:, m, :],
                        op0=ALU.mult, op1=ALU.add), reads=[rSC[m], rCW[l], rSD[m]], writes=[rSD[m]]))
        NDRIP = 4
        i = wslot(l, k0 + 2)
        mb = [bank() for _ in range(4)]
        state["hold"] = set(mb)
        sgu_pending = []
        vt = []
        for tb in range(4):
            bk = bank()
            mm_group(PS[bk][:], [(H[:, kc, tb * 128:(tb + 1) * 128], RINGB[:, i, kc * 512:(kc + 1) * 512])
                                  for kc in range(NCH)], [rRING[i]] + rH, bk)
            t = tmp()
            vt.append(t)
            v = tb
            sch.op("act", lambda e, t=t, bk=bk: e.activation(out=TMP[:, t, :], in_=PS[bk][:], func=AF.Gelu),
                   reads=[rPS[bk]], writes=[rTMP[t]])
            sch.op("dve", lambda e, t=t, v=v: e.bn_stats(out=ST6[:, v, 0:6], in_=TMP[:, t, :]),
                   reads=[rTMP[t]], writes=[rST[v]])
            sch.op("dve", lambda e, v=v: e.bn_aggr(out=MV[:, v, :], in_=ST6[:, v, 0:6]),
                   reads=[rST[v]], writes=[rST[v]])
        for tb in range(4):
            t = vt[tb]
            v = tb
            sch.op("act", lambda e, v=v: e.activation(out=MV[:, v, 1:2], in_=MV[:, v, 1:2], func=AF.Sqrt,
                                                     bias=CST[:, 1:2], scale=1.0),
                   reads=[rST[v], rCST], writes=[rST[v]])
            sch.op("dve", lambda e, v=v: e.reciprocal(out=MV[:, v, 1:2], in_=MV[:, v, 1:2]),
                   reads=[rST[v]], writes=[rST[v]])
            sch.op("dve", lambda e, t=t, v=v: e.tensor_scalar(out=VN[:, v, :], in0=TMP[:, t, :], scalar1=MV[:, v, 0:1],
                                                             scalar2=MV[:, v, 1:2], op0=ALU.subtract, op1=ALU.mult),
                   reads=[rTMP[t], rST[v]], writes=[rVN[v]])

            def sgu(v=v, tb=tb):
                def f(e):
                    ins = None
                    for g in range(4):
                        ins = e.matmul(PS[mb[g]][:, tb * 128:(tb + 1) * 128], VN[:, v, g * 128:(g + 1) * 128],
                                       WMT[:, l, g, :], start=True, stop=True)
                    return ins
                sch.op("pe", f, reads=[rVN[v], rWMT[l]], writes=[rPS[b] for b in mb])
            sgu_pending.append(sgu)
        i = wslot(l, k0 + 3)
        for m, bk in std_groups(i):
            if m >= 2:
                sgu_pending.pop(0)()
            sch.op("act", lambda e, m=m, bk=bk: e.activation(out=SA[:, m, :], in_=PS[bk][:], func=AF.Gelu),
                   reads=[rPS[bk]], writes=[rSA[m]])
            drip(NDRIP)
        i = wslot(l, k0 + 4)
        for m, bk in std_groups(i):
            if sgu_pending:
                sgu_pending.pop(0)()
            sch.op("act", lambda e, m=m, bk=bk: e.activation(out=SB_[:, m, :], in_=PS[bk][:], func=AF.Identity),
                   reads=[rPS[bk]], writes=[rSB[m]])
            drip(NDRIP)
        for g in range(4):
            t = tmp()
            sch.op("dve", lambda e, g=g, t=t: e.scalar_tensor_tensor(
                out=TMP[:, t, :].rearrange("p (a b) -> p a b", a=4),
                in0=PS[mb[g]][:].rearrange("p (a b) -> p a b", a=4),
                scalar=SP[:, b0 + 96 + g:b0 + 97 + g],
                in1=BM[:, l, g, :].unsqueeze(1).broadcast_to([128, 4, 128]),
                op0=ALU.mult, op1=ALU.add), reads=[rPS[mb[g]], rSP, rBM[l]], writes=[rTMP[t]])
            sch.op("dve", lambda e, g=g, t=t: e.tensor_tensor(out=G[:, g, :], in0=SA[:, g, :], in1=TMP[:, t, :], op=ALU.mult),
                   reads=[rSA[g], rTMP[t]], writes=[rG[g]])
            drip(2)
        i = wslot(l, k0 + 5)
        for m, bk in std_groups(i):
            sch.op("act", lambda e, m=m: e.activation(out=SCB[:, m, 0:2], in_=TAILB[:, l, m, :], func=AF.Identity),
                   reads=[rTB[l]], writes=[rSCB[m]])
            sch.op("dve", lambda e, m=m, bk=bk: e.tensor_tensor(out=SCB[:, m, 2:2 + T], in0=SB_[:, m, :], in1=PS[bk][:], op=ALU.mult),
                   reads=[rSB[m], rPS[bk]], writes=[rSCB[m]])
            sch.op("act", lambda e, m=m: e.activation(out=TAILB[:, l, m, :], in_=SCB[:, m, T:T + 2], func=AF.Identity),
                   reads=[rSCB[m]], writes=[rTB[l]])
            wb = b0 + 104 + m * 3
            sch.op("act", lambda e, m=m, wb=wb: e.activation(out=SB_[:, m, :], in_=SCB[:, m, 0:T], func=AF.Identity,
                                                            scale=SP[:, wb:wb + 1]),
                   reads=[rSCB[m], rSP], writes=[rSB[m]])
            for k in (1, 2):
                sch.op("dve", lambda e, m=m, wb=wb, k=k: e.scalar_tensor_tensor(
                    out=SB_[:, m, :], in0=SCB[:, m, k:k + T], scalar=SP[:, wb + k:wb + k + 1],
                    in1=SB_[:, m, :], op0=ALU.mult, op1=ALU.add), reads=[rSCB[m], rSP, rSB[m]], writes=[rSB[m]])
            drip(NDRIP)
        i = wslot(l, k0 + 6)
        for m, bk in std_groups(i):
            sch.op("dve", lambda e, m=m, bk=bk: e.tensor_tensor(out=G[:, 4 + m, :], in0=SB_[:, m, :], in1=PS[bk][:], op=ALU.mult),
                   reads=[rSB[m], rPS[bk]], writes=[rG[4 + m]])
            drip(NDRIP)
        state["hold"] = set()

        def gate_slot(dc):
            ig = wslot(l, k0 + 7 + dc)
            for i3 in range(3):
                bk = bank()
                mm_group(PS[bk][:], [(RINGB[:, ig, (i3 * 8 + kc) * 128:(i3 * 8 + kc + 1) * 128], H[:, kc, :])
                                      for kc in range(NCH)], [rRING[ig]] + rH, bk)
                sch.op("act", lambda e, i3=i3, bk=bk, dc=dc: e.activation(out=GS[:, dc * 3 + i3, :], in_=PS[bk][:],
                                                                       func=AF.Tanh, scale=0.5),
                       reads=[rPS[bk]], writes=[rGS[dc * 3 + i3]])
                drip(NDRIP + 2)
        for dc in range(6):
            gate_slot(dc)
        drip(len(pend))
        gate_slot(6)
        while ppend:
            ppend.pop(0)()
        for m in range(4):
            sch.op("act", lambda e, m=m: e.activation(out=G[:, 12 + m, :], in_=SD[:, m, :], func=AF.Identity),
                   reads=[rSD[m]], writes=[rG[12 + m]])
            sch.op("act", lambda e, m=m: e.activation(out=G[:, 16 + m, :], in_=SD[:, m, :], func=AF.Square),
                   reads=[rSD[m]], writes=[rG[16 + m]])
        gate_slot(7)
        b1, b2 = bank(), bank()
        mm_group(PS[b1][:], [(ONES[:], G[:, 12 + m, :]) for m in range(4)], [rONES] + rG[12:16], b1)
        mm_group(PS[b2][:], [(ONES[:], G[:, 16 + m, :]) for m in range(4)], [rONES] + rG[16:20], b2)
        sch.op("dve", lambda e: e.tensor_scalar(out=STAT[:, 0, :], in0=PS[b1][:], scalar1=1.0 / 512, scalar2=None, op0=ALU.mult),
               reads=[rPS[b1]], writes=[rSTAT[0]])
        sch.op("dve", lambda e: e.tensor_tensor(out=STAT[:, 1, :], in0=STAT[:, 0, :], in1=STAT[:, 0, :], op=ALU.mult),
               reads=[rSTAT[0]], writes=[rSTAT[1]])
        sch.op("dve", lambda e: e.scalar_tensor_tensor(out=STAT[:, 1, :], in0=PS[b2][:], scalar=1.0 / 512, in1=STAT[:, 1, :],
                                                       op0=ALU.mult, op1=ALU.subtract),
               reads=[rPS[b2], rSTAT[1]], writes=[rSTAT[1]])
        sch.op("act", lambda e: e.activation(out=STAT[:, 1, :], in_=STAT[:, 1, :], func=AF.Sqrt, bias=CST[:, 1:2], scale=1.0),
               reads=[rSTAT[1], rCST], writes=[rSTAT[1]])
        sch.op("dve", lambda e: e.reciprocal(out=STAT[:, 1, :], in_=STAT[:, 1, :]), reads=[rSTAT[1]], writes=[rSTAT[1]])
        for m in range(4):
            t = tmp()
            sch.op("dve", lambda e, m=m, t=t: e.tensor_tensor(out=TMP[:, t, :], in0=SD[:, m, :], in1=STAT[:, 0, :], op=ALU.subtract),
                   reads=[rSD[m], rSTAT[0]], writes=[rTMP[t]])
            sch.op("dve", lambda e, t=t: e.tensor_tensor(out=TMP[:, t, :], in0=TMP[:, t, :], in1=STAT[:, 1, :], op=ALU.mult),
                   reads=[rTMP[t], rSTAT[1]], writes=[rTMP[t]])
            sch.op("act", lambda e, m=m, t=t: e.activation(
                out=G[:, 8 + m, :], in_=TMP[:, t, :], func=AF.Silu,
                bias=SP[:, b0 + 248 + m:b0 + 249 + m], scale=SP[:, b0 + 244 + m:b0 + 245 + m]),
                reads=[rTMP[t], rSP], writes=[rG[8 + m]])
        for dc in range(NCH):
            ib = wslot(l, k0 + 15 + dc)
            pbk = []
            for i3 in range(3):
                bk = bank()
                pbk.append(bk)
                mm_group(PS[bk][:], [(RINGB[:, ib, (i3 * 4 + kc) * 128:(i3 * 4 + kc + 1) * 128], G[:, i3 * 4 + kc, :])
                                      for kc in range(4)], [rRING[ib]] + rG[i3 * 4:i3 * 4 + 4], bk)
            ta, tb_ = tmp(), tmp()
            sch.op("dve", lambda e, ta=ta, bk=pbk[0], dc=dc: e.scalar_tensor_tensor(
                out=TMP[:, ta, :], in0=GS[:, dc * 3 + 0, :], scalar=1.0, in1=PS[bk][:], op0=ALU.add, op1=ALU.mult),
                reads=[rGS[dc * 3 + 0], rPS[pbk[0]]], writes=[rTMP[ta]])
            sch.op("dve", lambda e, tb_=tb_, bk=pbk[1], dc=dc: e.scalar_tensor_tensor(
                out=TMP[:, tb_, :], in0=GS[:, dc * 3 + 1, :], scalar=1.0, in1=PS[bk][:], op0=ALU.add, op1=ALU.mult),
                reads=[rGS[dc * 3 + 1], rPS[pbk[1]]], writes=[rTMP[tb_]])
            sch.op("dve", lambda e, ta=ta, tb_=tb_: e.tensor_tensor(out=TMP[:, ta, :], in0=TMP[:, ta, :], in1=TMP[:, tb_, :], op=ALU.add),
                   reads=[rTMP[ta], rTMP[tb_]], writes=[rTMP[ta]])
            sch.op("dve", lambda e, tb_=tb_, bk=pbk[2], dc=dc: e.scalar_tensor_tensor(
                out=TMP[:, tb_, :], in0=GS[:, dc * 3 + 2, :], scalar=1.0, in1=PS[bk][:], op0=ALU.add, op1=ALU.mult),
                reads=[rGS[dc * 3 + 2], rPS[pbk[2]]], writes=[rTMP[tb_]])
            sch.op("dve", lambda e, ta=ta, tb_=tb_, dc=dc: e.tensor_tensor(out=SQ[:, dc, :], in0=TMP[:, ta, :], in1=TMP[:, tb_, :], op=ALU.add),
                   reads=[rTMP[ta], rTMP[tb_]], writes=[rSQ[dc]])
        wo_banks = {}
        for dc in range(NCH):
            if dc % 4 == 0:
                i = wslot(l, k0 + 23 + dc // 4)
            if dc == 0:
                grp = []
                for d2 in range(4):
                    wo_banks[d2] = bank()
                    grp.append((PS[wo_banks[d2]][:], wo_banks[d2],
                                [(RINGB[:, i, kc * 512 + d2 * 128:kc * 512 + (d2 + 1) * 128], SQ[:, kc, :]) for kc in range(NCH)]))
                mm_groups_splitk(grp, rRING[i], split=7, lo_reads=rSQ[0:7], hi_reads=[rSQ[7]])
            if dc in wo_banks:
                bk = wo_banks[dc]
            else:
                bk = bank()
                mm_group(PS[bk][:], [(RINGB[:, i, kc * 512 + (dc % 4) * 128:kc * 512 + (dc % 4 + 1) * 128], SQ[:, kc, :])
                                      for kc in range(NCH)], [rRING[i]] + rSQ, bk)
            sch.op("dve", lambda e, dc=dc, bk=bk: e.scalar_tensor_tensor(
                out=X[:, dc, :], in0=PS[bk][:], scalar=DER[:, l, 5 * 8 + dc:5 * 8 + dc + 1], in1=X[:, dc, :],
                op0=ALU.mult, op1=ALU.add), reads=[rPS[bk], rDER[l], rX[dc]], writes=[rX[dc]])

    def load_x(ti):
        for c in range(NCH):
            sch.dma("sp", lambda e, c=c, ti=ti: e.dma_start(out=X[:, c, :], in_=xT[c * 128:(c + 1) * 128, ti * T:(ti + 1) * T]),
                    f"xld{c}", writes=[rX[c]])

    load_x(0)
    for ti in range(n_tiles):
        for l in range(n_layers):
            if ti == 0:
                state["reserve7"] = True
                if l == 0:
                    ada_begin(0)
                ada_flush()
                if l + 1 < n_layers:
                    ada_begin(l + 1)
            else:
                state["reserve7"] = False
            ffn_stage(l, 0, 0, 1, 2)
            mixer_stage(l, 17)
            ffn_stage(l, 42, 6, 7, 8)
        def pre(c, ti=ti):
            yb = (SA if c < 4 else SB_)[:, c % 4, :]
            yres = (rSA if c < 4 else rSB)[c % 4]
            sch.op("act", lambda e: e.activation(out=yb, in_=X[:, c, :], func=AF.Identity, scale=GF[:, c:c + 1]),
                   reads=[rX[c], rGF], writes=[yres])
            if ti + 1 < n_tiles:
                sch.dma("sp", lambda e: e.dma_start(out=X[:, c, :], in_=xT[c * 128:(c + 1) * 128, (ti + 1) * T:(ti + 2) * T]),
                        f"xld{c}", writes=[rX[c]])
        rms_stats(dst=STAT[:, 0, :], dres=rSTAT[0], pre=pre)

        def out_ops(ti=ti):
            for c in range(NCH):
                o = c % 2
                yb = (SA if c < 4 else SB_)[:, c % 4, :]
                yres = (rSA if c < 4 else rSB)[c % 4]
                sch.op("dve", lambda e, yb=yb, o=o: e.tensor_tensor(out=OST[:, o, :], in0=yb, in1=STAT[:, 0, :], op=ALU.mult),
                       reads=[yres, rSTAT[0]], writes=[rOST[o]])
                sch.dma("sp", lambda e, c=c, o=o: e.dma_start(out=outT[c * 128:(c + 1) * 128, ti * T:(ti + 1) * T], in_=OST[:, o, :]),
                        f"ost{o}", reads=[rOST[o]])
        if ti + 1 < n_tiles:
            state["after_norm"] = [out_ops]
        else:
            out_ops()
    sch.final_wait("sp", "ost0")
    sch.final_wait("sp", "ost1")

    semnames = list(Sched.ENGS) + sorted(sch.dma_cnt.keys())
    sems = {n: es.enter_context(nc.semaphore("s_" + n)) for n in semnames}
    block = es.enter_context(nc.Block())

    def emit(ename):
        def body(eng):
            for waits, fn, inc in sch.ops[ename]:
                for k, v in waits:
                    eng.wait_ge(sems[k], v)
                if fn is None:
                    continue
                ins = fn(eng)
                ins.then_inc(sems[inc[0]], inc[1])
        return body

    block.tensor(emit("pe"))
    block.scalar(emit("act"))
    block.vector(emit("dve"))
    block.gpsimd(emit("pool"))
    block.sync(emit("sp"))
    try:
        print("sbuf bytes remaining/partition:", nc.sbuf_bytes_remaining)
    except Exception:
        pass
    es.close()
    return nc, recorded


def kernel(**inputs):
    ins = {k: np.asarray(v) for k, v in inputs.items()}
    nc, _ = build_program(DEPTH, SEQ // T, True)
    maps = prep_inputs(ins, DEPTH, SEQ // T, list(range(NCORES)))
    res = run_bass_kernel_spmd(nc, maps, core_ids=list(range(NCORES)))
    out = np.empty((NCORES, SEQ, D), np.float32)
    for b in range(NCORES):
        out[b] = res.results[b]["outT"].T
    return out
```
